# Optimizing a Trainium2 kernel written in Bass

```python
import jax, jax.numpy as jnp
from jax import lax
import numpy as np

D_MODEL = 1024
BATCH = 8
SEQ = 2048
DEPTH = 4
DEC_BATCH = 32
DEC_SEQ = 64
PAST_LEN = 1024

CHUNK = 64
HEAD_DIM = 64
N_HEADS = 8
KV_HEADS = 2
GROUP = N_HEADS // KV_HEADS
WINDOW = 128
WIN_CHUNKS = WINDOW // CHUNK
ATTN_WIDTH = N_HEADS * HEAD_DIM
KV_WIDTH = KV_HEADS * HEAD_DIM
CONV_DIM = 256
CONV_W = 3
MEM_HEADS = 4
MEM_WIDTH = MEM_HEADS * HEAD_DIM
N_MEM = 256
MIX_WIDTH = ATTN_WIDTH + CONV_DIM + MEM_WIDTH
SPLIT_IDX = [ATTN_WIDTH, ATTN_WIDTH + KV_WIDTH, ATTN_WIDTH + 2 * KV_WIDTH,
             ATTN_WIDTH + 2 * KV_WIDTH + CONV_DIM, ATTN_WIDTH + 2 * KV_WIDTH + 2 * CONV_DIM,
             ATTN_WIDTH + 2 * KV_WIDTH + 3 * CONV_DIM]
IN_WIDTH = ATTN_WIDTH + 2 * KV_WIDTH + 3 * CONV_DIM + MEM_WIDTH
D_FF = ((8 * D_MODEL // 3 + 255) // 256) * 256
EPS = 1e-6
ATTN_SCALE = HEAD_DIM ** -0.5
NEG = -1e30

kernel_name = "hymba_swa_sink_shortconv_memxattn_stream"


def rms_norm(x, g):
    xf = x.astype(jnp.float32)
    xf = xf * lax.rsqrt(jnp.mean(xf * xf, axis=-1, keepdims=True) + EPS)
    return xf.astype(x.dtype) * g


def sink_attention(q, k, v, sink, valid):
    s = jnp.einsum('bnqkgd,bnskd->bnkgqs', q.astype(jnp.float32), k.astype(jnp.float32)) * ATTN_SCALE
    s = jnp.where(valid[None, :, None, None, None, :], s, NEG)
    sl = sink.astype(jnp.float32).reshape(KV_HEADS, GROUP)[None, None, :, :, None, None]
    m = jnp.maximum(jnp.max(s, axis=-1, keepdims=True), sl)
    p = jnp.exp(s - m)
    p = p / (jnp.sum(p, axis=-1, keepdims=True) + jnp.exp(sl - m))
    return jnp.einsum('bnkgqs,bnskd->bnqkgd', p.astype(v.dtype), v)


def window_attention_prompt(q, k, v, sink):
    B, L = q.shape[0], q.shape[1]
    nc = L // CHUNK
    qb = q.reshape(B, nc, CHUNK, KV_HEADS, GROUP, HEAD_DIM)
    pad = ((0, 0), (WIN_CHUNKS, 0), (0, 0), (0, 0), (0, 0))
    kp = jnp.pad(k.reshape(B, nc, CHUNK, KV_HEADS, HEAD_DIM), pad)
    vp = jnp.pad(v.reshape(B, nc, CHUNK, KV_HEADS, HEAD_DIM), pad)
    kb = jnp.concatenate([kp[:, i:i + nc] for i in range(WIN_CHUNKS + 1)], axis=2)
    vb = jnp.concatenate([vp[:, i:i + nc] for i in range(WIN_CHUNKS + 1)], axis=2)
    key_block = jnp.arange((WIN_CHUNKS + 1) * CHUNK) // CHUNK
    valid = (jnp.arange(nc)[:, None] + key_block[None, :]) >= WIN_CHUNKS
    o = sink_attention(qb, kb, vb, sink, valid)
    return o.reshape(B, L, ATTN_WIDTH)


def window_attention_sample(q, k, v, sink, cache_k, cache_v):
    B, L = q.shape[0], q.shape[1]
    kk = jnp.concatenate([cache_k, k], axis=1)
    vv = jnp.concatenate([cache_v, v], axis=1)
    valid = jnp.ones((1, kk.shape[1]), dtype=bool)
    o = sink_attention(q.reshape(B, 1, L, KV_HEADS, GROUP, HEAD_DIM), kk[:, None], vv[:, None], sink, valid)
    return o.reshape(B, L, ATTN_WIDTH), kk[:, -WINDOW:], vv[:, -WINDOW:]


def causal_conv(u, state, w):
    L = u.shape[1]
    up = jnp.concatenate([state, u], axis=1)
    y = up[:, 0:L] * w[0]
    for i in range(1, CONV_W):
        y = y + up[:, i:i + L] * w[i]
    return y, up[:, -(CONV_W - 1):]


def memory_kv(mem, mem_norm_g, w_mem_kv, mk_norm_g):
    B = mem.shape[0]
    mk, mv = jnp.split(rms_norm(mem, mem_norm_g) @ w_mem_kv, 2, axis=-1)
    mk = rms_norm(mk.reshape(B, N_MEM, MEM_HEADS, HEAD_DIM), mk_norm_g)
    return mk, mv.reshape(B, N_MEM, MEM_HEADS, HEAD_DIM)


def memory_attention(mq, mk, mv):
    s = jnp.einsum('blhd,bmhd->bhlm', mq.astype(jnp.float32), mk.astype(jnp.float32)) * ATTN_SCALE
    p = jax.nn.softmax(s, axis=-1)
    o = jnp.einsum('bhlm,bmhd->blhd', p.astype(mv.dtype), mv)
    return o.reshape(mq.shape[0], mq.shape[1], MEM_WIDTH)


def layer(x, mem_k, mem_v, cache_k, cache_v, conv_state, attn_norm_g, w_in, q_norm_g, k_norm_g,
          sinks, conv_w, mq_norm_g, out_norm_g, w_out, ffn_norm_g, w_gate_up, w_down):
    B, L, _ = x.shape
    u = rms_norm(x, attn_norm_g) @ w_in
    q, k, v, cb, cc, cx, mq = jnp.split(u, SPLIT_IDX, axis=-1)
    q = rms_norm(q.reshape(B, L, N_HEADS, HEAD_DIM), q_norm_g)
    k = rms_norm(k.reshape(B, L, KV_HEADS, HEAD_DIM), k_norm_g)
    v = v.reshape(B, L, KV_HEADS, HEAD_DIM)
    if cache_k is None:
        a = window_attention_prompt(q, k, v, sinks)
        new_k, new_v = k[:, -WINDOW:], v[:, -WINDOW:]
        conv_state = jnp.zeros((B, CONV_W - 1, CONV_DIM), dtype=x.dtype)
    else:
        a, new_k, new_v = window_attention_sample(q, k, v, sinks, cache_k, cache_v)
    cy, new_conv = causal_conv(cc * cx, conv_state, conv_w)
    cy = cb * cy
    mq = rms_norm(mq.reshape(B, L, MEM_HEADS, HEAD_DIM), mq_norm_g)
    mo = memory_attention(mq, mem_k, mem_v)
    mix = jnp.concatenate([
        rms_norm(a, out_norm_g[:ATTN_WIDTH]),
        rms_norm(cy, out_norm_g[ATTN_WIDTH:ATTN_WIDTH + CONV_DIM]),
        rms_norm(mo, out_norm_g[ATTN_WIDTH + CONV_DIM:]),
    ], axis=-1)
    h = x + mix @ w_out
    gate, up = jnp.split(rms_norm(h, ffn_norm_g) @ w_gate_up, 2, axis=-1)
    y = h + (jax.nn.silu(gate) * up) @ w_down
    return y, new_k, new_v, new_conv


def setup_inputs(seed: int = 0) -> dict:
    key = jax.random.key(seed)
    ks = jax.random.split(key, 24)
    f32 = jnp.float32
    nrm = lambda k, shape, scale: jax.random.normal(k, shape, f32) * scale
    gain = lambda k, shape: 1.0 + 0.05 * jax.random.normal(k, shape, f32)
    win_rows = min(WINDOW, PAST_LEN)
    return {
        "x_prompt": nrm(ks[0], (BATCH, SEQ, D_MODEL), 1.0),
        "x_sample": nrm(ks[1], (DEC_BATCH, DEC_SEQ, D_MODEL), 1.0),
        "mem_prompt": nrm(ks[2], (BATCH, N_MEM, D_MODEL), 1.0),
        "cache_win_k": nrm(ks[3], (DEPTH, DEC_BATCH, win_rows, KV_HEADS, HEAD_DIM), 1.0),
        "cache_win_v": nrm(ks[4], (DEPTH, DEC_BATCH, win_rows, KV_HEADS, HEAD_DIM), 1.0),
        "cache_conv": nrm(ks[5], (DEPTH, DEC_BATCH, CONV_W - 1, CONV_DIM), 1.0),
        "cache_mem_k": nrm(ks[6], (DEPTH, DEC_BATCH, N_MEM, MEM_HEADS, HEAD_DIM), 1.0),
        "cache_mem_v": nrm(ks[7], (DEPTH, DEC_BATCH, N_MEM, MEM_HEADS, HEAD_DIM), 1.0),
        "attn_norm_g": gain(ks[8], (DEPTH, D_MODEL)),
        "w_in": nrm(ks[9], (DEPTH, D_MODEL, IN_WIDTH), D_MODEL ** -0.5),
        "q_norm_g": gain(ks[10], (DEPTH, HEAD_DIM)),
        "k_norm_g": gain(ks[11], (DEPTH, HEAD_DIM)),
        "sinks": nrm(ks[12], (DEPTH, N_HEADS), 0.5),
        "conv_w": nrm(ks[13], (DEPTH, CONV_W, CONV_DIM), CONV_W ** -0.5),
        "mem_norm_g": gain(ks[14], (DEPTH, D_MODEL)),
        "w_mem_kv": nrm(ks[15], (DEPTH, D_MODEL, 2 * MEM_WIDTH), D_MODEL ** -0.5),
        "mq_norm_g": gain(ks[16], (DEPTH, HEAD_DIM)),
        "mk_norm_g": gain(ks[17], (DEPTH, HEAD_DIM)),
        "out_norm_g": gain(ks[18], (DEPTH, MIX_WIDTH)),
        "w_out": nrm(ks[19], (DEPTH, MIX_WIDTH, D_MODEL), MIX_WIDTH ** -0.5),
        "ffn_norm_g": gain(ks[20], (DEPTH, D_MODEL)),
        "w_gate_up": nrm(ks[21], (DEPTH, D_MODEL, 2 * D_FF), D_MODEL ** -0.5),
        "w_down": nrm(ks[22], (DEPTH, D_FF, D_MODEL), D_FF ** -0.5),
    }


def reference(x_prompt, x_sample, mem_prompt, cache_win_k, cache_win_v, cache_conv, cache_mem_k,
              cache_mem_v, attn_norm_g, w_in, q_norm_g, k_norm_g, sinks, conv_w, mem_norm_g,
              w_mem_kv, mq_norm_g, mk_norm_g, out_norm_g, w_out, ffn_norm_g, w_gate_up, w_down):
    yp, ys = x_prompt, x_sample
    wk_p, wv_p, cv_p, mk_p_all, mv_p_all = [], [], [], [], []
    wk_s, wv_s, cv_s = [], [], []
    for l in range(DEPTH):
        lw = (attn_norm_g[l], w_in[l], q_norm_g[l], k_norm_g[l], sinks[l], conv_w[l],
              mq_norm_g[l], out_norm_g[l], w_out[l], ffn_norm_g[l], w_gate_up[l], w_down[l])
        mk_p, mv_p = memory_kv(mem_prompt, mem_norm_g[l], w_mem_kv[l], mk_norm_g[l])
        yp, k_p, v_p, c_p = layer(yp, mk_p, mv_p, None, None, None, *lw)
        wk_p.append(k_p); wv_p.append(v_p); cv_p.append(c_p)
        mk_p_all.append(mk_p); mv_p_all.append(mv_p)
        ys, k_s, v_s, c_s = layer(ys, cache_mem_k[l], cache_mem_v[l], cache_win_k[l], cache_win_v[l],
                                  cache_conv[l], *lw)
        wk_s.append(k_s); wv_s.append(v_s); cv_s.append(c_s)
    return (yp, ys,
            jnp.stack(wk_p), jnp.stack(wv_p), jnp.stack(cv_p),
            jnp.stack(mk_p_all), jnp.stack(mv_p_all),
            jnp.stack(wk_s), jnp.stack(wv_s), jnp.stack(cv_s))
```

```python
import numpy as np
from contextlib import ExitStack
import concourse.bass as bass
import concourse.mybir as mybir
from concourse.bass_utils import run_bass_kernel_spmd

F32 = mybir.dt.float32
BF16 = mybir.dt.bfloat16
AF = mybir.ActivationFunctionType
ALU = mybir.AluOpType

DEPTH = 4
D = 1024
NCORES = 8
TH = 1152
SL = [(0, 512), (512, 512), (1024, 128)]
EPS = 1e-6
GL = 46
NSLOT = 3
PF = 2
D_FF = 2816
STOP = 0
POOL_STRICT = True
WARM_K = 3
POOLS_SHARED = True


class _Stop(Exception):
    pass


def _stage(k):
    if STOP == k:
        raise _Stop()


class Ev:
    __slots__ = ("sem", "val", "eng")

    def __init__(self, sem, val, eng):
        self.sem, self.val, self.eng = sem, val, eng


class Buf:
    def __init__(self, ap, key):
        self.ap, self.key = ap, key


class Ring:
    def __init__(self, bufs):
        self.bufs, self.i = bufs, 0

    def next(self):
        b = self.bufs[self.i % len(self.bufs)]
        self.i += 1
        return b


class Sched:
    CENG = ("pe", "act", "dve", "pool")

    def __init__(self, nc, es):
        self.nc, self.es = nc, es
        self.prog = {e: [] for e in ("pe", "act", "dve", "pool", "sp")}
        self.csem = {e: es.enter_context(nc.semaphore("c_" + e)) for e in self.CENG}
        self.cnt = {e: 0 for e in self.CENG}
        self.waited = {e: {} for e in self.prog}
        self.lastw = {}
        self.readers = {}
        self.dcnt = {}
        self.dsems = {}
        self.out_sems = set()

    def dsem(self, name):
        if name not in self.dsems:
            self.dsems[name] = self.es.enter_context(self.nc.semaphore(name))
            self.dcnt[name] = 0
        return name

    def _need(self, eng, ev, need, raw):
        if ev is None:
            return
        if ev.eng == eng:
            if eng == "pe" or eng == "sp":
                return
            if not raw and (eng != "pool" or not POOL_STRICT):
                return
        if need.get(ev.sem, 0) < ev.val:
            need[ev.sem] = ev.val

    def _deps(self, eng, reads, writes):
        need = {}
        for k in reads:
            self._need(eng, self.lastw.get(k), need, True)
        for k in writes:
            self._need(eng, self.lastw.get(k), need, False)
            for ev in self.readers.get(k, ()):
                self._need(eng, ev, need, False)
        waits = []
        for sid, val in need.items():
            if self.waited[eng].get(sid, 0) < val:
                self.waited[eng][sid] = val
                waits.append((sid, val))
        return waits

    def _commit(self, ev, reads, writes):
        for k in writes:
            self.lastw[k] = ev
            self.readers[k] = []
        for k in reads:
            self.readers.setdefault(k, []).append(ev)

    def op(self, eng, fn, reads=(), writes=(), inc=True):
        waits = self._deps(eng, reads, writes)
        if inc:
            self.cnt[eng] += 1
            ev = Ev("c_" + eng, self.cnt[eng], eng)
        else:
            ev = Ev("c_" + eng, self.cnt[eng] + 1, eng)
        self._commit(ev, reads, writes)
        self.prog[eng].append((waits, fn, ("c_" + eng, 1) if inc else None))

    def dma(self, q, out, in_, sem, reads=(), writes=(), is_out=False):
        self.dsem(sem)
        waits = self._deps(q, reads, writes)
        waits = [w for w in waits if w[0] != sem]
        self.dcnt[sem] += 16
        ev = Ev(sem, self.dcnt[sem], "dma")
        self._commit(ev, reads, writes)
        self.prog[q].append((waits, lambda e, o=out, i=in_: e.dma_start(out=o, in_=i), (sem, 16)))
        if is_out:
            self.out_sems.add(sem)

    def finish(self):
        waits = [(s, self.dcnt[s]) for s in sorted(self.dcnt) if self.dcnt[s] > 0]
        self.prog["sp"].append((waits, None, None))

    def semh(self, sid):
        return self.csem[sid[2:]] if sid.startswith("c_") else self.dsems[sid]

    def replay(self, eng, e):
        for waits, fn, inc in self.prog[eng]:
            for sid, val in waits:
                e.wait_ge(self.semh(sid), val)
            if fn is None:
                continue
            ins = fn(e)
            if inc is not None:
                ins.then_inc(self.semh(inc[0]), inc[1])


def build_program(n_layers=DEPTH):
    nc = bass.Bass("TRN2", target_bir_lowering=False)
    L = DEPTH

    def din(name, shape):
        return nc.dram_tensor(name, shape, F32, kind="ExternalInput").ap()

    def dout(name, shape):
        return nc.dram_tensor(name, shape, F32, kind="ExternalOutput").ap()

    xT_in = din("xT", [2, D, TH])
    memT_in = din("memT", [D, 256])
    kcT_in = din("kcT", [L, 4, 128, 128])
    vc_in = din("vc", [L, 4, 128, 128])
    cv_in = din("cv", [128, L * 16])
    mkcT_in = din("mkcT", [L, 4, 256, 256])
    mvc_in = din("mvc", [L, 4, 256, 256])
    WA = din("WA", [L, 35, 128, 2048])
    WD = din("WD", [L, 2, 8, 128, 1408])
    gtab_in = din("gtab", [128, L * GL])
    cdiag_in = din("cdiag", [L, 128, 768])

    yT = dout("yT", [2, D, TH])
    o_kp = dout("o_kp", [L, 128, 128])
    o_vp = dout("o_vp", [L, 128, 128])
    o_cp = dout("o_cp", [L, 128, 4])
    o_mk = dout("o_mk", [L, 2, 128, 256])
    o_mv = dout("o_mv", [L, 256, 256])
    o_ks = dout("o_ks", [L, 4, 128, 128])
    o_vs = dout("o_vs", [L, 4, 128, 128])
    o_cs = dout("o_cs", [L, 128, 2, 4, 2])

    with ExitStack() as es:
        S = Sched(nc, es)

        def sb(name, shape, dt):
            return es.enter_context(nc.sbuf_tensor(name, shape, dt))

        xT = sb("xT_sb", [128, 8, TH], F32)
        xn = sb("xn", [128, 8, TH], BF16)
        uq = sb("uq", [128, 12, TH], BF16)
        kf = sb("kf", [128, TH], F32)
        kh = sb("kh", [128, 128 + TH], BF16)
        vtok = sb("vtok", [128, 10, 128], BF16)
        wsl = [sb("wsl%d" % i, [128, 2048], BF16) for i in range(NSLOT)]
        gtab = sb("gtab_sb", [128, L * GL], F32)
        esink = sb("esink", [128, L * 4], F32)
        ones1024 = sb("ones1024", [128, 128], BF16)
        ones512 = sb("ones512", [128, 128], BF16)
        ones256 = sb("ones256", [128, 128], BF16)
        bd64 = sb("bd64", [128, 128], BF16)
        ones1 = sb("ones1", [128, 64], BF16)
        sq_ring = Ring([Buf(sb("sqb%d" % i, [128, 512], BF16), ("sqb", i)) for i in range(8)])
        ln_ring = Ring([Buf(sb("lnb%d" % i, [128, 512], F32), ("lnb", i)) for i in range(5)])
        a_f = sb("a_f", [128, 4, 512], F32)
        cy_f = sb("cy_f", [128, 2, 512], F32)
        mo_f = sb("mo_f", [128, 2, 512], F32)
        pe_ring = Ring([Buf(sb("pex%d" % i, [128, 2, 514], BF16), ("pex", i)) for i in range(2)])
        ptl_ring = Ring([Buf(sb("ptl%d" % i, [128, 2, 2], F32), ("ptl", i)) for i in range(2)])
        cdg = [sb("cdg%d" % i, [128, 768], BF16) for i in range(2)]
        pT_ring = Ring([Buf(sb("pT%d" % i, [128, 2, 256], BF16), ("pT", i)) for i in range(6)])
        den_ring = Ring([Buf(sb("den%d" % i, [128, 512], F32), ("den", i)) for i in range(2)])
        rcp_ring = Ring([Buf(sb("rcp%d" % i, [128, 512], F32), ("rcp", i)) for i in range(2)])
        sg_ring = Ring([Buf(sb("sg%d" % i, [128, 512], F32), ("sg", i)) for i in range(3)])
        vst_ring = Ring([Buf(sb("vst%d" % i, [128, 128], F32), ("vst", i)) for i in range(2)])
        kcTs = [sb("kcTs%d" % i, [128, 2, 128], BF16) for i in range(2)]
        vcs = [sb("vcs%d" % i, [128, 2, 128], BF16) for i in range(2)]
        mkcTs = [sb("mkcTs%d" % i, [128, 2, 2, 256], BF16) for i in range(2)]
        mvcs = [sb("mvcs%d" % i, [128, 2, 2, 256], BF16) for i in range(2)]
        cv = sb("cv_sb", [128, L * 16], F32)
        memT = sb("memT_sb", [128, 8, 256], F32)
        mrstd = sb("mrstd", [128, 256], F32)
        memn = sb("memn", [128, 8, 256], BF16)
        mk_f = sb("mk_f", [128, 2, 256], F32)
        mv_f = sb("mv_f", [128, 2, 256], F32)
        mkh = [sb("mkh%d" % l, [128, 2, 256], BF16) for l in range(L)]
        mvb = [sb("mvb%d" % l, [128, 2, 256], BF16) for l in range(L)]
        kcar = [sb("kcar%d" % l, [128, 128], BF16) for l in range(L)]
        vcar = [sb("vcar%d" % l, [128, 128], BF16) for l in range(L)]
        pcar = [sb("pcar%d" % l, [128, 2, 2], BF16) for l in range(L)]

        banks = [Buf(es.enter_context(nc.psum_tensor("psb%d" % i, [128, 512], F32)), ("ps", i)) for i in range(8)]
        if POOLS_SHARED:
            if WARM_K:
                pools = {"G": Ring(banks[0:4]), "S": Ring(banks[0:4]), "O": Ring(banks[4:6]), "O4": Ring(banks[4:7]), "N": Ring(banks[6:7])}
            else:
                pools = {"G": Ring(banks[0:4]), "S": Ring(banks[0:4]), "O": Ring(banks[4:6]), "O4": Ring(banks[4:8]), "N": Ring(banks[6:8])}
        else:
            pools = {"G": Ring(banks[0:4]), "S": Ring(banks[4:6]), "O": Ring(banks[6:7]), "N": Ring(banks[7:8])}

        def gcol(l, j, n=1):
            return gtab[:, l * GL + j: l * GL + j + n]

        evac_ctr = [0]

        def copy_op(out, in_, reads, writes, eng=None):
            if eng is None:
                eng = ("act", "dve")[evac_ctr[0] % 2]
                evac_ctr[0] += 1
            if eng == "act":
                S.op("act", lambda e, o=out, i=in_: e.activation(out=o, in_=i, func=AF.Copy), reads, writes)
            else:
                S.op("dve", lambda e, o=out, i=in_: e.tensor_copy(out=o, in_=i), reads, writes)

        S.op("pool", lambda e: e.memset(ones1024[:], 1.0 / 1024), writes=["ones1024"])
        S.op("pool", lambda e: e.memset(ones512[:], 1.0 / 512), writes=["ones512"])
        S.op("pool", lambda e: e.memset(ones256[:], 1.0 / 256), writes=["ones256"])
        S.op("pool", lambda e: e.memset(ones1[:], 1.0), writes=["ones1"])
        S.op("pool", lambda e: e.memset(bd64[:], 0.0), writes=["bd64"])
        S.op("pool", lambda e: e.memset(bd64[0:64, 0:64], 1.0 / 64), writes=["bd64"])
        S.op("pool", lambda e: e.memset(bd64[64:128, 64:128], 1.0 / 64), writes=["bd64"])
        S.dma("sp", gtab[:], gtab_in, "ld_init", writes=["gtab"])
        S.dma("sp", cv[:], cv_in, "ld_init", writes=["cv"])
        S.dma("sp", memT[:], memT_in.rearrange("(c p) t -> p c t", p=128), "ld_init", writes=["memT"])
        for k in ("gtab", "cv", "memT"):
            S.lastw[k] = Ev("ld_init", S.dcnt["ld_init"], "dma")
        for l in range(L):
            for s in range(4):
                S.dma("sp", o_ks[l, s, :, 0:64], kcT_in[l, s, :, 64:128], "st_d2d", is_out=True)
                S.dma("sp", o_vs[l, s, 0:64, :], vc_in[l, s, 64:128, :], "st_d2d", is_out=True)
        for l in range(L):
            S.op("act", lambda e, l=l: e.activation(out=esink[:, 4 * l:4 * l + 4], in_=gcol(l, 42, 4), func=AF.Exp),
                 reads=["gtab"], writes=["esink"])

        wseq = []
        for h in range(2):
            for l in range(n_layers):
                if h == 0:
                    wseq += [WA[l, 7], WA[l, 8]]
                wseq += [WA[l, t] for t in (0, 1, 2, 3, 4, 5, 6, 9, 10, 11, 12)]
                for hf in range(2):
                    wseq += [WA[l, 13 + 11 * hf + jj] for jj in range(11)]
                    wseq += [WD[l, hf, c] for c in range(8)]
        wstate = {"next_load": 0, "next_use": 0}

        def wtile():
            i = wstate["next_use"]
            wstate["next_use"] += 1
            while wstate["next_load"] <= min(i + PF, len(wseq) - 1):
                j = wstate["next_load"]
                src = wseq[j]
                E = src.shape[-1]
                S.dma("pool", wsl[j % NSLOT][:, 0:E], src, "ld_w%d" % (j % NSLOT), writes=[("w", j % NSLOT)])
                wstate["next_load"] += 1
            return wsl[i % NSLOT], ("w", i % NSLOT)

        def rms_rstd_batch(chains):
            assert len(chains) <= 3 and sum(len(c[0]) for c in chains) <= 8
            sqs = []
            for (srcs, ones_ap, ones_key, n) in chains:
                row = []
                for (ap, keys) in srcs:
                    sq = sq_ring.next()
                    S.op("act", lambda e, o=sq.ap[:, 0:n], a=ap: e.activation(out=o, in_=a, func=AF.Square),
                         reads=keys, writes=[sq.key])
                    row.append(sq)
                sqs.append(row)
            bks = []
            for (srcs, ones_ap, ones_key, n), row in zip(chains, sqs):
                bank = pools["N"].next()
                last = len(row) - 1
                for i, sq in enumerate(row):
                    S.op("pe", lambda e, o=bank.ap[:, 0:n], w=ones_ap, r=sq.ap[:, 0:n], i=i, last=last:
                         e.matmul(o, w, r, start=(i == 0), stop=(i == last)),
                         reads=[sq.key, ones_key], writes=[bank.key], inc=True)
                ln = ln_ring.next()
                S.op("act", lambda e, o=ln.ap[:, 0:n], a=bank.ap[:, 0:n]: e.activation(out=o, in_=a, func=AF.Ln, bias=EPS, scale=1.0),
                     reads=[bank.key], writes=[ln.key])
                bks.append(ln)
            out = []
            for (srcs, ones_ap, ones_key, n), ln in zip(chains, bks):
                S.op("act", lambda e, o=ln.ap[:, 0:n], a=ln.ap[:, 0:n]: e.activation(out=o, in_=a, func=AF.Exp, scale=-0.5),
                     reads=[ln.key], writes=[ln.key])
                out.append(ln)
            return out

        def rms_rstd(srcs, ones_ap, ones_key, n):
            return rms_rstd_batch([(srcs, ones_ap, ones_key, n)])[0]

        def norm_apply(out, in_, g_ap, rs, n, reads, writes):
            S.op("dve", lambda e, o=out, i=in_, g=g_ap, r=rs.ap[:, 0:n]:
                 e.scalar_tensor_tensor(out=o, in0=i, scalar=g, in1=r, op0=ALU.mult, op1=ALU.mult),
                 reads=list(reads) + [rs.key, "gtab"], writes=writes)

        def gemm_B(wt, wkey, KC, NCOL, ocs, rhs_fn, evac_fn):
            wv = wt[:, 0:KC * NCOL].rearrange("p (k n) -> p k n", k=KC)
            for oi in ocs:
                for si, (t0, n) in enumerate(SL):
                    bank = pools["G"].next()
                    for kc in range(KC):
                        rap, rkeys = rhs_fn(kc, si)
                        S.op("pe", lambda e, o=bank.ap[:, 0:n], w=wv[:, kc, oi * 128:(oi + 1) * 128], r=rap, kc=kc:
                             e.matmul(o, w, r, start=(kc == 0), stop=(kc == KC - 1)),
                             reads=[wkey] + rkeys, writes=[bank.key], inc=(kc == KC - 1))
                    evac_fn(oi, si, bank)

        pending = []

        def gemm_groups(get_wt, KC, NCOL, ocs, rhs_fn, evac_fn):
            st = {}

            def grp(oi, si):
                if "wt" not in st:
                    st["wt"], st["wkey"] = get_wt()
                wt, wkey = st["wt"], st["wkey"]
                wv = wt[:, 0:KC * NCOL].rearrange("p (k n) -> p k n", k=KC)
                t0, n = SL[si]
                bank = pools["G"].next()
                for kc in range(KC):
                    rap, rkeys = rhs_fn(kc, si)
                    S.op("pe", lambda e, o=bank.ap[:, 0:n], w=wv[:, kc, oi * 128:(oi + 1) * 128], r=rap, kc=kc:
                         e.matmul(o, w, r, start=(kc == 0), stop=(kc == KC - 1)),
                         reads=[wkey] + rkeys, writes=[bank.key], inc=(kc == KC - 1))
                evac_fn(oi, si, bank)
            return [(lambda oi=oi, si=si: grp(oi, si)) for oi in ocs for si in range(len(SL))]

        def warm(k=None):
            for _ in range(WARM_K if k is None else k):
                S.op("pe", lambda e: e.matmul(banks[7].ap[:, 0:512], ones1024[:],
                                              ones1024[:, 0:128].unsqueeze(1).to_broadcast([128, 4, 128]), start=True, stop=True),
                     reads=["ones1024"], writes=[], inc=False)

        def pump(k=1):
            for _ in range(k):
                if pending:
                    pending.pop(0)()

        def xn_rhs(kc, si):
            t0, n = SL[si]
            return xn[:, kc, t0:t0 + n], [("xn", kc, si)]

        rs_m = rms_rstd([(memT[:, c, :], ["memT"]) for c in range(8)], ones1024[:], "ones1024", 256)
        copy_op(mrstd[:], rs_m.ap[:, 0:256], [rs_m.key], ["mrstd"], eng="dve")

        def main_body():
            for h in range(2):
                S.dma("sp", xT[:], xT_in[h].rearrange("(c p) t -> p c t", p=128), "ld_x",
                      writes=[("x", c, si) for c in range(8) for si in range(3)])
                for l in range(n_layers):
                    par = (h * L + l) % 2
                    ckey = ("cache", par)
                    S.dma("pool", cdg[par][:], cdiag_in[l], "ld_c%d" % par, writes=[ckey])
                    for j in range(2):
                        s = 2 * h + j
                        S.dma("pool", kcTs[par][:, j, :], kcT_in[l, s], "ld_c%d" % par, writes=[ckey])
                        S.dma("pool", vcs[par][:, j, :], vc_in[l, s], "ld_c%d" % par, writes=[ckey])
                        S.dma("pool", mkcTs[par][:, j, :, :], mkcT_in[l, s].rearrange("(c p) k -> p c k", p=128),
                              "ld_c%d" % par, writes=[ckey])
                        S.dma("pool", mvcs[par][:, j, :, :], mvc_in[l, s].rearrange("(t p) f -> p t f", p=128),
                              "ld_c%d" % par, writes=[ckey])

                    for si, (t0, n) in enumerate(SL):
                        rs = rms_rstd([(xT[:, c, t0:t0 + n], [("x", c, si)]) for c in range(8)], ones1024[:], "ones1024", n)
                        for c in range(8):
                            norm_apply(xn[:, c, t0:t0 + n], xT[:, c, t0:t0 + n], gcol(l, c), rs, n,
                                       [("x", c, si)], [("xn", c, si)])

                    _stage(2)
                    if h == 0:
                        for c in range(8):
                            S.op("dve", lambda e, c=c, l=l: e.scalar_tensor_tensor(
                                out=memn[:, c, :], in0=memT[:, c, :], scalar=gcol(l, 16 + c), in1=mrstd[:],
                                op0=ALU.mult, op1=ALU.mult), reads=["memT", "mrstd", "gtab"], writes=[("memn", c)])
                        _stage(21)
                        wt, wkey = wtile()
                        wv = wt[:].rearrange("p (k n) -> p k n", k=8)
                        for c in range(2):
                            bank = pools["G"].next()
                            for kc in range(8):
                                S.op("pe", lambda e, o=bank.ap[:, 0:256], w=wv[:, kc, c * 128:(c + 1) * 128], r=memn[:, kc, :], kc=kc:
                                     e.matmul(o, w, r, start=(kc == 0), stop=(kc == 7)),
                                     reads=[wkey, ("memn", kc)], writes=[bank.key], inc=(kc == 7))
                            copy_op(mk_f[:, c, :], bank.ap[:, 0:256], [bank.key], [("mk_f", c)])
                            rs = rms_rstd([(mk_f[:, c, :], [("mk_f", c)])], bd64[:], "bd64", 256)
                            norm_apply(mk_f[:, c, :], mk_f[:, c, :], gcol(l, 35), rs, 256, [("mk_f", c)], [("mk_f", c)])
                            copy_op(mkh[l][:, c, :], mk_f[:, c, :], [("mk_f", c)], [("mkh", l)])
                            S.dma("sp", o_mk[l, c], mk_f[:, c, :], "st_mk%d" % c, reads=[("mk_f", c)], is_out=True)
                        _stage(22)
                        wt, wkey = wtile()
                        wv = wt[:].rearrange("p (k n) -> p k n", k=8)
                        for kt in range(2):
                            bank = pools["G"].next()
                            for kc in range(8):
                                S.op("pe", lambda e, o=bank.ap[:, 0:256], w=memn[:, kc, kt * 128:(kt + 1) * 128], r=wv[:, kc, :], kc=kc:
                                     e.matmul(o, w, r, start=(kc == 0), stop=(kc == 7)),
                                     reads=[wkey, ("memn", kc)], writes=[bank.key], inc=(kc == 7))
                            copy_op(mv_f[:, kt, :], bank.ap[:, 0:256], [bank.key], [("mv_f", kt)], eng="act")
                            copy_op(mvb[l][:, kt, :], mv_f[:, kt, :], [("mv_f", kt)], [("mvb", l)], eng="dve")
                            S.dma("sp", o_mv[l, kt * 128:(kt + 1) * 128, :], mv_f[:, kt, :], "st_mv%d" % kt, reads=[("mv_f", kt)], is_out=True)

                    _stage(3)
                    if h == 1:
                        copy_op(kh[:, 0:128], kcar[l][:], [("kcar", l)], [("kh", -1)])
                        copy_op(vtok[:, 0, :], vcar[l][:], [("vcar", l)], [("v", 0)])
                    umap = {0: (0, 1), 1: (2, 3), 3: (4, 5), 4: (6, 7), 5: (8, 9), 6: (10, 11)}
                    for t in range(7):
                        if t != 2:
                            chunks = umap[t]

                            def ev_u(oi, si, bank, chunks=chunks, eng=(None if t < 2 else "dve")):
                                t0, n = SL[si]
                                copy_op(uq[:, chunks[oi], t0:t0 + n], bank.ap[:, 0:n], [bank.key], [("uq", chunks[oi], si)], eng=eng)
                            grps = gemm_groups(wtile, 8, 256, (0, 1), xn_rhs, ev_u)
                            if t < 2:
                                for gfn in grps:
                                    gfn()
                            else:
                                pending.extend(grps)
                        else:
                            wt, wkey = wtile()
                            def ev_k(oi, si, bank):
                                t0, n = SL[si]
                                copy_op(kf[:, t0:t0 + n], bank.ap[:, 0:n], [bank.key], [("kf", si)])
                            gemm_B(wt, wkey, 8, 256, (0,), xn_rhs, ev_k)
                            wv = wt[:].rearrange("p (k n) -> p k n", k=8)
                            for tt in range(9):
                                si = tt // 4
                                bank = pools["G"].next()
                                for kc in range(8):
                                    S.op("pe", lambda e, o=bank.ap[:, 0:128], w=xn[:, kc, tt * 128:(tt + 1) * 128], r=wv[:, kc, 128:256], kc=kc:
                                         e.matmul(o, w, r, start=(kc == 0), stop=(kc == 7)),
                                         reads=[wkey, ("xn", kc, si)], writes=[bank.key], inc=(kc == 7))
                                if tt == 8 or (tt == 7 and h == 1):
                                    vs = vst_ring.next()
                                    sem = "st_v%d" % vs.key[1]
                                    copy_op(vs.ap[:], bank.ap[:, 0:128], [bank.key], [vs.key], eng="act")
                                    copy_op(vtok[:, 1 + tt, :], vs.ap[:], [vs.key], [("v", 1 + tt)], eng="dve")
                                    if tt == 7:
                                        S.dma("sp", o_vp[l], vs.ap[:], sem, reads=[vs.key], is_out=True)
                                    else:
                                        for j in range(2):
                                            S.dma("sp", o_vs[l, 2 * h + j, 64:128, :], vs.ap[64 * j:64 * j + 64, :], sem,
                                                  reads=[vs.key], is_out=True)
                                else:
                                    copy_op(vtok[:, 1 + tt, :], bank.ap[:, 0:128], [bank.key], [("v", 1 + tt)])
                                if tt == 7 and h == 0:
                                    copy_op(vcar[l][:], vtok[:, 1 + tt, :], [("v", 1 + tt)], [("vcar", l)])

                    _stage(4)
                    for si, (t0, n) in enumerate(SL):
                        hn = [(c, uq[:, c, t0:t0 + n], [("uq", c, si)], gcol(l, 32)) for c in (0, 1, 2, 3)]
                        hn.append((-1, kf[:, t0:t0 + n], [("kf", si)], gcol(l, 33)))
                        for b0 in range(0, len(hn), 3):
                            grp = hn[b0:b0 + 3]
                            rss = rms_rstd_batch([([(ap, keys)], bd64[:], "bd64", n) for (_, ap, keys, _) in grp])
                            pump(2)
                            for (c, ap, keys, g_ap), rs in zip(grp, rss):
                                norm_apply(ap, ap, g_ap, rs, n, keys, keys)
                        copy_op(kh[:, 128 + t0:128 + t0 + n], kf[:, t0:t0 + n], [("kf", si)], [("kh", si)], eng="dve")
                        if si == 1 and h == 0:
                            copy_op(kcar[l][:], kf[:, 896:1024], [("kf", 1)], [("kcar", l)], eng="dve")
                        if si == 1 and h == 1:
                            S.dma("sp", o_kp[l], kf[:, 896:1024], "st_k1", reads=[("kf", 1)], is_out=True)
                        if si == 2:
                            for j in range(2):
                                S.dma("sp", o_ks[l, 2 * h + j, :, 64:128], kf[:, 1024 + 64 * j:1088 + 64 * j], "st_k2",
                                      reads=[("kf", 2)], is_out=True)

                    _stage(5)
                    pe_prev = None
                    for phase in (0, 1):
                        if phase == 1:
                            pump(len(pending))
                            for si, (t0, n) in enumerate(SL):
                                hn = [(c, uq[:, c, t0:t0 + n], [("uq", c, si)], gcol(l, 34)) for c in (10, 11)]
                                rss = rms_rstd_batch([([(ap, keys)], bd64[:], "bd64", n) for (_, ap, keys, _) in hn])
                                for (c, ap, keys, g_ap), rs in zip(hn, rss):
                                    norm_apply(ap, ap, g_ap, rs, n, keys, keys)
                        for si, (t0, n) in enumerate(SL):
                            if phase == 1:
                                segs = [(t0, n, None)] if si < 2 else [(t0, 64, 0), (t0 + 64, 64, 1)]
                                for (s0, sn, j) in segs:
                                    pe = pe_ring.next()
                                    if j is not None:
                                        c0 = 16 * l
                                        src = cv[:, c0:c0 + 16].rearrange("p (c s r) -> p c s r", c=2, s=4)[:, :, 2 * h + j, :]
                                        copy_op(pe.ap[:, :, 0:2], src, ["cv"], [pe.key], eng="dve")
                                    elif si == 0 and h == 0:
                                        S.op("dve", lambda e, o=pe.ap[:, :, 0:2]: e.memset(o, 0.0), writes=[pe.key])
                                    elif si == 0:
                                        copy_op(pe.ap[:, :, 0:2], pcar[l][:], [("pcar", l)], [pe.key], eng="dve")
                                    else:
                                        copy_op(pe.ap[:, :, 0:2], pe_prev.ap[:, :, 512:514], [pe_prev.key], [pe.key], eng="dve")
                                    ckeys = [("uq", c, si) for c in (6, 7, 8, 9)]
                                    S.op("dve", lambda e, o=pe.ap[:, :, 2:2 + sn], a=uq[:, 6:8, s0:s0 + sn], b=uq[:, 8:10, s0:s0 + sn]:
                                         e.tensor_tensor(out=o, in0=a, in1=b, op=ALU.mult), reads=ckeys, writes=[pe.key])
                                    want_out = (j is not None) or (si == 1 and h == 1)
                                    if want_out:
                                        ptl = ptl_ring.next()
                                        e0 = s0 + sn - 2
                                        S.op("dve", lambda e, o=ptl.ap[:], a=uq[:, 6:8, e0:e0 + 2], b=uq[:, 8:10, e0:e0 + 2]:
                                             e.tensor_tensor(out=o, in0=a, in1=b, op=ALU.mult), reads=ckeys, writes=[ptl.key])
                                        sem = "st_ptl%d" % ptl.key[1]
                                        if j is not None:
                                            S.dma("sp", o_cs[l, :, :, 2 * h + j, :], ptl.ap[:], sem, reads=[ptl.key], is_out=True)
                                        else:
                                            S.dma("sp", o_cp[l].rearrange("p (c r) -> p c r", c=2), ptl.ap[:], sem, reads=[ptl.key], is_out=True)
                                    if si == 1 and h == 0:
                                        copy_op(pcar[l][:], pe.ap[:, :, 512:514], [pe.key], [("pcar", l)], eng="dve")
                                    off = 0 if j is None else 64 * j
                                    for c in range(2):
                                        bank = pools["S"].next()
                                        for r in range(3):
                                            S.op("pe", lambda e, o=bank.ap[:, 0:sn], w=cdg[par][:, (2 * r + c) * 128:(2 * r + c + 1) * 128],
                                                 x=pe.ap[:, c, r:r + sn], r=r:
                                                 e.matmul(o, w, x, start=(r == 0), stop=(r == 2)),
                                                 reads=[pe.key, ckey], writes=[bank.key], inc=(r == 2))
                                        S.op("dve", lambda e, o=cy_f[:, c, off:off + sn], a=bank.ap[:, 0:sn], b=uq[:, 4 + c, s0:s0 + sn]:
                                             e.tensor_tensor(out=o, in0=a, in1=b, op=ALU.mult),
                                             reads=[bank.key, ("uq", 4 + c, si)], writes=["cy_f"])
                                    pe_prev = pe

                            if phase == 0:
                                nq = n // 64
                                chunk_state = {}

                                def swa_blocks(qi):
                                    qc = t0 + 64 * qi
                                    blocks = []
                                    if si < 2:
                                        m = qc // 64
                                        mg = 16 * h + m
                                        lo = max(0, mg - 2) - 16 * h
                                        nl = lo
                                        while nl <= m:
                                            if nl % 2 == 0 and nl + 1 <= m:
                                                nk, pb = 128, 0
                                            else:
                                                nk, pb = 64, 64 * (nl % 2)
                                            col = 128 + 64 * nl
                                            kkey = ("kh", -1) if nl < 0 else ("kh", (64 * nl) // 512)
                                            tile = (nl + 2) // 2
                                            blocks.append((lambda g, col=col, nk=nk: kh[64 * g:64 * g + 64, col:col + nk], [kkey],
                                                           lambda g, tile=tile, pb=pb, nk=nk: vtok[pb:pb + nk, tile, 64 * g:64 * g + 64],
                                                           [("v", tile)], pb, nk))
                                            nl += nk // 64
                                    else:
                                        j = qi
                                        blocks.append((lambda g, j=j: kcTs[par][64 * g:64 * g + 64, j, :], [ckey],
                                                       lambda g, j=j: vcs[par][:, j, 64 * g:64 * g + 64], [ckey], 0, 128))
                                        col = 128 + 1024 + 64 * j
                                        pb = 64 * j
                                        blocks.append((lambda g, col=col: kh[64 * g:64 * g + 64, col:col + 64], [("kh", 2)],
                                                       lambda g, pb=pb: vtok[pb:pb + 64, 9, 64 * g:64 * g + 64], [("v", 9)], pb, 64))
                                    return blocks

                                def swa_A(qi):
                                    qc = t0 + 64 * qi
                                    blocks = swa_blocks(qi)
                                    qkeys = [("uq", c, si) for c in range(4)]
                                    pTs = []
                                    for g in range(2):
                                        bank = pools["S"].next()
                                        pT = pT_ring.next()
                                        pTs.append(pT)
                                        for bi, (kfn, kkeys, vfn, vkeys, pb, nk) in enumerate(blocks):
                                            S.op("pe", lambda e, o=bank.ap[pb:pb + nk, bi * 256:(bi + 1) * 256], w=kfn(g),
                                                 r=uq[64 * g:64 * g + 64, 0:4, qc:qc + 64], g=g, pb=pb:
                                                 e.matmul(o, w, r, start=True, stop=True, tile_position=(64 * g, pb)),
                                                 reads=kkeys + qkeys, writes=[bank.key], inc=True)
                                        for bi, (kfn, kkeys, vfn, vkeys, pb, nk) in enumerate(blocks):
                                            S.op("act", lambda e, o=pT.ap[pb:pb + nk, bi, :], a=bank.ap[pb:pb + nk, bi * 256:(bi + 1) * 256]:
                                                 e.activation(out=o, in_=a, func=AF.Exp, scale=0.125),
                                                 reads=[bank.key], writes=[pT.key])
                                    chunk_state[qi] = (blocks, pTs)

                                def swa_B(qi):
                                    blocks, pTs = chunk_state.pop(qi)
                                    bo = pools["O"].next()
                                    nb = len(blocks)
                                    for g in range(2):
                                        for part in range(2):
                                            for bi, (kfn, kkeys, vfn, vkeys, pb, nk) in enumerate(blocks):
                                                lhs = vfn(g) if part == 0 else ones1[pb:pb + nk, 0:64]
                                                S.op("pe", lambda e, o=bo.ap[64 * g:64 * g + 64, part * 256:(part + 1) * 256], w=lhs,
                                                     r=pTs[g].ap[pb:pb + nk, bi, :], bi=bi, pb=pb, g=g:
                                                     e.matmul(o, w, r, start=(bi == 0), stop=(bi == nb - 1), tile_position=(pb, 64 * g)),
                                                     reads=(vkeys if part == 0 else ["ones1"]) + [pTs[g].key], writes=[bo.key],
                                                     inc=(bi == nb - 1))
                                    den = den_ring.next()
                                    rcp = rcp_ring.next()
                                    S.op("dve", lambda e, o=den.ap[:, 0:256].rearrange("p (a q) -> p a q", a=4),
                                         i=bo.ap[:, 256:512].rearrange("p (a q) -> p a q", a=4),
                                         b=esink[:, 4 * l:4 * l + 4].unsqueeze(2).to_broadcast([128, 4, 64]):
                                         e.tensor_tensor(out=o, in0=i, in1=b, op=ALU.add),
                                         reads=[bo.key, "esink"], writes=[den.key])
                                    S.op("act", lambda e, o=rcp.ap[:, 0:256], i=den.ap[:, 0:256]: e.activation(out=o, in_=i, func=AF.Ln),
                                         reads=[den.key], writes=[rcp.key])
                                    S.op("act", lambda e, o=rcp.ap[:, 0:256], i=rcp.ap[:, 0:256]: e.activation(out=o, in_=i, func=AF.Exp, scale=-1.0),
                                         reads=[rcp.key], writes=[rcp.key])
                                    S.op("dve", lambda e, o=a_f[:, :, 64 * qi:64 * qi + 64],
                                         i=bo.ap[:, 0:256].rearrange("p (a q) -> p a q", a=4),
                                         r=rcp.ap[:, 0:256].rearrange("p (a q) -> p a q", a=4):
                                         e.tensor_tensor(out=o, in0=i, in1=r, op=ALU.mult),
                                         reads=[bo.key, rcp.key], writes=["a_f"])

                                ustate = {}

                                def pair_A(u):
                                    pi, g = divmod(u, 2)
                                    m = 8 * si + 2 * pi
                                    mg = 16 * h + m
                                    qc = 64 * m
                                    tiles = []
                                    if mg >= 2:
                                        nl = m - 2
                                        tiles.append((128 + 64 * nl, ("kh", -1) if nl < 0 else ("kh", (64 * nl) // 512), m // 2, "a"))
                                    tiles.append((128 + 64 * m, ("kh", si), m // 2 + 1, "b"))
                                    qkeys = [("uq", c, si) for c in range(4)]
                                    res = []
                                    for (col, kkey, vt, kind) in tiles:
                                        bank = pools["S"].next()
                                        pT = pT_ring.next()
                                        S.op("pe", lambda e, o=bank.ap[:, 0:512], w=kh[64 * g:64 * g + 64, col:col + 128],
                                             r=uq[64 * g:64 * g + 64, 0:4, qc:qc + 128], g=g:
                                             e.matmul(o, w, r, start=True, stop=True, tile_position=(64 * g, 0)),
                                             reads=[kkey] + qkeys, writes=[bank.key], inc=True)
                                        pv = pT.ap[:].rearrange("p b q -> p (b q)")
                                        S.op("act", lambda e, o=pv, a=bank.ap[:, 0:512]: e.activation(out=o, in_=a, func=AF.Exp, scale=0.125),
                                             reads=[bank.key], writes=[pT.key])
                                        p4 = pv.rearrange("p (a t) -> p a t", a=4)
                                        z = p4[0:64, :, 64:128] if kind == "a" else p4[64:128, :, 0:64]
                                        S.op("dve", lambda e, o=z: e.memset(o, 0.0), reads=[pT.key], writes=[pT.key])
                                        res.append((pv, pT.key, vt))
                                    ustate[u] = res

                                def pair_B(u):
                                    pi, g = divmod(u, 2)
                                    res = ustate.pop(u)
                                    if g == 0:
                                        ustate[("bo", pi)] = (pools["O4"].next(), pools["O4"].next())
                                    bo_pv, bo_dn = ustate[("bo", pi)]
                                    nt = len(res)
                                    for part, bo in ((0, bo_pv), (1, bo_dn)):
                                        for ti, (pv, pkey, vt) in enumerate(res):
                                            lhs = vtok[:, vt, 64 * g:64 * g + 64] if part == 0 else ones1[:, 0:64]
                                            S.op("pe", lambda e, o=bo.ap[64 * g:64 * g + 64, 0:512], w=lhs, r=pv, ti=ti, g=g:
                                                 e.matmul(o, w, r, start=(ti == 0), stop=(ti == nt - 1), tile_position=(0, 64 * g)),
                                                 reads=([("v", vt)] if part == 0 else ["ones1"]) + [pkey], writes=[bo.key],
                                                 inc=(ti == nt - 1))
                                    if g == 1:
                                        del ustate[("bo", pi)]
                                        den = den_ring.next()
                                        rcp = rcp_ring.next()
                                        S.op("dve", lambda e, o=den.ap[:].rearrange("p (a t) -> p a t", a=4),
                                             i=bo_dn.ap[:, 0:512].rearrange("p (a t) -> p a t", a=4),
                                             b=esink[:, 4 * l:4 * l + 4].unsqueeze(2).to_broadcast([128, 4, 128]):
                                             e.tensor_tensor(out=o, in0=i, in1=b, op=ALU.add),
                                             reads=[bo_dn.key, "esink"], writes=[den.key])
                                        S.op("act", lambda e, o=rcp.ap[:], i=den.ap[:]: e.activation(out=o, in_=i, func=AF.Ln),
                                             reads=[den.key], writes=[rcp.key])
                                        S.op("act", lambda e, o=rcp.ap[:]: e.activation(out=o, in_=o, func=AF.Exp, scale=-1.0),
                                             reads=[rcp.key], writes=[rcp.key])
                                        S.op("dve", lambda e, o=a_f[:, :, 128 * pi:128 * pi + 128],
                                             i=bo_pv.ap[:, 0:512].rearrange("p (a t) -> p a t", a=4),
                                             r=rcp.ap[:].rearrange("p (a t) -> p a t", a=4):
                                             e.tensor_tensor(out=o, in0=i, in1=r, op=ALU.mult),
                                             reads=[bo_pv.key, rcp.key], writes=["a_f"])

                                if si < 2:
                                    nu = 8
                                    for u in range(nu + 1):
                                        if u < nu:
                                            pair_A(u)
                                            pump(1)
                                            warm()
                                        if u >= 1:
                                            pair_B(u - 1)
                                            pump(1)
                                            warm()
                                else:
                                    for qi in range(nq + 1):
                                        if qi < nq:
                                            swa_A(qi)
                                            pump(1)
                                            warm()
                                        if qi >= 1:
                                            swa_B(qi - 1)
                                            pump(1)
                                            warm()

                                pump(len(pending))
                                rs = rms_rstd([(a_f[:, c, 0:n], ["a_f"]) for c in range(4)], ones512[:], "ones512", n)
                                for c in range(4):
                                    norm_apply(xn[:, c, t0:t0 + n], a_f[:, c, 0:n], gcol(l, 24 + c), rs, n, ["a_f"], [("xn", c, si)])
                            if phase == 1:
                                _stage(7)
                                if si < 2:
                                    units = [(t0 + 256 * sub, 256, 256 * sub, None) for sub in range(2)]
                                else:
                                    units = [(t0 + 64 * j, 64, 64 * j, j) for j in range(2)]
                                mem_items = [(u, c) for u in units for c in range(2)]
                                mstate = {}

                                def mem_A(idx):
                                    (u0, un, uoff, j), c = mem_items[idx]
                                    pTs = []
                                    for hj in range(2):
                                        bank = pools["S"].next()
                                        pT = pT_ring.next()
                                        pTs.append(pT)
                                        for kt in range(2):
                                            if j is None:
                                                lhs, lkeys = mkh[l][64 * hj:64 * hj + 64, c, kt * 128:(kt + 1) * 128], [("mkh", l)]
                                            else:
                                                lhs, lkeys = mkcTs[par][64 * hj:64 * hj + 64, j, c, kt * 128:(kt + 1) * 128], [ckey]
                                            S.op("pe", lambda e, o=bank.ap[:, kt * 256:kt * 256 + un], w=lhs,
                                                 r=uq[64 * hj:64 * hj + 64, 10 + c, u0:u0 + un], hj=hj:
                                                 e.matmul(o, w, r, start=True, stop=True, tile_position=(64 * hj, 0)),
                                                 reads=lkeys + [("uq", 10 + c, si)], writes=[bank.key], inc=True)
                                        S.op("act", lambda e, o=pT.ap[:, :, 0:un], a=bank.ap[:].rearrange("p (k q) -> p k q", k=2)[:, :, 0:un]:
                                             e.activation(out=o, in_=a, func=AF.Exp, scale=0.125),
                                             reads=[bank.key], writes=[pT.key])
                                    mstate[idx] = pTs

                                def mem_B(idx):
                                    (u0, un, uoff, j), c = mem_items[idx]
                                    pTs = mstate.pop(idx)
                                    bo = pools["O"].next()
                                    for hj in range(2):
                                        hcol = (2 * c + hj) * 64
                                        for part in range(2):
                                            for kt in range(2):
                                                if part == 1:
                                                    lhs, lkeys = ones1[:, 0:64], ["ones1"]
                                                elif j is None:
                                                    lhs, lkeys = mvb[l][:, kt, hcol:hcol + 64], [("mvb", l)]
                                                else:
                                                    lhs, lkeys = mvcs[par][:, j, kt, hcol:hcol + 64], [ckey]
                                                S.op("pe", lambda e, o=bo.ap[64 * hj:64 * hj + 64, part * 256:part * 256 + un], w=lhs,
                                                     r=pTs[hj].ap[:, kt, 0:un], kt=kt, hj=hj:
                                                     e.matmul(o, w, r, start=(kt == 0), stop=(kt == 1), tile_position=(0, 64 * hj)),
                                                     reads=lkeys + [pTs[hj].key], writes=[bo.key], inc=(kt == 1))
                                    rcp = rcp_ring.next()
                                    S.op("act", lambda e, o=rcp.ap[:, 0:un], i=bo.ap[:, 256:256 + un]: e.activation(out=o, in_=i, func=AF.Ln),
                                         reads=[bo.key], writes=[rcp.key])
                                    S.op("act", lambda e, o=rcp.ap[:, 0:un], i=rcp.ap[:, 0:un]: e.activation(out=o, in_=i, func=AF.Exp, scale=-1.0),
                                         reads=[rcp.key], writes=[rcp.key])
                                    S.op("dve", lambda e, o=mo_f[:, c, uoff:uoff + un], i=bo.ap[:, 0:un], r=rcp.ap[:, 0:un]:
                                         e.tensor_tensor(out=o, in0=i, in1=r, op=ALU.mult),
                                         reads=[bo.key, rcp.key], writes=["mo_f"])

                                for idx in range(len(mem_items) + 1):
                                    if idx < len(mem_items):
                                        mem_A(idx)
                                        warm()
                                    if idx >= 1:
                                        mem_B(idx - 1)
                                        warm()

                                _stage(8)
                                rss = rms_rstd_batch([
                                    ([(cy_f[:, c, 0:n], ["cy_f"]) for c in range(2)], ones256[:], "ones256", n),
                                    ([(mo_f[:, c, 0:n], ["mo_f"]) for c in range(2)], ones256[:], "ones256", n)])
                                for c in range(2):
                                    norm_apply(xn[:, 4 + c, t0:t0 + n], cy_f[:, c, 0:n], gcol(l, 28 + c), rss[0], n, ["cy_f"], [("xn", 4 + c, si)])
                                for c in range(2):
                                    norm_apply(xn[:, 6 + c, t0:t0 + n], mo_f[:, c, 0:n], gcol(l, 30 + c), rss[1], n, ["mo_f"], [("xn", 6 + c, si)])

                    _stage(9)
                    def ev_res(oc):
                        def f(oi, si, bank, oc=oc):
                            t0, n = SL[si]
                            c = oc(oi)
                            S.op("dve", lambda e, o=xT[:, c, t0:t0 + n], b=bank.ap[:, 0:n]:
                                 e.tensor_tensor(out=o, in0=b, in1=o, op=ALU.add),
                                 reads=[bank.key, ("x", c, si)], writes=[("x", c, si)])
                        return f
                    for t in range(4):
                        wt, wkey = wtile()
                        gemm_B(wt, wkey, 8, 256, (0, 1), xn_rhs, ev_res(lambda oi, t=t: 2 * t + oi))

                    _stage(10)
                    for si, (t0, n) in enumerate(SL):
                        rs = rms_rstd([(xT[:, c, t0:t0 + n], [("x", c, si)]) for c in range(8)], ones1024[:], "ones1024", n)
                        for c in range(8):
                            norm_apply(xn[:, c, t0:t0 + n], xT[:, c, t0:t0 + n], gcol(l, 8 + c), rs, n,
                                       [("x", c, si)], [("xn", c, si)])
                    for hf in range(2):
                        for jj in range(11):
                            wt, wkey = wtile()
                            wv = wt[:].rearrange("p (k n) -> p k n", k=8)
                            for si, (t0, n) in enumerate(SL):
                                bg = pools["G"].next()
                                bu = pools["G"].next()
                                for (bank, co) in ((bg, 0), (bu, 128)):
                                    for kc in range(8):
                                        S.op("pe", lambda e, o=bank.ap[:, 0:n], w=wv[:, kc, co:co + 128], r=xn[:, kc, t0:t0 + n], kc=kc:
                                             e.matmul(o, w, r, start=(kc == 0), stop=(kc == 7)),
                                             reads=[wkey, ("xn", kc, si)], writes=[bank.key], inc=(kc == 7))
                                sg = sg_ring.next()
                                S.op("act", lambda e, o=sg.ap[:, 0:n], a=bg.ap[:, 0:n]: e.activation(out=o, in_=a, func=AF.Silu),
                                     reads=[bg.key], writes=[sg.key])
                                S.op("dve", lambda e, o=uq[:, jj, t0:t0 + n], a=bu.ap[:, 0:n], b=sg.ap[:, 0:n]:
                                     e.tensor_tensor(out=o, in0=a, in1=b, op=ALU.mult),
                                     reads=[bu.key, sg.key], writes=[("uq", jj, si)])
                        for c in range(8):
                            wt, wkey = wtile()

                            def act_rhs(kc, si):
                                t0, n = SL[si]
                                return uq[:, kc, t0:t0 + n], [("uq", kc, si)]
                            gemm_B(wt, wkey, 11, 128, (0,), act_rhs, ev_res(lambda oi, c=c: c))

                for c in range(8):
                    S.dma("sp", yT[h, c * 128:(c + 1) * 128, :], xT[:, c, :], "st_y",
                          reads=[("x", c, si) for si in range(3)], is_out=True)

        try:
            _stage(1)
            main_body()
        except _Stop:
            pass
        S.finish()
        block = es.enter_context(nc.Block())

        @block.tensor
        def _(e):
            S.replay("pe", e)

        @block.scalar
        def _(e):
            S.replay("act", e)

        @block.vector
        def _(e):
            S.replay("dve", e)

        @block.gpsimd
        def _(e):
            S.replay("pool", e)

        @block.sync
        def _(e):
            S.replay("sp", e)
    return nc


def _tile_cols(wblk):
    K = wblk.shape[0] // 128
    return np.ascontiguousarray(wblk.reshape(K, 128, wblk.shape[1]).transpose(1, 0, 2)).reshape(128, -1)


def _prep_shared(inp):
    L = DEPTH
    w_in, w_mem_kv, w_out, w_gu, w_down = (np.asarray(inp[k], np.float32) for k in
                                           ("w_in", "w_mem_kv", "w_out", "w_gate_up", "w_down"))
    WA = np.empty((L, 35, 128, 2048), np.float32)
    WD = np.empty((L, 2, 8, 128, 1408), np.float32)
    qperm = np.concatenate([np.concatenate([np.arange(64) + 64 * c, np.arange(64) + 64 * (4 + c)]) for c in range(4)])
    rperm = np.concatenate([qperm, np.arange(512, 1024)])
    for l in range(L):
        wq = w_in[l][:, qperm]
        WA[l, 0] = _tile_cols(wq[:, 0:256])
        WA[l, 1] = _tile_cols(wq[:, 256:512])
        for t, c0 in zip(range(2, 7), (512, 768, 1024, 1280, 1536)):
            WA[l, t] = _tile_cols(w_in[l][:, c0:c0 + 256])
        WA[l, 7] = _tile_cols(w_mem_kv[l][:, 0:256])
        WA[l, 8] = _tile_cols(w_mem_kv[l][:, 256:512])
        wo = w_out[l][rperm, :]
        for t in range(4):
            WA[l, 9 + t] = _tile_cols(wo[:, 256 * t:256 * (t + 1)])
        for j in range(22):
            blk = np.concatenate([w_gu[l][:, 128 * j:128 * (j + 1)], w_gu[l][:, D_FF + 128 * j:D_FF + 128 * (j + 1)]], axis=1)
            WA[l, 13 + j] = _tile_cols(blk)
        for hf in range(2):
            rows = w_down[l][1408 * hf:1408 * (hf + 1)]
            for c in range(8):
                WD[l, hf, c] = _tile_cols(rows[:, 128 * c:128 * (c + 1)])
    gt = np.zeros((128, L * GL), np.float32)

    def cols(v):
        return np.asarray(v, np.float32).reshape(8, 128).T
    for l in range(L):
        b = l * GL
        gt[:, b + 0:b + 8] = cols(inp["attn_norm_g"][l])
        gt[:, b + 8:b + 16] = cols(inp["ffn_norm_g"][l])
        gt[:, b + 16:b + 24] = cols(inp["mem_norm_g"][l])
        gt[:, b + 24:b + 32] = cols(np.asarray(inp["out_norm_g"][l])[rperm])
        gt[:, b + 32] = np.tile(np.asarray(inp["q_norm_g"][l]), 2)
        gt[:, b + 33] = np.tile(np.asarray(inp["k_norm_g"][l]), 2)
        gt[:, b + 34] = np.tile(np.asarray(inp["mq_norm_g"][l]), 2)
        gt[:, b + 35] = np.tile(np.asarray(inp["mk_norm_g"][l]), 2)
        cw = np.asarray(inp["conv_w"][l], np.float32)
        for r in range(3):
            for c in range(2):
                gt[:, b + 36 + 2 * r + c] = cw[r, 128 * c:128 * (c + 1)]
        sk = np.asarray(inp["sinks"][l], np.float32)
        for g in range(2):
            for a in range(4):
                gt[64 * g:64 * (g + 1), b + 42 + a] = sk[4 * g + a]
    cd = np.zeros((L, 128, 6, 128), np.float32)
    ar = np.arange(128)
    for l in range(L):
        cw = np.asarray(inp["conv_w"][l], np.float32)
        for r in range(3):
            for c in range(2):
                cd[l, ar, 2 * r + c, ar] = cw[r, 128 * c:128 * (c + 1)]
    return {"WA": WA, "WD": WD, "gtab": gt, "cdiag": cd.reshape(L, 128, 768)}


def _prep_core(inp, core):
    L = DEPTH
    xp = np.asarray(inp["x_prompt"][core], np.float32)
    xs = np.asarray(inp["x_sample"][4 * core:4 * core + 4], np.float32)
    xT = np.empty((2, D, TH), np.float32)
    for h in range(2):
        xT[h, :, 0:1024] = xp[1024 * h:1024 * (h + 1)].T
        for j in range(2):
            xT[h, :, 1024 + 64 * j:1088 + 64 * j] = xs[2 * h + j].T
    sl = slice(4 * core, 4 * core + 4)
    ck = np.asarray(inp["cache_win_k"][:, sl], np.float32).reshape(L, 4, 128, 128)
    cvv = np.asarray(inp["cache_win_v"][:, sl], np.float32).reshape(L, 4, 128, 128)
    cc = np.asarray(inp["cache_conv"][:, sl], np.float32)
    cv = np.ascontiguousarray(cc.reshape(L, 4, 2, 2, 128).transpose(4, 0, 3, 1, 2)).reshape(128, L * 16)
    cmk = np.asarray(inp["cache_mem_k"][:, sl], np.float32).reshape(L, 4, 256, 256)
    cmv = np.asarray(inp["cache_mem_v"][:, sl], np.float32).reshape(L, 4, 256, 256)
    return {
        "xT": xT,
        "memT": np.ascontiguousarray(np.asarray(inp["mem_prompt"][core], np.float32).T),
        "kcT": np.ascontiguousarray(ck.transpose(0, 1, 3, 2)),
        "vc": np.ascontiguousarray(cvv),
        "cv": cv,
        "mkcT": np.ascontiguousarray(cmk.transpose(0, 1, 3, 2)),
        "mvc": np.ascontiguousarray(cmv),
    }


_NC_CACHE = {}


def kernel(**inputs):
    L = DEPTH
    n_layers = int(inputs.pop("_n_layers", DEPTH))
    if n_layers not in _NC_CACHE:
        _NC_CACHE[n_layers] = build_program(n_layers)
    nc = _NC_CACHE[n_layers]
    shared = _prep_shared(inputs)
    in_maps = []
    for core in range(NCORES):
        m = dict(shared)
        m.update(_prep_core(inputs, core))
        in_maps.append(m)
    res = run_bass_kernel_spmd(nc, in_maps, core_ids=list(range(NCORES)))
    R = res.results
    yp = np.empty((8, 2048, D), np.float32)
    ys = np.empty((32, 64, D), np.float32)
    wk_p = np.empty((L, 8, 128, 2, 64), np.float32)
    wv_p = np.empty((L, 8, 128, 2, 64), np.float32)
    cv_p = np.empty((L, 8, 2, 256), np.float32)
    mk_p = np.empty((L, 8, 256, 4, 64), np.float32)
    mv_p = np.empty((L, 8, 256, 4, 64), np.float32)
    wk_s = np.empty((L, 32, 128, 2, 64), np.float32)
    wv_s = np.empty((L, 32, 128, 2, 64), np.float32)
    cv_s = np.empty((L, 32, 2, 256), np.float32)
    for core in range(NCORES):
        r = R[core]
        yT = np.asarray(r["yT"])
        for h in range(2):
            yp[core, 1024 * h:1024 * (h + 1)] = yT[h][:, 0:1024].T
            for j in range(2):
                ys[4 * core + 2 * h + j] = yT[h][:, 1024 + 64 * j:1088 + 64 * j].T
        wk_p[:, core] = np.asarray(r["o_kp"]).transpose(0, 2, 1).reshape(L, 128, 2, 64)
        wv_p[:, core] = np.asarray(r["o_vp"]).reshape(L, 128, 2, 64)
        cv_p[:, core] = np.asarray(r["o_cp"]).reshape(L, 128, 2, 2).transpose(0, 3, 2, 1).reshape(L, 2, 256)
        mk_p[:, core] = np.asarray(r["o_mk"]).reshape(L, 256, 256).transpose(0, 2, 1).reshape(L, 256, 4, 64)
        mv_p[:, core] = np.asarray(r["o_mv"]).reshape(L, 256, 4, 64)
        wk_s[:, 4 * core:4 * core + 4] = np.asarray(r["o_ks"]).transpose(0, 1, 3, 2).reshape(L, 4, 128, 2, 64)
        wv_s[:, 4 * core:4 * core + 4] = np.asarray(r["o_vs"]).reshape(L, 4, 128, 2, 64)
        cv_s[:, 4 * core:4 * core + 4] = np.asarray(r["o_cs"]).transpose(0, 3, 4, 2, 1).reshape(L, 4, 2, 256)
    return (yp, ys, wk_p, wv_p, cv_p, mk_p, mv_p, wk_s, wv_s, cv_s)
```

```python
import numpy as np
from contextlib import ExitStack
import concourse.bass as bass
import concourse.mybir as mybir
from concourse.bass_utils import run_bass_kernel_spmd

F32 = mybir.dt.float32
BF16 = mybir.dt.bfloat16
AF = mybir.ActivationFunctionType
ALU = mybir.AluOpType

DEPTH = 4
D = 1024
NCORES = 8
TH = 1152
SL = [(0, 512), (512, 512), (1024, 128)]
EPS = 1e-6
GL = 46
NSLOT = 3
PF = 2
D_FF = 2816
STOP = 0
POOL_STRICT = True
PUMP_BURST = 4
POOLS_SHARED = True


class _Stop(Exception):
    pass


def _stage(k):
    if STOP == k:
        raise _Stop()


class Ev:
    __slots__ = ("sem", "val", "eng")

    def __init__(self, sem, val, eng):
        self.sem, self.val, self.eng = sem, val, eng


class Buf:
    def __init__(self, ap, key):
        self.ap, self.key = ap, key


class Ring:
    def __init__(self, bufs):
        self.bufs, self.i = bufs, 0

    def next(self):
        b = self.bufs[self.i % len(self.bufs)]
        self.i += 1
        return b


class Sched:
    CENG = ("pe", "act", "dve", "pool")

    def __init__(self, nc, es):
        self.nc, self.es = nc, es
        self.prog = {e: [] for e in ("pe", "act", "dve", "pool", "sp")}
        self.csem = {e: es.enter_context(nc.semaphore("c_" + e)) for e in self.CENG}
        self.cnt = {e: 0 for e in self.CENG}
        self.waited = {e: {} for e in self.prog}
        self.lastw = {}
        self.readers = {}
        self.dcnt = {}
        self.dsems = {}
        self.out_sems = set()

    def dsem(self, name):
        if name not in self.dsems:
            self.dsems[name] = self.es.enter_context(self.nc.semaphore(name))
            self.dcnt[name] = 0
        return name

    def _need(self, eng, ev, need, raw):
        if ev is None:
            return
        if ev.eng == eng:
            if eng == "pe" or eng == "sp":
                return
            if not raw and (eng != "pool" or not POOL_STRICT):
                return
        if need.get(ev.sem, 0) < ev.val:
            need[ev.sem] = ev.val

    def _deps(self, eng, reads, writes):
        need = {}
        for k in reads:
            self._need(eng, self.lastw.get(k), need, True)
        for k in writes:
            self._need(eng, self.lastw.get(k), need, False)
            for ev in self.readers.get(k, ()):
                self._need(eng, ev, need, False)
        waits = []
        for sid, val in need.items():
            if self.waited[eng].get(sid, 0) < val:
                self.waited[eng][sid] = val
                waits.append((sid, val))
        return waits

    def _commit(self, ev, reads, writes):
        for k in writes:
            self.lastw[k] = ev
            self.readers[k] = []
        for k in reads:
            self.readers.setdefault(k, []).append(ev)

    def op(self, eng, fn, reads=(), writes=(), inc=True):
        waits = self._deps(eng, reads, writes)
        if inc:
            self.cnt[eng] += 1
            ev = Ev("c_" + eng, self.cnt[eng], eng)
        else:
            ev = Ev("c_" + eng, self.cnt[eng] + 1, eng)
        self._commit(ev, reads, writes)
        self.prog[eng].append((waits, fn, ("c_" + eng, 1) if inc else None))

    def dma(self, q, out, in_, sem, reads=(), writes=(), is_out=False):
        self.dsem(sem)
        waits = self._deps(q, reads, writes)
        waits = [w for w in waits if w[0] != sem]
        self.dcnt[sem] += 16
        ev = Ev(sem, self.dcnt[sem], "dma")
        self._commit(ev, reads, writes)
        self.prog[q].append((waits, lambda e, o=out, i=in_: e.dma_start(out=o, in_=i), (sem, 16)))
        if is_out:
            self.out_sems.add(sem)

    def finish(self):
        waits = [(s, self.dcnt[s]) for s in sorted(self.dcnt) if self.dcnt[s] > 0]
        self.prog["sp"].append((waits, None, None))

    def semh(self, sid):
        return self.csem[sid[2:]] if sid.startswith("c_") else self.dsems[sid]

    def replay(self, eng, e):
        for waits, fn, inc in self.prog[eng]:
            for sid, val in waits:
                e.wait_ge(self.semh(sid), val)
            if fn is None:
                continue
            ins = fn(e)
            if inc is not None:
                ins.then_inc(self.semh(inc[0]), inc[1])


def build_program(n_layers=DEPTH):
    nc = bass.Bass("TRN2", target_bir_lowering=False)
    L = DEPTH

    def din(name, shape):
        return nc.dram_tensor(name, shape, F32, kind="ExternalInput").ap()

    def dout(name, shape):
        return nc.dram_tensor(name, shape, F32, kind="ExternalOutput").ap()

    xT_in = din("xT", [2, D, TH])
    memT_in = din("memT", [D, 256])
    kcT_in = din("kcT", [L, 4, 128, 128])
    vc_in = din("vc", [L, 4, 128, 128])
    cv_in = din("cv", [128, L * 16])
    mkcT_in = din("mkcT", [L, 4, 256, 256])
    mvc_in = din("mvc", [L, 4, 256, 256])
    WA = din("WA", [L, 35, 128, 2048])
    WD = din("WD", [L, 2, 8, 128, 1408])
    gtab_in = din("gtab", [128, L * GL])
    cdiag_in = din("cdiag", [L, 128, 768])

    yT = dout("yT", [2, D, TH])
    o_kp = dout("o_kp", [L, 128, 128])
    o_vp = dout("o_vp", [L, 128, 128])
    o_cp = dout("o_cp", [L, 128, 4])
    o_mk = dout("o_mk", [L, 2, 128, 256])
    o_mv = dout("o_mv", [L, 256, 256])
    o_ks = dout("o_ks", [L, 4, 128, 128])
    o_vs = dout("o_vs", [L, 4, 128, 128])
    o_cs = dout("o_cs", [L, 128, 2, 4, 2])

    with ExitStack() as es:
        S = Sched(nc, es)

        def sb(name, shape, dt):
            return es.enter_context(nc.sbuf_tensor(name, shape, dt))

        xT = sb("xT_sb", [128, 8, TH], F32)
        xn = sb("xn", [128, 8, TH], BF16)
        uq = sb("uq", [128, 12, TH], BF16)
        kf = sb("kf", [128, TH], F32)
        kh = sb("kh", [128, 128 + TH], BF16)
        vtok = sb("vtok", [128, 10, 128], BF16)
        wsl = [sb("wsl%d" % i, [128, 2048], BF16) for i in range(NSLOT)]
        gtab = sb("gtab_sb", [128, L * GL], F32)
        esink = sb("esink", [128, L * 4], F32)
        ones1024 = sb("ones1024", [128, 128], BF16)
        ones512 = sb("ones512", [128, 128], BF16)
        ones256 = sb("ones256", [128, 128], BF16)
        bd64 = sb("bd64", [128, 128], BF16)
        ones1 = sb("ones1", [128, 64], BF16)
        sq_ring = Ring([Buf(sb("sqb%d" % i, [128, 512], BF16), ("sqb", i)) for i in range(8)])
        ln_ring = Ring([Buf(sb("lnb%d" % i, [128, 512], F32), ("lnb", i)) for i in range(5)])
        a_f = sb("a_f", [128, 4, 512], F32)
        cy_f = sb("cy_f", [128, 2, 512], F32)
        mo_f = sb("mo_f", [128, 2, 512], F32)
        pe_ring = Ring([Buf(sb("pex%d" % i, [128, 2, 514], BF16), ("pex", i)) for i in range(2)])
        ptl_ring = Ring([Buf(sb("ptl%d" % i, [128, 2, 2], F32), ("ptl", i)) for i in range(2)])
        cdg = [sb("cdg%d" % i, [128, 768], BF16) for i in range(2)]
        pT_ring = Ring([Buf(sb("pT%d" % i, [128, 2, 256], BF16), ("pT", i)) for i in range(6)])
        den_ring = Ring([Buf(sb("den%d" % i, [128, 512], F32), ("den", i)) for i in range(2)])
        rcp_ring = Ring([Buf(sb("rcp%d" % i, [128, 512], F32), ("rcp", i)) for i in range(2)])
        sg_ring = Ring([Buf(sb("sg%d" % i, [128, 512], F32), ("sg", i)) for i in range(3)])
        vst_ring = Ring([Buf(sb("vst%d" % i, [128, 128], F32), ("vst", i)) for i in range(2)])
        kcTs = [sb("kcTs%d" % i, [128, 2, 128], BF16) for i in range(2)]
        vcs = [sb("vcs%d" % i, [128, 2, 128], BF16) for i in range(2)]
        mkcTs = [sb("mkcTs%d" % i, [128, 2, 2, 256], BF16) for i in range(2)]
        mvcs = [sb("mvcs%d" % i, [128, 2, 2, 256], BF16) for i in range(2)]
        cv = sb("cv_sb", [128, L * 16], F32)
        memT = sb("memT_sb", [128, 8, 256], F32)
        mrstd = sb("mrstd", [128, 256], F32)
        memn = sb("memn", [128, 8, 256], BF16)
        mk_f = sb("mk_f", [128, 2, 256], F32)
        mv_f = sb("mv_f", [128, 2, 256], F32)
        mkh = [sb("mkh%d" % l, [128, 2, 256], BF16) for l in range(L)]
        mvb = [sb("mvb%d" % l, [128, 2, 256], BF16) for l in range(L)]
        kcar = [sb("kcar%d" % l, [128, 128], BF16) for l in range(L)]
        vcar = [sb("vcar%d" % l, [128, 128], BF16) for l in range(L)]
        pcar = [sb("pcar%d" % l, [128, 2, 2], BF16) for l in range(L)]

        banks = [Buf(es.enter_context(nc.psum_tensor("psb%d" % i, [128, 512], F32)), ("ps", i)) for i in range(8)]
        if POOLS_SHARED:
            pools = {"G": Ring(banks[0:4]), "S": Ring(banks[0:4]), "O": Ring(banks[4:6]), "O4": Ring(banks[4:8]), "N": Ring(banks[6:8])}
        else:
            pools = {"G": Ring(banks[0:4]), "S": Ring(banks[4:6]), "O": Ring(banks[6:7]), "N": Ring(banks[7:8])}

        def gcol(l, j, n=1):
            return gtab[:, l * GL + j: l * GL + j + n]

        evac_ctr = [0]

        def copy_op(out, in_, reads, writes, eng=None):
            if eng is None:
                eng = ("act", "dve")[evac_ctr[0] % 2]
                evac_ctr[0] += 1
            if eng == "act":
                S.op("act", lambda e, o=out, i=in_: e.activation(out=o, in_=i, func=AF.Copy), reads, writes)
            else:
                S.op("dve", lambda e, o=out, i=in_: e.tensor_copy(out=o, in_=i), reads, writes)

        S.op("pool", lambda e: e.memset(ones1024[:], 1.0 / 1024), writes=["ones1024"])
        S.op("pool", lambda e: e.memset(ones512[:], 1.0 / 512), writes=["ones512"])
        S.op("pool", lambda e: e.memset(ones256[:], 1.0 / 256), writes=["ones256"])
        S.op("pool", lambda e: e.memset(ones1[:], 1.0), writes=["ones1"])
        S.op("pool", lambda e: e.memset(bd64[:], 0.0), writes=["bd64"])
        S.op("pool", lambda e: e.memset(bd64[0:64, 0:64], 1.0 / 64), writes=["bd64"])
        S.op("pool", lambda e: e.memset(bd64[64:128, 64:128], 1.0 / 64), writes=["bd64"])
        S.dma("sp", gtab[:], gtab_in, "ld_init", writes=["gtab"])
        S.dma("sp", cv[:], cv_in, "ld_init", writes=["cv"])
        S.dma("sp", memT[:], memT_in.rearrange("(c p) t -> p c t", p=128), "ld_init", writes=["memT"])
        for k in ("gtab", "cv", "memT"):
            S.lastw[k] = Ev("ld_init", S.dcnt["ld_init"], "dma")
        for l in range(L):
            for s in range(4):
                S.dma("sp", o_ks[l, s, :, 0:64], kcT_in[l, s, :, 64:128], "st_d2d", is_out=True)
                S.dma("sp", o_vs[l, s, 0:64, :], vc_in[l, s, 64:128, :], "st_d2d", is_out=True)
        for l in range(L):
            S.op("act", lambda e, l=l: e.activation(out=esink[:, 4 * l:4 * l + 4], in_=gcol(l, 42, 4), func=AF.Exp),
                 reads=["gtab"], writes=["esink"])

        wseq = []
        for h in range(2):
            for l in range(n_layers):
                if h == 0:
                    wseq += [WA[l, 7], WA[l, 8]]
                wseq += [WA[l, t] for t in (0, 1, 2, 3, 4, 5, 6, 9, 10, 11, 12)]
                for hf in range(2):
                    wseq += [WA[l, 13 + 11 * hf + jj] for jj in range(11)]
                    wseq += [WD[l, hf, c] for c in range(8)]
        wstate = {"next_load": 0, "next_use": 0}

        def wtile():
            i = wstate["next_use"]
            wstate["next_use"] += 1
            while wstate["next_load"] <= min(i + PF, len(wseq) - 1):
                j = wstate["next_load"]
                src = wseq[j]
                E = src.shape[-1]
                S.dma("pool", wsl[j % NSLOT][:, 0:E], src, "ld_w%d" % (j % NSLOT), writes=[("w", j % NSLOT)])
                wstate["next_load"] += 1
            return wsl[i % NSLOT], ("w", i % NSLOT)

        def rms_rstd_batch(chains):
            assert len(chains) <= 3 and sum(len(c[0]) for c in chains) <= 8
            sqs = []
            for (srcs, ones_ap, ones_key, n) in chains:
                row = []
                for (ap, keys) in srcs:
                    sq = sq_ring.next()
                    S.op("act", lambda e, o=sq.ap[:, 0:n], a=ap: e.activation(out=o, in_=a, func=AF.Square),
                         reads=keys, writes=[sq.key])
                    row.append(sq)
                sqs.append(row)
            bks = []
            for (srcs, ones_ap, ones_key, n), row in zip(chains, sqs):
                bank = pools["N"].next()
                last = len(row) - 1
                for i, sq in enumerate(row):
                    S.op("pe", lambda e, o=bank.ap[:, 0:n], w=ones_ap, r=sq.ap[:, 0:n], i=i, last=last:
                         e.matmul(o, w, r, start=(i == 0), stop=(i == last)),
                         reads=[sq.key, ones_key], writes=[bank.key], inc=True)
                ln = ln_ring.next()
                S.op("act", lambda e, o=ln.ap[:, 0:n], a=bank.ap[:, 0:n]: e.activation(out=o, in_=a, func=AF.Ln, bias=EPS, scale=1.0),
                     reads=[bank.key], writes=[ln.key])
                bks.append(ln)
            out = []
            for (srcs, ones_ap, ones_key, n), ln in zip(chains, bks):
                S.op("act", lambda e, o=ln.ap[:, 0:n], a=ln.ap[:, 0:n]: e.activation(out=o, in_=a, func=AF.Exp, scale=-0.5),
                     reads=[ln.key], writes=[ln.key])
                out.append(ln)
            return out

        def rms_rstd(srcs, ones_ap, ones_key, n):
            return rms_rstd_batch([(srcs, ones_ap, ones_key, n)])[0]

        def norm_apply(out, in_, g_ap, rs, n, reads, writes):
            S.op("dve", lambda e, o=out, i=in_, g=g_ap, r=rs.ap[:, 0:n]:
                 e.scalar_tensor_tensor(out=o, in0=i, scalar=g, in1=r, op0=ALU.mult, op1=ALU.mult),
                 reads=list(reads) + [rs.key, "gtab"], writes=writes)

        def gemm_B(wt, wkey, KC, NCOL, ocs, rhs_fn, evac_fn):
            wv = wt[:, 0:KC * NCOL].rearrange("p (k n) -> p k n", k=KC)
            for oi in ocs:
                for si, (t0, n) in enumerate(SL):
                    bank = pools["G"].next()
                    for kc in range(KC):
                        rap, rkeys = rhs_fn(kc, si)
                        S.op("pe", lambda e, o=bank.ap[:, 0:n], w=wv[:, kc, oi * 128:(oi + 1) * 128], r=rap, kc=kc:
                             e.matmul(o, w, r, start=(kc == 0), stop=(kc == KC - 1)),
                             reads=[wkey] + rkeys, writes=[bank.key], inc=(kc == KC - 1))
                    evac_fn(oi, si, bank)

        pending = []

        def gemm_groups(get_wt, KC, NCOL, ocs, rhs_fn, evac_fn):
            st = {}

            def grp(oi, si):
                if "wt" not in st:
                    st["wt"], st["wkey"] = get_wt()
                wt, wkey = st["wt"], st["wkey"]
                wv = wt[:, 0:KC * NCOL].rearrange("p (k n) -> p k n", k=KC)
                t0, n = SL[si]
                bank = pools["G"].next()
                for kc in range(KC):
                    rap, rkeys = rhs_fn(kc, si)
                    S.op("pe", lambda e, o=bank.ap[:, 0:n], w=wv[:, kc, oi * 128:(oi + 1) * 128], r=rap, kc=kc:
                         e.matmul(o, w, r, start=(kc == 0), stop=(kc == KC - 1)),
                         reads=[wkey] + rkeys, writes=[bank.key], inc=(kc == KC - 1))
                evac_fn(oi, si, bank)
            return [(lambda oi=oi, si=si: grp(oi, si)) for oi in ocs for si in range(len(SL))]

        pump_ctr = [0]

        def pump(k=1):
            pump_ctr[0] += k
            if pump_ctr[0] < PUMP_BURST:
                return
            k2, pump_ctr[0] = pump_ctr[0], 0
            for _ in range(k2):
                if pending:
                    pending.pop(0)()

        def flush():
            pump_ctr[0] = 0
            while pending:
                pending.pop(0)()

        def xn_rhs(kc, si):
            t0, n = SL[si]
            return xn[:, kc, t0:t0 + n], [("xn", kc, si)]

        rs_m = rms_rstd([(memT[:, c, :], ["memT"]) for c in range(8)], ones1024[:], "ones1024", 256)
        copy_op(mrstd[:], rs_m.ap[:, 0:256], [rs_m.key], ["mrstd"], eng="dve")

        def main_body():
            for h in range(2):
                S.dma("sp", xT[:], xT_in[h].rearrange("(c p) t -> p c t", p=128), "ld_x",
                      writes=[("x", c, si) for c in range(8) for si in range(3)])
                for l in range(n_layers):
                    par = (h * L + l) % 2
                    ckey = ("cache", par)
                    S.dma("pool", cdg[par][:], cdiag_in[l], "ld_c%d" % par, writes=[ckey])
                    for j in range(2):
                        s = 2 * h + j
                        S.dma("pool", kcTs[par][:, j, :], kcT_in[l, s], "ld_c%d" % par, writes=[ckey])
                        S.dma("pool", vcs[par][:, j, :], vc_in[l, s], "ld_c%d" % par, writes=[ckey])
                        S.dma("pool", mkcTs[par][:, j, :, :], mkcT_in[l, s].rearrange("(c p) k -> p c k", p=128),
                              "ld_c%d" % par, writes=[ckey])
                        S.dma("pool", mvcs[par][:, j, :, :], mvc_in[l, s].rearrange("(t p) f -> p t f", p=128),
                              "ld_c%d" % par, writes=[ckey])

                    for si, (t0, n) in enumerate(SL):
                        rs = rms_rstd([(xT[:, c, t0:t0 + n], [("x", c, si)]) for c in range(8)], ones1024[:], "ones1024", n)
                        for c in range(8):
                            norm_apply(xn[:, c, t0:t0 + n], xT[:, c, t0:t0 + n], gcol(l, c), rs, n,
                                       [("x", c, si)], [("xn", c, si)])

                    _stage(2)
                    if h == 0:
                        for c in range(8):
                            S.op("dve", lambda e, c=c, l=l: e.scalar_tensor_tensor(
                                out=memn[:, c, :], in0=memT[:, c, :], scalar=gcol(l, 16 + c), in1=mrstd[:],
                                op0=ALU.mult, op1=ALU.mult), reads=["memT", "mrstd", "gtab"], writes=[("memn", c)])
                        _stage(21)
                        wt, wkey = wtile()
                        wv = wt[:].rearrange("p (k n) -> p k n", k=8)
                        for c in range(2):
                            bank = pools["G"].next()
                            for kc in range(8):
                                S.op("pe", lambda e, o=bank.ap[:, 0:256], w=wv[:, kc, c * 128:(c + 1) * 128], r=memn[:, kc, :], kc=kc:
                                     e.matmul(o, w, r, start=(kc == 0), stop=(kc == 7)),
                                     reads=[wkey, ("memn", kc)], writes=[bank.key], inc=(kc == 7))
                            copy_op(mk_f[:, c, :], bank.ap[:, 0:256], [bank.key], [("mk_f", c)])
                            rs = rms_rstd([(mk_f[:, c, :], [("mk_f", c)])], bd64[:], "bd64", 256)
                            norm_apply(mk_f[:, c, :], mk_f[:, c, :], gcol(l, 35), rs, 256, [("mk_f", c)], [("mk_f", c)])
                            copy_op(mkh[l][:, c, :], mk_f[:, c, :], [("mk_f", c)], [("mkh", l)])
                            S.dma("sp", o_mk[l, c], mk_f[:, c, :], "st_mk%d" % c, reads=[("mk_f", c)], is_out=True)
                        _stage(22)
                        wt, wkey = wtile()
                        wv = wt[:].rearrange("p (k n) -> p k n", k=8)
                        for kt in range(2):
                            bank = pools["G"].next()
                            for kc in range(8):
                                S.op("pe", lambda e, o=bank.ap[:, 0:256], w=memn[:, kc, kt * 128:(kt + 1) * 128], r=wv[:, kc, :], kc=kc:
                                     e.matmul(o, w, r, start=(kc == 0), stop=(kc == 7)),
                                     reads=[wkey, ("memn", kc)], writes=[bank.key], inc=(kc == 7))
                            copy_op(mv_f[:, kt, :], bank.ap[:, 0:256], [bank.key], [("mv_f", kt)], eng="act")
                            copy_op(mvb[l][:, kt, :], mv_f[:, kt, :], [("mv_f", kt)], [("mvb", l)], eng="dve")
                            S.dma("sp", o_mv[l, kt * 128:(kt + 1) * 128, :], mv_f[:, kt, :], "st_mv%d" % kt, reads=[("mv_f", kt)], is_out=True)

                    _stage(3)
                    if h == 1:
                        copy_op(kh[:, 0:128], kcar[l][:], [("kcar", l)], [("kh", -1)])
                        copy_op(vtok[:, 0, :], vcar[l][:], [("vcar", l)], [("v", 0)])
                    umap = {0: (0, 1), 1: (2, 3), 3: (4, 5), 4: (6, 7), 5: (8, 9), 6: (10, 11)}
                    for t in range(7):
                        if t != 2:
                            chunks = umap[t]

                            def ev_u(oi, si, bank, chunks=chunks, eng=(None if t < 2 else "dve")):
                                t0, n = SL[si]
                                copy_op(uq[:, chunks[oi], t0:t0 + n], bank.ap[:, 0:n], [bank.key], [("uq", chunks[oi], si)], eng=eng)
                            grps = gemm_groups(wtile, 8, 256, (0, 1), xn_rhs, ev_u)
                            if t < 2:
                                for gfn in grps:
                                    gfn()
                            else:
                                pending.extend(grps)
                        else:
                            wt, wkey = wtile()
                            def ev_k(oi, si, bank):
                                t0, n = SL[si]
                                copy_op(kf[:, t0:t0 + n], bank.ap[:, 0:n], [bank.key], [("kf", si)])
                            gemm_B(wt, wkey, 8, 256, (0,), xn_rhs, ev_k)
                            wv = wt[:].rearrange("p (k n) -> p k n", k=8)
                            for tt in range(9):
                                si = tt // 4
                                bank = pools["G"].next()
                                for kc in range(8):
                                    S.op("pe", lambda e, o=bank.ap[:, 0:128], w=xn[:, kc, tt * 128:(tt + 1) * 128], r=wv[:, kc, 128:256], kc=kc:
                                         e.matmul(o, w, r, start=(kc == 0), stop=(kc == 7)),
                                         reads=[wkey, ("xn", kc, si)], writes=[bank.key], inc=(kc == 7))
                                if tt == 8 or (tt == 7 and h == 1):
                                    vs = vst_ring.next()
                                    sem = "st_v%d" % vs.key[1]
                                    copy_op(vs.ap[:], bank.ap[:, 0:128], [bank.key], [vs.key], eng="act")
                                    copy_op(vtok[:, 1 + tt, :], vs.ap[:], [vs.key], [("v", 1 + tt)], eng="dve")
                                    if tt == 7:
                                        S.dma("sp", o_vp[l], vs.ap[:], sem, reads=[vs.key], is_out=True)
                                    else:
                                        for j in range(2):
                                            S.dma("sp", o_vs[l, 2 * h + j, 64:128, :], vs.ap[64 * j:64 * j + 64, :], sem,
                                                  reads=[vs.key], is_out=True)
                                else:
                                    copy_op(vtok[:, 1 + tt, :], bank.ap[:, 0:128], [bank.key], [("v", 1 + tt)])
                                if tt == 7 and h == 0:
                                    copy_op(vcar[l][:], vtok[:, 1 + tt, :], [("v", 1 + tt)], [("vcar", l)])

                    _stage(4)
                    for si, (t0, n) in enumerate(SL):
                        hn = [(c, uq[:, c, t0:t0 + n], [("uq", c, si)], gcol(l, 32)) for c in (0, 1, 2, 3)]
                        hn.append((-1, kf[:, t0:t0 + n], [("kf", si)], gcol(l, 33)))
                        for b0 in range(0, len(hn), 3):
                            grp = hn[b0:b0 + 3]
                            rss = rms_rstd_batch([([(ap, keys)], bd64[:], "bd64", n) for (_, ap, keys, _) in grp])
                            pump(2)
                            for (c, ap, keys, g_ap), rs in zip(grp, rss):
                                norm_apply(ap, ap, g_ap, rs, n, keys, keys)
                        copy_op(kh[:, 128 + t0:128 + t0 + n], kf[:, t0:t0 + n], [("kf", si)], [("kh", si)], eng="dve")
                        if si == 1 and h == 0:
                            copy_op(kcar[l][:], kf[:, 896:1024], [("kf", 1)], [("kcar", l)], eng="dve")
                        if si == 1 and h == 1:
                            S.dma("sp", o_kp[l], kf[:, 896:1024], "st_k1", reads=[("kf", 1)], is_out=True)
                        if si == 2:
                            for j in range(2):
                                S.dma("sp", o_ks[l, 2 * h + j, :, 64:128], kf[:, 1024 + 64 * j:1088 + 64 * j], "st_k2",
                                      reads=[("kf", 2)], is_out=True)

                    _stage(5)
                    pe_prev = None
                    for phase in (0, 1):
                        if phase == 1:
                            flush()
                            for si, (t0, n) in enumerate(SL):
                                hn = [(c, uq[:, c, t0:t0 + n], [("uq", c, si)], gcol(l, 34)) for c in (10, 11)]
                                rss = rms_rstd_batch([([(ap, keys)], bd64[:], "bd64", n) for (_, ap, keys, _) in hn])
                                for (c, ap, keys, g_ap), rs in zip(hn, rss):
                                    norm_apply(ap, ap, g_ap, rs, n, keys, keys)
                        for si, (t0, n) in enumerate(SL):
                            if phase == 1:
                                segs = [(t0, n, None)] if si < 2 else [(t0, 64, 0), (t0 + 64, 64, 1)]
                                for (s0, sn, j) in segs:
                                    pe = pe_ring.next()
                                    if j is not None:
                                        c0 = 16 * l
                                        src = cv[:, c0:c0 + 16].rearrange("p (c s r) -> p c s r", c=2, s=4)[:, :, 2 * h + j, :]
                                        copy_op(pe.ap[:, :, 0:2], src, ["cv"], [pe.key], eng="dve")
                                    elif si == 0 and h == 0:
                                        S.op("dve", lambda e, o=pe.ap[:, :, 0:2]: e.memset(o, 0.0), writes=[pe.key])
                                    elif si == 0:
                                        copy_op(pe.ap[:, :, 0:2], pcar[l][:], [("pcar", l)], [pe.key], eng="dve")
                                    else:
                                        copy_op(pe.ap[:, :, 0:2], pe_prev.ap[:, :, 512:514], [pe_prev.key], [pe.key], eng="dve")
                                    ckeys = [("uq", c, si) for c in (6, 7, 8, 9)]
                                    S.op("dve", lambda e, o=pe.ap[:, :, 2:2 + sn], a=uq[:, 6:8, s0:s0 + sn], b=uq[:, 8:10, s0:s0 + sn]:
                                         e.tensor_tensor(out=o, in0=a, in1=b, op=ALU.mult), reads=ckeys, writes=[pe.key])
                                    want_out = (j is not None) or (si == 1 and h == 1)
                                    if want_out:
                                        ptl = ptl_ring.next()
                                        e0 = s0 + sn - 2
                                        S.op("dve", lambda e, o=ptl.ap[:], a=uq[:, 6:8, e0:e0 + 2], b=uq[:, 8:10, e0:e0 + 2]:
                                             e.tensor_tensor(out=o, in0=a, in1=b, op=ALU.mult), reads=ckeys, writes=[ptl.key])
                                        sem = "st_ptl%d" % ptl.key[1]
                                        if j is not None:
                                            S.dma("sp", o_cs[l, :, :, 2 * h + j, :], ptl.ap[:], sem, reads=[ptl.key], is_out=True)
                                        else:
                                            S.dma("sp", o_cp[l].rearrange("p (c r) -> p c r", c=2), ptl.ap[:], sem, reads=[ptl.key], is_out=True)
                                    if si == 1 and h == 0:
                                        copy_op(pcar[l][:], pe.ap[:, :, 512:514], [pe.key], [("pcar", l)], eng="dve")
                                    off = 0 if j is None else 64 * j
                                    for c in range(2):
                                        bank = pools["S"].next()
                                        for r in range(3):
                                            S.op("pe", lambda e, o=bank.ap[:, 0:sn], w=cdg[par][:, (2 * r + c) * 128:(2 * r + c + 1) * 128],
                                                 x=pe.ap[:, c, r:r + sn], r=r:
                                                 e.matmul(o, w, x, start=(r == 0), stop=(r == 2)),
                                                 reads=[pe.key, ckey], writes=[bank.key], inc=(r == 2))
                                        S.op("dve", lambda e, o=cy_f[:, c, off:off + sn], a=bank.ap[:, 0:sn], b=uq[:, 4 + c, s0:s0 + sn]:
                                             e.tensor_tensor(out=o, in0=a, in1=b, op=ALU.mult),
                                             reads=[bank.key, ("uq", 4 + c, si)], writes=["cy_f"])
                                    pe_prev = pe

                            if phase == 0:
                                nq = n // 64
                                chunk_state = {}

                                def swa_blocks(qi):
                                    qc = t0 + 64 * qi
                                    blocks = []
                                    if si < 2:
                                        m = qc // 64
                                        mg = 16 * h + m
                                        lo = max(0, mg - 2) - 16 * h
                                        nl = lo
                                        while nl <= m:
                                            if nl % 2 == 0 and nl + 1 <= m:
                                                nk, pb = 128, 0
                                            else:
                                                nk, pb = 64, 64 * (nl % 2)
                                            col = 128 + 64 * nl
                                            kkey = ("kh", -1) if nl < 0 else ("kh", (64 * nl) // 512)
                                            tile = (nl + 2) // 2
                                            blocks.append((lambda g, col=col, nk=nk: kh[64 * g:64 * g + 64, col:col + nk], [kkey],
                                                           lambda g, tile=tile, pb=pb, nk=nk: vtok[pb:pb + nk, tile, 64 * g:64 * g + 64],
                                                           [("v", tile)], pb, nk))
                                            nl += nk // 64
                                    else:
                                        j = qi
                                        blocks.append((lambda g, j=j: kcTs[par][64 * g:64 * g + 64, j, :], [ckey],
                                                       lambda g, j=j: vcs[par][:, j, 64 * g:64 * g + 64], [ckey], 0, 128))
                                        col = 128 + 1024 + 64 * j
                                        pb = 64 * j
                                        blocks.append((lambda g, col=col: kh[64 * g:64 * g + 64, col:col + 64], [("kh", 2)],
                                                       lambda g, pb=pb: vtok[pb:pb + 64, 9, 64 * g:64 * g + 64], [("v", 9)], pb, 64))
                                    return blocks

                                def swa_A(qi):
                                    qc = t0 + 64 * qi
                                    blocks = swa_blocks(qi)
                                    qkeys = [("uq", c, si) for c in range(4)]
                                    pTs = []
                                    for g in range(2):
                                        bank = pools["S"].next()
                                        pT = pT_ring.next()
                                        pTs.append(pT)
                                        for bi, (kfn, kkeys, vfn, vkeys, pb, nk) in enumerate(blocks):
                                            S.op("pe", lambda e, o=bank.ap[pb:pb + nk, bi * 256:(bi + 1) * 256], w=kfn(g),
                                                 r=uq[64 * g:64 * g + 64, 0:4, qc:qc + 64], g=g, pb=pb:
                                                 e.matmul(o, w, r, start=True, stop=True, tile_position=(64 * g, pb)),
                                                 reads=kkeys + qkeys, writes=[bank.key], inc=True)
                                        for bi, (kfn, kkeys, vfn, vkeys, pb, nk) in enumerate(blocks):
                                            S.op("act", lambda e, o=pT.ap[pb:pb + nk, bi, :], a=bank.ap[pb:pb + nk, bi * 256:(bi + 1) * 256]:
                                                 e.activation(out=o, in_=a, func=AF.Exp, scale=0.125),
                                                 reads=[bank.key], writes=[pT.key])
                                    chunk_state[qi] = (blocks, pTs)

                                def swa_B(qi):
                                    blocks, pTs = chunk_state.pop(qi)
                                    bo = pools["O"].next()
                                    nb = len(blocks)
                                    for g in range(2):
                                        for part in range(2):
                                            for bi, (kfn, kkeys, vfn, vkeys, pb, nk) in enumerate(blocks):
                                                lhs = vfn(g) if part == 0 else ones1[pb:pb + nk, 0:64]
                                                S.op("pe", lambda e, o=bo.ap[64 * g:64 * g + 64, part * 256:(part + 1) * 256], w=lhs,
                                                     r=pTs[g].ap[pb:pb + nk, bi, :], bi=bi, pb=pb, g=g:
                                                     e.matmul(o, w, r, start=(bi == 0), stop=(bi == nb - 1), tile_position=(pb, 64 * g)),
                                                     reads=(vkeys if part == 0 else ["ones1"]) + [pTs[g].key], writes=[bo.key],
                                                     inc=(bi == nb - 1))
                                    den = den_ring.next()
                                    rcp = rcp_ring.next()
                                    S.op("dve", lambda e, o=den.ap[:, 0:256].rearrange("p (a q) -> p a q", a=4),
                                         i=bo.ap[:, 256:512].rearrange("p (a q) -> p a q", a=4),
                                         b=esink[:, 4 * l:4 * l + 4].unsqueeze(2).to_broadcast([128, 4, 64]):
                                         e.tensor_tensor(out=o, in0=i, in1=b, op=ALU.add),
                                         reads=[bo.key, "esink"], writes=[den.key])
                                    S.op("act", lambda e, o=rcp.ap[:, 0:256], i=den.ap[:, 0:256]: e.activation(out=o, in_=i, func=AF.Ln),
                                         reads=[den.key], writes=[rcp.key])
                                    S.op("act", lambda e, o=rcp.ap[:, 0:256], i=rcp.ap[:, 0:256]: e.activation(out=o, in_=i, func=AF.Exp, scale=-1.0),
                                         reads=[rcp.key], writes=[rcp.key])
                                    S.op("dve", lambda e, o=a_f[:, :, 64 * qi:64 * qi + 64],
                                         i=bo.ap[:, 0:256].rearrange("p (a q) -> p a q", a=4),
                                         r=rcp.ap[:, 0:256].rearrange("p (a q) -> p a q", a=4):
                                         e.tensor_tensor(out=o, in0=i, in1=r, op=ALU.mult),
                                         reads=[bo.key, rcp.key], writes=["a_f"])

                                ustate = {}

                                def pair_A(u):
                                    pi, g = divmod(u, 2)
                                    m = 8 * si + 2 * pi
                                    mg = 16 * h + m
                                    qc = 64 * m
                                    tiles = []
                                    if mg >= 2:
                                        nl = m - 2
                                        tiles.append((128 + 64 * nl, ("kh", -1) if nl < 0 else ("kh", (64 * nl) // 512), m // 2, "a"))
                                    tiles.append((128 + 64 * m, ("kh", si), m // 2 + 1, "b"))
                                    qkeys = [("uq", c, si) for c in range(4)]
                                    res = []
                                    for (col, kkey, vt, kind) in tiles:
                                        bank = pools["S"].next()
                                        pT = pT_ring.next()
                                        S.op("pe", lambda e, o=bank.ap[:, 0:512], w=kh[64 * g:64 * g + 64, col:col + 128],
                                             r=uq[64 * g:64 * g + 64, 0:4, qc:qc + 128], g=g:
                                             e.matmul(o, w, r, start=True, stop=True, tile_position=(64 * g, 0)),
                                             reads=[kkey] + qkeys, writes=[bank.key], inc=True)
                                        pv = pT.ap[:].rearrange("p b q -> p (b q)")
                                        S.op("act", lambda e, o=pv, a=bank.ap[:, 0:512]: e.activation(out=o, in_=a, func=AF.Exp, scale=0.125),
                                             reads=[bank.key], writes=[pT.key])
                                        p4 = pv.rearrange("p (a t) -> p a t", a=4)
                                        z = p4[0:64, :, 64:128] if kind == "a" else p4[64:128, :, 0:64]
                                        S.op("dve", lambda e, o=z: e.memset(o, 0.0), reads=[pT.key], writes=[pT.key])
                                        res.append((pv, pT.key, vt))
                                    ustate[u] = res

                                def pair_B(u):
                                    pi, g = divmod(u, 2)
                                    res = ustate.pop(u)
                                    if g == 0:
                                        ustate[("bo", pi)] = (pools["O4"].next(), pools["O4"].next())
                                    bo_pv, bo_dn = ustate[("bo", pi)]
                                    nt = len(res)
                                    for part, bo in ((0, bo_pv), (1, bo_dn)):
                                        for ti, (pv, pkey, vt) in enumerate(res):
                                            lhs = vtok[:, vt, 64 * g:64 * g + 64] if part == 0 else ones1[:, 0:64]
                                            S.op("pe", lambda e, o=bo.ap[64 * g:64 * g + 64, 0:512], w=lhs, r=pv, ti=ti, g=g:
                                                 e.matmul(o, w, r, start=(ti == 0), stop=(ti == nt - 1), tile_position=(0, 64 * g)),
                                                 reads=([("v", vt)] if part == 0 else ["ones1"]) + [pkey], writes=[bo.key],
                                                 inc=(ti == nt - 1))
                                    if g == 1:
                                        del ustate[("bo", pi)]
                                        den = den_ring.next()
                                        rcp = rcp_ring.next()
                                        S.op("dve", lambda e, o=den.ap[:].rearrange("p (a t) -> p a t", a=4),
                                             i=bo_dn.ap[:, 0:512].rearrange("p (a t) -> p a t", a=4),
                                             b=esink[:, 4 * l:4 * l + 4].unsqueeze(2).to_broadcast([128, 4, 128]):
                                             e.tensor_tensor(out=o, in0=i, in1=b, op=ALU.add),
                                             reads=[bo_dn.key, "esink"], writes=[den.key])
                                        S.op("act", lambda e, o=rcp.ap[:], i=den.ap[:]: e.activation(out=o, in_=i, func=AF.Ln),
                                             reads=[den.key], writes=[rcp.key])
                                        S.op("act", lambda e, o=rcp.ap[:]: e.activation(out=o, in_=o, func=AF.Exp, scale=-1.0),
                                             reads=[rcp.key], writes=[rcp.key])
                                        S.op("dve", lambda e, o=a_f[:, :, 128 * pi:128 * pi + 128],
                                             i=bo_pv.ap[:, 0:512].rearrange("p (a t) -> p a t", a=4),
                                             r=rcp.ap[:].rearrange("p (a t) -> p a t", a=4):
                                             e.tensor_tensor(out=o, in0=i, in1=r, op=ALU.mult),
                                             reads=[bo_pv.key, rcp.key], writes=["a_f"])

                                if si < 2:
                                    nu = 8
                                    for u in range(nu + 1):
                                        if u < nu:
                                            pair_A(u)
                                            pump(1)
                                        if u >= 1:
                                            pair_B(u - 1)
                                            pump(1)
                                else:
                                    for qi in range(nq + 1):
                                        if qi < nq:
                                            swa_A(qi)
                                            pump(1)
                                        if qi >= 1:
                                            swa_B(qi - 1)
                                            pump(1)

                                flush()
                                rs = rms_rstd([(a_f[:, c, 0:n], ["a_f"]) for c in range(4)], ones512[:], "ones512", n)
                                for c in range(4):
                                    norm_apply(xn[:, c, t0:t0 + n], a_f[:, c, 0:n], gcol(l, 24 + c), rs, n, ["a_f"], [("xn", c, si)])
                            if phase == 1:
                                _stage(7)
                                if si < 2:
                                    units = [(t0 + 256 * sub, 256, 256 * sub, None) for sub in range(2)]
                                else:
                                    units = [(t0 + 64 * j, 64, 64 * j, j) for j in range(2)]
                                mem_items = [(u, c) for u in units for c in range(2)]
                                mstate = {}

                                def mem_A(idx):
                                    (u0, un, uoff, j), c = mem_items[idx]
                                    pTs = []
                                    for hj in range(2):
                                        bank = pools["S"].next()
                                        pT = pT_ring.next()
                                        pTs.append(pT)
                                        for kt in range(2):
                                            if j is None:
                                                lhs, lkeys = mkh[l][64 * hj:64 * hj + 64, c, kt * 128:(kt + 1) * 128], [("mkh", l)]
                                            else:
                                                lhs, lkeys = mkcTs[par][64 * hj:64 * hj + 64, j, c, kt * 128:(kt + 1) * 128], [ckey]
                                            S.op("pe", lambda e, o=bank.ap[:, kt * 256:kt * 256 + un], w=lhs,
                                                 r=uq[64 * hj:64 * hj + 64, 10 + c, u0:u0 + un], hj=hj:
                                                 e.matmul(o, w, r, start=True, stop=True, tile_position=(64 * hj, 0)),
                                                 reads=lkeys + [("uq", 10 + c, si)], writes=[bank.key], inc=True)
                                        S.op("act", lambda e, o=pT.ap[:, :, 0:un], a=bank.ap[:].rearrange("p (k q) -> p k q", k=2)[:, :, 0:un]:
                                             e.activation(out=o, in_=a, func=AF.Exp, scale=0.125),
                                             reads=[bank.key], writes=[pT.key])
                                    mstate[idx] = pTs

                                def mem_B(idx):
                                    (u0, un, uoff, j), c = mem_items[idx]
                                    pTs = mstate.pop(idx)
                                    bo = pools["O"].next()
                                    for hj in range(2):
                                        hcol = (2 * c + hj) * 64
                                        for part in range(2):
                                            for kt in range(2):
                                                if part == 1:
                                                    lhs, lkeys = ones1[:, 0:64], ["ones1"]
                                                elif j is None:
                                                    lhs, lkeys = mvb[l][:, kt, hcol:hcol + 64], [("mvb", l)]
                                                else:
                                                    lhs, lkeys = mvcs[par][:, j, kt, hcol:hcol + 64], [ckey]
                                                S.op("pe", lambda e, o=bo.ap[64 * hj:64 * hj + 64, part * 256:part * 256 + un], w=lhs,
                                                     r=pTs[hj].ap[:, kt, 0:un], kt=kt, hj=hj:
                                                     e.matmul(o, w, r, start=(kt == 0), stop=(kt == 1), tile_position=(0, 64 * hj)),
                                                     reads=lkeys + [pTs[hj].key], writes=[bo.key], inc=(kt == 1))
                                    rcp = rcp_ring.next()
                                    S.op("act", lambda e, o=rcp.ap[:, 0:un], i=bo.ap[:, 256:256 + un]: e.activation(out=o, in_=i, func=AF.Ln),
                                         reads=[bo.key], writes=[rcp.key])
                                    S.op("act", lambda e, o=rcp.ap[:, 0:un], i=rcp.ap[:, 0:un]: e.activation(out=o, in_=i, func=AF.Exp, scale=-1.0),
                                         reads=[rcp.key], writes=[rcp.key])
                                    S.op("dve", lambda e, o=mo_f[:, c, uoff:uoff + un], i=bo.ap[:, 0:un], r=rcp.ap[:, 0:un]:
                                         e.tensor_tensor(out=o, in0=i, in1=r, op=ALU.mult),
                                         reads=[bo.key, rcp.key], writes=["mo_f"])

                                for idx in range(len(mem_items) + 1):
                                    if idx < len(mem_items):
                                        mem_A(idx)
                                    if idx >= 1:
                                        mem_B(idx - 1)

                                _stage(8)
                                rss = rms_rstd_batch([
                                    ([(cy_f[:, c, 0:n], ["cy_f"]) for c in range(2)], ones256[:], "ones256", n),
                                    ([(mo_f[:, c, 0:n], ["mo_f"]) for c in range(2)], ones256[:], "ones256", n)])
                                for c in range(2):
                                    norm_apply(xn[:, 4 + c, t0:t0 + n], cy_f[:, c, 0:n], gcol(l, 28 + c), rss[0], n, ["cy_f"], [("xn", 4 + c, si)])
                                for c in range(2):
                                    norm_apply(xn[:, 6 + c, t0:t0 + n], mo_f[:, c, 0:n], gcol(l, 30 + c), rss[1], n, ["mo_f"], [("xn", 6 + c, si)])

                    _stage(9)
                    def ev_res(oc):
                        def f(oi, si, bank, oc=oc):
                            t0, n = SL[si]
                            c = oc(oi)
                            S.op("dve", lambda e, o=xT[:, c, t0:t0 + n], b=bank.ap[:, 0:n]:
                                 e.tensor_tensor(out=o, in0=b, in1=o, op=ALU.add),
                                 reads=[bank.key, ("x", c, si)], writes=[("x", c, si)])
                        return f
                    for t in range(4):
                        wt, wkey = wtile()
                        gemm_B(wt, wkey, 8, 256, (0, 1), xn_rhs, ev_res(lambda oi, t=t: 2 * t + oi))

                    _stage(10)
                    for si, (t0, n) in enumerate(SL):
                        rs = rms_rstd([(xT[:, c, t0:t0 + n], [("x", c, si)]) for c in range(8)], ones1024[:], "ones1024", n)
                        for c in range(8):
                            norm_apply(xn[:, c, t0:t0 + n], xT[:, c, t0:t0 + n], gcol(l, 8 + c), rs, n,
                                       [("x", c, si)], [("xn", c, si)])
                    for hf in range(2):
                        for jj in range(11):
                            wt, wkey = wtile()
                            wv = wt[:].rearrange("p (k n) -> p k n", k=8)
                            for si, (t0, n) in enumerate(SL):
                                bg = pools["G"].next()
                                bu = pools["G"].next()
                                for (bank, co) in ((bg, 0), (bu, 128)):
                                    for kc in range(8):
                                        S.op("pe", lambda e, o=bank.ap[:, 0:n], w=wv[:, kc, co:co + 128], r=xn[:, kc, t0:t0 + n], kc=kc:
                                             e.matmul(o, w, r, start=(kc == 0), stop=(kc == 7)),
                                             reads=[wkey, ("xn", kc, si)], writes=[bank.key], inc=(kc == 7))
                                sg = sg_ring.next()
                                S.op("act", lambda e, o=sg.ap[:, 0:n], a=bg.ap[:, 0:n]: e.activation(out=o, in_=a, func=AF.Silu),
                                     reads=[bg.key], writes=[sg.key])
                                S.op("dve", lambda e, o=uq[:, jj, t0:t0 + n], a=bu.ap[:, 0:n], b=sg.ap[:, 0:n]:
                                     e.tensor_tensor(out=o, in0=a, in1=b, op=ALU.mult),
                                     reads=[bu.key, sg.key], writes=[("uq", jj, si)])
                        for c in range(8):
                            wt, wkey = wtile()

                            def act_rhs(kc, si):
                                t0, n = SL[si]
                                return uq[:, kc, t0:t0 + n], [("uq", kc, si)]
                            gemm_B(wt, wkey, 11, 128, (0,), act_rhs, ev_res(lambda oi, c=c: c))

                for c in range(8):
                    S.dma("sp", yT[h, c * 128:(c + 1) * 128, :], xT[:, c, :], "st_y",
                          reads=[("x", c, si) for si in range(3)], is_out=True)

        try:
            _stage(1)
            main_body()
        except _Stop:
            pass
        S.finish()
        block = es.enter_context(nc.Block())

        @block.tensor
        def _(e):
            S.replay("pe", e)

        @block.scalar
        def _(e):
            S.replay("act", e)

        @block.vector
        def _(e):
            S.replay("dve", e)

        @block.gpsimd
        def _(e):
            S.replay("pool", e)

        @block.sync
        def _(e):
            S.replay("sp", e)
    return nc


def _tile_cols(wblk):
    K = wblk.shape[0] // 128
    return np.ascontiguousarray(wblk.reshape(K, 128, wblk.shape[1]).transpose(1, 0, 2)).reshape(128, -1)


def _prep_shared(inp):
    L = DEPTH
    w_in, w_mem_kv, w_out, w_gu, w_down = (np.asarray(inp[k], np.float32) for k in
                                           ("w_in", "w_mem_kv", "w_out", "w_gate_up", "w_down"))
    WA = np.empty((L, 35, 128, 2048), np.float32)
    WD = np.empty((L, 2, 8, 128, 1408), np.float32)
    qperm = np.concatenate([np.concatenate([np.arange(64) + 64 * c, np.arange(64) + 64 * (4 + c)]) for c in range(4)])
    rperm = np.concatenate([qperm, np.arange(512, 1024)])
    for l in range(L):
        wq = w_in[l][:, qperm]
        WA[l, 0] = _tile_cols(wq[:, 0:256])
        WA[l, 1] = _tile_cols(wq[:, 256:512])
        for t, c0 in zip(range(2, 7), (512, 768, 1024, 1280, 1536)):
            WA[l, t] = _tile_cols(w_in[l][:, c0:c0 + 256])
        WA[l, 7] = _tile_cols(w_mem_kv[l][:, 0:256])
        WA[l, 8] = _tile_cols(w_mem_kv[l][:, 256:512])
        wo = w_out[l][rperm, :]
        for t in range(4):
            WA[l, 9 + t] = _tile_cols(wo[:, 256 * t:256 * (t + 1)])
        for j in range(22):
            blk = np.concatenate([w_gu[l][:, 128 * j:128 * (j + 1)], w_gu[l][:, D_FF + 128 * j:D_FF + 128 * (j + 1)]], axis=1)
            WA[l, 13 + j] = _tile_cols(blk)
        for hf in range(2):
            rows = w_down[l][1408 * hf:1408 * (hf + 1)]
            for c in range(8):
                WD[l, hf, c] = _tile_cols(rows[:, 128 * c:128 * (c + 1)])
    gt = np.zeros((128, L * GL), np.float32)

    def cols(v):
        return np.asarray(v, np.float32).reshape(8, 128).T
    for l in range(L):
        b = l * GL
        gt[:, b + 0:b + 8] = cols(inp["attn_norm_g"][l])
        gt[:, b + 8:b + 16] = cols(inp["ffn_norm_g"][l])
        gt[:, b + 16:b + 24] = cols(inp["mem_norm_g"][l])
        gt[:, b + 24:b + 32] = cols(np.asarray(inp["out_norm_g"][l])[rperm])
        gt[:, b + 32] = np.tile(np.asarray(inp["q_norm_g"][l]), 2)
        gt[:, b + 33] = np.tile(np.asarray(inp["k_norm_g"][l]), 2)
        gt[:, b + 34] = np.tile(np.asarray(inp["mq_norm_g"][l]), 2)
        gt[:, b + 35] = np.tile(np.asarray(inp["mk_norm_g"][l]), 2)
        cw = np.asarray(inp["conv_w"][l], np.float32)
        for r in range(3):
            for c in range(2):
                gt[:, b + 36 + 2 * r + c] = cw[r, 128 * c:128 * (c + 1)]
        sk = np.asarray(inp["sinks"][l], np.float32)
        for g in range(2):
            for a in range(4):
                gt[64 * g:64 * (g + 1), b + 42 + a] = sk[4 * g + a]
    cd = np.zeros((L, 128, 6, 128), np.float32)
    ar = np.arange(128)
    for l in range(L):
        cw = np.asarray(inp["conv_w"][l], np.float32)
        for r in range(3):
            for c in range(2):
                cd[l, ar, 2 * r + c, ar] = cw[r, 128 * c:128 * (c + 1)]
    return {"WA": WA, "WD": WD, "gtab": gt, "cdiag": cd.reshape(L, 128, 768)}


def _prep_core(inp, core):
    L = DEPTH
    xp = np.asarray(inp["x_prompt"][core], np.float32)
    xs = np.asarray(inp["x_sample"][4 * core:4 * core + 4], np.float32)
    xT = np.empty((2, D, TH), np.float32)
    for h in range(2):
        xT[h, :, 0:1024] = xp[1024 * h:1024 * (h + 1)].T
        for j in range(2):
            xT[h, :, 1024 + 64 * j:1088 + 64 * j] = xs[2 * h + j].T
    sl = slice(4 * core, 4 * core + 4)
    ck = np.asarray(inp["cache_win_k"][:, sl], np.float32).reshape(L, 4, 128, 128)
    cvv = np.asarray(inp["cache_win_v"][:, sl], np.float32).reshape(L, 4, 128, 128)
    cc = np.asarray(inp["cache_conv"][:, sl], np.float32)
    cv = np.ascontiguousarray(cc.reshape(L, 4, 2, 2, 128).transpose(4, 0, 3, 1, 2)).reshape(128, L * 16)
    cmk = np.asarray(inp["cache_mem_k"][:, sl], np.float32).reshape(L, 4, 256, 256)
    cmv = np.asarray(inp["cache_mem_v"][:, sl], np.float32).reshape(L, 4, 256, 256)
    return {
        "xT": xT,
        "memT": np.ascontiguousarray(np.asarray(inp["mem_prompt"][core], np.float32).T),
        "kcT": np.ascontiguousarray(ck.transpose(0, 1, 3, 2)),
        "vc": np.ascontiguousarray(cvv),
        "cv": cv,
        "mkcT": np.ascontiguousarray(cmk.transpose(0, 1, 3, 2)),
        "mvc": np.ascontiguousarray(cmv),
    }


_NC_CACHE = {}


def kernel(**inputs):
    L = DEPTH
    n_layers = int(inputs.pop("_n_layers", DEPTH))
    if n_layers not in _NC_CACHE:
        _NC_CACHE[n_layers] = build_program(n_layers)
    nc = _NC_CACHE[n_layers]
    shared = _prep_shared(inputs)
    in_maps = []
    for core in range(NCORES):
        m = dict(shared)
        m.update(_prep_core(inputs, core))
        in_maps.append(m)
    res = run_bass_kernel_spmd(nc, in_maps, core_ids=list(range(NCORES)))
    R = res.results
    yp = np.empty((8, 2048, D), np.float32)
    ys = np.empty((32, 64, D), np.float32)
    wk_p = np.empty((L, 8, 128, 2, 64), np.float32)
    wv_p = np.empty((L, 8, 128, 2, 64), np.float32)
    cv_p = np.empty((L, 8, 2, 256), np.float32)
    mk_p = np.empty((L, 8, 256, 4, 64), np.float32)
    mv_p = np.empty((L, 8, 256, 4, 64), np.float32)
    wk_s = np.empty((L, 32, 128, 2, 64), np.float32)
    wv_s = np.empty((L, 32, 128, 2, 64), np.float32)
    cv_s = np.empty((L, 32, 2, 256), np.float32)
    for core in range(NCORES):
        r = R[core]
        yT = np.asarray(r["yT"])
        for h in range(2):
            yp[core, 1024 * h:1024 * (h + 1)] = yT[h][:, 0:1024].T
            for j in range(2):
                ys[4 * core + 2 * h + j] = yT[h][:, 1024 + 64 * j:1088 + 64 * j].T
        wk_p[:, core] = np.asarray(r["o_kp"]).transpose(0, 2, 1).reshape(L, 128, 2, 64)
        wv_p[:, core] = np.asarray(r["o_vp"]).reshape(L, 128, 2, 64)
        cv_p[:, core] = np.asarray(r["o_cp"]).reshape(L, 128, 2, 2).transpose(0, 3, 2, 1).reshape(L, 2, 256)
        mk_p[:, core] = np.asarray(r["o_mk"]).reshape(L, 256, 256).transpose(0, 2, 1).reshape(L, 256, 4, 64)
        mv_p[:, core] = np.asarray(r["o_mv"]).reshape(L, 256, 4, 64)
        wk_s[:, 4 * core:4 * core + 4] = np.asarray(r["o_ks"]).transpose(0, 1, 3, 2).reshape(L, 4, 128, 2, 64)
        wv_s[:, 4 * core:4 * core + 4] = np.asarray(r["o_vs"]).reshape(L, 4, 128, 2, 64)
        cv_s[:, 4 * core:4 * core + 4] = np.asarray(r["o_cs"]).transpose(0, 3, 4, 2, 1).reshape(L, 4, 2, 256)
    return (yp, ys, wk_p, wv_p, cv_p, mk_p, mv_p, wk_s, wv_s, cv_s)
```

```python
import numpy as np
from contextlib import ExitStack
import concourse.bass as bass
import concourse.mybir as mybir
from concourse.bass_utils import run_bass_kernel_spmd

F32 = mybir.dt.float32
BF16 = mybir.dt.bfloat16
AF = mybir.ActivationFunctionType
ALU = mybir.AluOpType

DEPTH = 4
D = 1024
NCORES = 8
TH = 1152
SL = [(0, 512), (512, 512), (1024, 128)]
EPS = 1e-6
GL = 46
NSLOT = 3
PF = 2
D_FF = 2816
STOP = 0
POOL_STRICT = True
POOLS_SHARED = True


class _Stop(Exception):
    pass


def _stage(k):
    if STOP == k:
        raise _Stop()


class Ev:
    __slots__ = ("sem", "val", "eng")

    def __init__(self, sem, val, eng):
        self.sem, self.val, self.eng = sem, val, eng


class Buf:
    def __init__(self, ap, key):
        self.ap, self.key = ap, key


class Ring:
    def __init__(self, bufs):
        self.bufs, self.i = bufs, 0

    def next(self):
        b = self.bufs[self.i % len(self.bufs)]
        self.i += 1
        return b


class Sched:
    CENG = ("pe", "act", "dve", "pool")

    def __init__(self, nc, es):
        self.nc, self.es = nc, es
        self.prog = {e: [] for e in ("pe", "act", "dve", "pool", "sp")}
        self.csem = {e: es.enter_context(nc.semaphore("c_" + e)) for e in self.CENG}
        self.cnt = {e: 0 for e in self.CENG}
        self.waited = {e: {} for e in self.prog}
        self.lastw = {}
        self.readers = {}
        self.dcnt = {}
        self.dsems = {}
        self.out_sems = set()

    def dsem(self, name):
        if name not in self.dsems:
            self.dsems[name] = self.es.enter_context(self.nc.semaphore(name))
            self.dcnt[name] = 0
        return name

    def _need(self, eng, ev, need, raw):
        if ev is None:
            return
        if ev.eng == eng:
            if eng == "pe" or eng == "sp":
                return
            if not raw and (eng != "pool" or not POOL_STRICT):
                return
        if need.get(ev.sem, 0) < ev.val:
            need[ev.sem] = ev.val

    def _deps(self, eng, reads, writes):
        need = {}
        for k in reads:
            self._need(eng, self.lastw.get(k), need, True)
        for k in writes:
            self._need(eng, self.lastw.get(k), need, False)
            for ev in self.readers.get(k, ()):
                self._need(eng, ev, need, False)
        waits = []
        for sid, val in need.items():
            if self.waited[eng].get(sid, 0) < val:
                self.waited[eng][sid] = val
                waits.append((sid, val))
        return waits

    def _commit(self, ev, reads, writes):
        for k in writes:
            self.lastw[k] = ev
            self.readers[k] = []
        for k in reads:
            self.readers.setdefault(k, []).append(ev)

    def op(self, eng, fn, reads=(), writes=(), inc=True):
        waits = self._deps(eng, reads, writes)
        if inc:
            self.cnt[eng] += 1
            ev = Ev("c_" + eng, self.cnt[eng], eng)
        else:
            ev = Ev("c_" + eng, self.cnt[eng] + 1, eng)
        self._commit(ev, reads, writes)
        self.prog[eng].append((waits, fn, ("c_" + eng, 1) if inc else None))

    def dma(self, q, out, in_, sem, reads=(), writes=(), is_out=False):
        self.dsem(sem)
        waits = self._deps(q, reads, writes)
        waits = [w for w in waits if w[0] != sem]
        self.dcnt[sem] += 16
        ev = Ev(sem, self.dcnt[sem], "dma")
        self._commit(ev, reads, writes)
        self.prog[q].append((waits, lambda e, o=out, i=in_: e.dma_start(out=o, in_=i), (sem, 16)))
        if is_out:
            self.out_sems.add(sem)

    def finish(self):
        waits = [(s, self.dcnt[s]) for s in sorted(self.dcnt) if self.dcnt[s] > 0]
        self.prog["sp"].append((waits, None, None))

    def semh(self, sid):
        return self.csem[sid[2:]] if sid.startswith("c_") else self.dsems[sid]

    def replay(self, eng, e):
        for waits, fn, inc in self.prog[eng]:
            for sid, val in waits:
                e.wait_ge(self.semh(sid), val)
            if fn is None:
                continue
            ins = fn(e)
            if inc is not None:
                ins.then_inc(self.semh(inc[0]), inc[1])


def build_program(n_layers=DEPTH):
    nc = bass.Bass("TRN2", target_bir_lowering=False)
    L = DEPTH

    def din(name, shape):
        return nc.dram_tensor(name, shape, F32, kind="ExternalInput").ap()

    def dout(name, shape):
        return nc.dram_tensor(name, shape, F32, kind="ExternalOutput").ap()

    xT_in = din("xT", [2, D, TH])
    memT_in = din("memT", [D, 256])
    kcT_in = din("kcT", [L, 4, 128, 128])
    vc_in = din("vc", [L, 4, 128, 128])
    cv_in = din("cv", [128, L * 16])
    mkcT_in = din("mkcT", [L, 4, 256, 256])
    mvc_in = din("mvc", [L, 4, 256, 256])
    WA = din("WA", [L, 35, 128, 2048])
    WD = din("WD", [L, 2, 8, 128, 1408])
    gtab_in = din("gtab", [128, L * GL])
    cdiag_in = din("cdiag", [L, 128, 768])

    yT = dout("yT", [2, D, TH])
    o_kp = dout("o_kp", [L, 128, 128])
    o_vp = dout("o_vp", [L, 128, 128])
    o_cp = dout("o_cp", [L, 128, 4])
    o_mk = dout("o_mk", [L, 2, 128, 256])
    o_mv = dout("o_mv", [L, 256, 256])
    o_ks = dout("o_ks", [L, 4, 128, 128])
    o_vs = dout("o_vs", [L, 4, 128, 128])
    o_cs = dout("o_cs", [L, 128, 2, 4, 2])

    with ExitStack() as es:
        S = Sched(nc, es)

        def sb(name, shape, dt):
            return es.enter_context(nc.sbuf_tensor(name, shape, dt))

        xT = sb("xT_sb", [128, 8, TH], F32)
        xn = sb("xn", [128, 8, TH], BF16)
        uq = sb("uq", [128, 12, TH], BF16)
        kf = sb("kf", [128, TH], F32)
        kh = sb("kh", [128, 128 + TH], BF16)
        vtok = sb("vtok", [128, 10, 128], BF16)
        wsl = [sb("wsl%d" % i, [128, 2048], BF16) for i in range(NSLOT)]
        gtab = sb("gtab_sb", [128, L * GL], F32)
        esink = sb("esink", [128, L * 4], F32)
        ones1024 = sb("ones1024", [128, 128], BF16)
        ones512 = sb("ones512", [128, 128], BF16)
        ones256 = sb("ones256", [128, 128], BF16)
        bd64 = sb("bd64", [128, 128], BF16)
        ones1 = sb("ones1", [128, 64], BF16)
        sq_ring = Ring([Buf(sb("sqb%d" % i, [128, 512], BF16), ("sqb", i)) for i in range(8)])
        ln_ring = Ring([Buf(sb("lnb%d" % i, [128, 512], F32), ("lnb", i)) for i in range(5)])
        a_f = sb("a_f", [128, 4, 512], F32)
        cy_f = sb("cy_f", [128, 2, 512], F32)
        mo_f = sb("mo_f", [128, 2, 512], F32)
        pe_ring = Ring([Buf(sb("pex%d" % i, [128, 2, 514], BF16), ("pex", i)) for i in range(2)])
        ptl_ring = Ring([Buf(sb("ptl%d" % i, [128, 2, 2], F32), ("ptl", i)) for i in range(2)])
        cdg = [sb("cdg%d" % i, [128, 768], BF16) for i in range(2)]
        pT_ring = Ring([Buf(sb("pT%d" % i, [128, 2, 256], BF16), ("pT", i)) for i in range(6)])
        den_ring = Ring([Buf(sb("den%d" % i, [128, 512], F32), ("den", i)) for i in range(2)])
        rcp_ring = Ring([Buf(sb("rcp%d" % i, [128, 512], F32), ("rcp", i)) for i in range(2)])
        sg_ring = Ring([Buf(sb("sg%d" % i, [128, 512], F32), ("sg", i)) for i in range(3)])
        vst_ring = Ring([Buf(sb("vst%d" % i, [128, 128], F32), ("vst", i)) for i in range(2)])
        kcTs = [sb("kcTs%d" % i, [128, 2, 128], BF16) for i in range(2)]
        vcs = [sb("vcs%d" % i, [128, 2, 128], BF16) for i in range(2)]
        mkcTs = [sb("mkcTs%d" % i, [128, 2, 2, 256], BF16) for i in range(2)]
        mvcs = [sb("mvcs%d" % i, [128, 2, 2, 256], BF16) for i in range(2)]
        cv = sb("cv_sb", [128, L * 16], F32)
        memT = sb("memT_sb", [128, 8, 256], F32)
        mrstd = sb("mrstd", [128, 256], F32)
        memn = sb("memn", [128, 8, 256], BF16)
        mk_f = sb("mk_f", [128, 2, 256], F32)
        mv_f = sb("mv_f", [128, 2, 256], F32)
        mkh = [sb("mkh%d" % l, [128, 2, 256], BF16) for l in range(L)]
        mvb = [sb("mvb%d" % l, [128, 2, 256], BF16) for l in range(L)]
        kcar = [sb("kcar%d" % l, [128, 128], BF16) for l in range(L)]
        vcar = [sb("vcar%d" % l, [128, 128], BF16) for l in range(L)]
        pcar = [sb("pcar%d" % l, [128, 2, 2], BF16) for l in range(L)]

        banks = [Buf(es.enter_context(nc.psum_tensor("psb%d" % i, [128, 512], F32)), ("ps", i)) for i in range(8)]
        if POOLS_SHARED:
            pools = {"G": Ring(banks[0:4]), "S": Ring(banks[0:4]), "O": Ring(banks[4:6]), "O4": Ring(banks[4:8]), "N": Ring(banks[6:8])}
        else:
            pools = {"G": Ring(banks[0:4]), "S": Ring(banks[4:6]), "O": Ring(banks[6:7]), "N": Ring(banks[7:8])}

        def gcol(l, j, n=1):
            return gtab[:, l * GL + j: l * GL + j + n]

        evac_ctr = [0]

        def copy_op(out, in_, reads, writes, eng=None):
            if eng is None:
                eng = ("act", "dve")[evac_ctr[0] % 2]
                evac_ctr[0] += 1
            if eng == "act":
                S.op("act", lambda e, o=out, i=in_: e.activation(out=o, in_=i, func=AF.Copy), reads, writes)
            else:
                S.op("dve", lambda e, o=out, i=in_: e.tensor_copy(out=o, in_=i), reads, writes)

        S.op("pool", lambda e: e.memset(ones1024[:], 1.0 / 1024), writes=["ones1024"])
        S.op("pool", lambda e: e.memset(ones512[:], 1.0 / 512), writes=["ones512"])
        S.op("pool", lambda e: e.memset(ones256[:], 1.0 / 256), writes=["ones256"])
        S.op("pool", lambda e: e.memset(ones1[:], 1.0), writes=["ones1"])
        S.op("pool", lambda e: e.memset(bd64[:], 0.0), writes=["bd64"])
        S.op("pool", lambda e: e.memset(bd64[0:64, 0:64], 1.0 / 64), writes=["bd64"])
        S.op("pool", lambda e: e.memset(bd64[64:128, 64:128], 1.0 / 64), writes=["bd64"])
        S.dma("sp", gtab[:], gtab_in, "ld_init", writes=["gtab"])
        S.dma("sp", cv[:], cv_in, "ld_init", writes=["cv"])
        S.dma("sp", memT[:], memT_in.rearrange("(c p) t -> p c t", p=128), "ld_init", writes=["memT"])
        for k in ("gtab", "cv", "memT"):
            S.lastw[k] = Ev("ld_init", S.dcnt["ld_init"], "dma")
        for l in range(L):
            for s in range(4):
                S.dma("sp", o_ks[l, s, :, 0:64], kcT_in[l, s, :, 64:128], "st_d2d", is_out=True)
                S.dma("sp", o_vs[l, s, 0:64, :], vc_in[l, s, 64:128, :], "st_d2d", is_out=True)
        for l in range(L):
            S.op("act", lambda e, l=l: e.activation(out=esink[:, 4 * l:4 * l + 4], in_=gcol(l, 42, 4), func=AF.Exp),
                 reads=["gtab"], writes=["esink"])

        wseq = []
        for h in range(2):
            for l in range(n_layers):
                if h == 0:
                    wseq += [WA[l, 7], WA[l, 8]]
                wseq += [WA[l, t] for t in (0, 1, 2, 6, 3, 4, 5, 9, 10, 11, 12)]
                for hf in range(2):
                    wseq += [WA[l, 13 + 11 * hf + jj] for jj in range(11)]
                    wseq += [WD[l, hf, c] for c in range(8)]
        wstate = {"next_load": 0, "next_use": 0}

        def wtile():
            i = wstate["next_use"]
            wstate["next_use"] += 1
            while wstate["next_load"] <= min(i + PF, len(wseq) - 1):
                j = wstate["next_load"]
                src = wseq[j]
                E = src.shape[-1]
                S.dma("pool", wsl[j % NSLOT][:, 0:E], src, "ld_w%d" % (j % NSLOT), writes=[("w", j % NSLOT)])
                wstate["next_load"] += 1
            return wsl[i % NSLOT], ("w", i % NSLOT)

        def rms_rstd_batch(chains):
            assert len(chains) <= 3 and sum(len(c[0]) for c in chains) <= 8
            sqs = []
            for (srcs, ones_ap, ones_key, n) in chains:
                row = []
                for (ap, keys) in srcs:
                    sq = sq_ring.next()
                    S.op("act", lambda e, o=sq.ap[:, 0:n], a=ap: e.activation(out=o, in_=a, func=AF.Square),
                         reads=keys, writes=[sq.key])
                    row.append(sq)
                sqs.append(row)
            bks = []
            for (srcs, ones_ap, ones_key, n), row in zip(chains, sqs):
                bank = pools["N"].next()
                last = len(row) - 1
                for i, sq in enumerate(row):
                    S.op("pe", lambda e, o=bank.ap[:, 0:n], w=ones_ap, r=sq.ap[:, 0:n], i=i, last=last:
                         e.matmul(o, w, r, start=(i == 0), stop=(i == last)),
                         reads=[sq.key, ones_key], writes=[bank.key], inc=True)
                ln = ln_ring.next()
                S.op("act", lambda e, o=ln.ap[:, 0:n], a=bank.ap[:, 0:n]: e.activation(out=o, in_=a, func=AF.Ln, bias=EPS, scale=1.0),
                     reads=[bank.key], writes=[ln.key])
                bks.append(ln)
            out = []
            for (srcs, ones_ap, ones_key, n), ln in zip(chains, bks):
                S.op("act", lambda e, o=ln.ap[:, 0:n], a=ln.ap[:, 0:n]: e.activation(out=o, in_=a, func=AF.Exp, scale=-0.5),
                     reads=[ln.key], writes=[ln.key])
                out.append(ln)
            return out

        def rms_rstd(srcs, ones_ap, ones_key, n):
            return rms_rstd_batch([(srcs, ones_ap, ones_key, n)])[0]

        def norm_apply(out, in_, g_ap, rs, n, reads, writes):
            S.op("dve", lambda e, o=out, i=in_, g=g_ap, r=rs.ap[:, 0:n]:
                 e.scalar_tensor_tensor(out=o, in0=i, scalar=g, in1=r, op0=ALU.mult, op1=ALU.mult),
                 reads=list(reads) + [rs.key, "gtab"], writes=writes)

        def gemm_B(wt, wkey, KC, NCOL, ocs, rhs_fn, evac_fn):
            wv = wt[:, 0:KC * NCOL].rearrange("p (k n) -> p k n", k=KC)
            for oi in ocs:
                for si, (t0, n) in enumerate(SL):
                    bank = pools["G"].next()
                    for kc in range(KC):
                        rap, rkeys = rhs_fn(kc, si)
                        S.op("pe", lambda e, o=bank.ap[:, 0:n], w=wv[:, kc, oi * 128:(oi + 1) * 128], r=rap, kc=kc:
                             e.matmul(o, w, r, start=(kc == 0), stop=(kc == KC - 1)),
                             reads=[wkey] + rkeys, writes=[bank.key], inc=(kc == KC - 1))
                    evac_fn(oi, si, bank)

        pending = []

        def gemm_groups(get_wt, KC, NCOL, ocs, rhs_fn, evac_fn):
            st = {}

            def grp(oi, si):
                if "wt" not in st:
                    st["wt"], st["wkey"] = get_wt()
                wt, wkey = st["wt"], st["wkey"]
                wv = wt[:, 0:KC * NCOL].rearrange("p (k n) -> p k n", k=KC)
                t0, n = SL[si]
                bank = pools["G"].next()
                for kc in range(KC):
                    rap, rkeys = rhs_fn(kc, si)
                    S.op("pe", lambda e, o=bank.ap[:, 0:n], w=wv[:, kc, oi * 128:(oi + 1) * 128], r=rap, kc=kc:
                         e.matmul(o, w, r, start=(kc == 0), stop=(kc == KC - 1)),
                         reads=[wkey] + rkeys, writes=[bank.key], inc=(kc == KC - 1))
                evac_fn(oi, si, bank)
            return [(lambda oi=oi, si=si: grp(oi, si)) for oi in ocs for si in range(len(SL))]

        def pump(k=1):
            for _ in range(k):
                if pending:
                    pending.pop(0)()

        def xn_rhs(kc, si):
            t0, n = SL[si]
            return xn[:, kc, t0:t0 + n], [("xn", kc, si)]

        rs_m = rms_rstd([(memT[:, c, :], ["memT"]) for c in range(8)], ones1024[:], "ones1024", 256)
        copy_op(mrstd[:], rs_m.ap[:, 0:256], [rs_m.key], ["mrstd"], eng="dve")

        def main_body():
            for h in range(2):
                S.dma("sp", xT[:], xT_in[h].rearrange("(c p) t -> p c t", p=128), "ld_x",
                      writes=[("x", c, si) for c in range(8) for si in range(3)])
                for l in range(n_layers):
                    par = (h * L + l) % 2
                    ckey = ("cache", par)
                    S.dma("pool", cdg[par][:], cdiag_in[l], "ld_c%d" % par, writes=[ckey])
                    for j in range(2):
                        s = 2 * h + j
                        S.dma("pool", kcTs[par][:, j, :], kcT_in[l, s], "ld_c%d" % par, writes=[ckey])
                        S.dma("pool", vcs[par][:, j, :], vc_in[l, s], "ld_c%d" % par, writes=[ckey])
                        S.dma("pool", mkcTs[par][:, j, :, :], mkcT_in[l, s].rearrange("(c p) k -> p c k", p=128),
                              "ld_c%d" % par, writes=[ckey])
                        S.dma("pool", mvcs[par][:, j, :, :], mvc_in[l, s].rearrange("(t p) f -> p t f", p=128),
                              "ld_c%d" % par, writes=[ckey])

                    for si, (t0, n) in enumerate(SL):
                        rs = rms_rstd([(xT[:, c, t0:t0 + n], [("x", c, si)]) for c in range(8)], ones1024[:], "ones1024", n)
                        for c in range(8):
                            norm_apply(xn[:, c, t0:t0 + n], xT[:, c, t0:t0 + n], gcol(l, c), rs, n,
                                       [("x", c, si)], [("xn", c, si)])

                    _stage(2)
                    if h == 0:
                        for c in range(8):
                            S.op("dve", lambda e, c=c, l=l: e.scalar_tensor_tensor(
                                out=memn[:, c, :], in0=memT[:, c, :], scalar=gcol(l, 16 + c), in1=mrstd[:],
                                op0=ALU.mult, op1=ALU.mult), reads=["memT", "mrstd", "gtab"], writes=[("memn", c)])
                        _stage(21)
                        wt, wkey = wtile()
                        wv = wt[:].rearrange("p (k n) -> p k n", k=8)
                        for c in range(2):
                            bank = pools["G"].next()
                            for kc in range(8):
                                S.op("pe", lambda e, o=bank.ap[:, 0:256], w=wv[:, kc, c * 128:(c + 1) * 128], r=memn[:, kc, :], kc=kc:
                                     e.matmul(o, w, r, start=(kc == 0), stop=(kc == 7)),
                                     reads=[wkey, ("memn", kc)], writes=[bank.key], inc=(kc == 7))
                            copy_op(mk_f[:, c, :], bank.ap[:, 0:256], [bank.key], [("mk_f", c)])
                            rs = rms_rstd([(mk_f[:, c, :], [("mk_f", c)])], bd64[:], "bd64", 256)
                            norm_apply(mk_f[:, c, :], mk_f[:, c, :], gcol(l, 35), rs, 256, [("mk_f", c)], [("mk_f", c)])
                            copy_op(mkh[l][:, c, :], mk_f[:, c, :], [("mk_f", c)], [("mkh", l)])
                            S.dma("sp", o_mk[l, c], mk_f[:, c, :], "st_mk%d" % c, reads=[("mk_f", c)], is_out=True)
                        _stage(22)
                        wt, wkey = wtile()
                        wv = wt[:].rearrange("p (k n) -> p k n", k=8)
                        for kt in range(2):
                            bank = pools["G"].next()
                            for kc in range(8):
                                S.op("pe", lambda e, o=bank.ap[:, 0:256], w=memn[:, kc, kt * 128:(kt + 1) * 128], r=wv[:, kc, :], kc=kc:
                                     e.matmul(o, w, r, start=(kc == 0), stop=(kc == 7)),
                                     reads=[wkey, ("memn", kc)], writes=[bank.key], inc=(kc == 7))
                            copy_op(mv_f[:, kt, :], bank.ap[:, 0:256], [bank.key], [("mv_f", kt)], eng="act")
                            copy_op(mvb[l][:, kt, :], mv_f[:, kt, :], [("mv_f", kt)], [("mvb", l)], eng="dve")
                            S.dma("sp", o_mv[l, kt * 128:(kt + 1) * 128, :], mv_f[:, kt, :], "st_mv%d" % kt, reads=[("mv_f", kt)], is_out=True)

                    _stage(3)
                    if h == 1:
                        copy_op(kh[:, 0:128], kcar[l][:], [("kcar", l)], [("kh", -1)])
                        copy_op(vtok[:, 0, :], vcar[l][:], [("vcar", l)], [("v", 0)])
                    umap = {0: (0, 1), 1: (2, 3), 3: (4, 5), 4: (6, 7), 5: (8, 9), 6: (10, 11)}
                    for t in (0, 1, 2, 6, 3, 4, 5):
                        if t != 2:
                            chunks = umap[t]

                            def ev_u(oi, si, bank, chunks=chunks, eng=(None if t < 2 else "dve")):
                                t0, n = SL[si]
                                copy_op(uq[:, chunks[oi], t0:t0 + n], bank.ap[:, 0:n], [bank.key], [("uq", chunks[oi], si)], eng=eng)
                            grps = gemm_groups(wtile, 8, 256, (0, 1), xn_rhs, ev_u)
                            if t < 2:
                                for gfn in grps:
                                    gfn()
                            else:
                                pending.extend(grps)
                        else:
                            wt, wkey = wtile()
                            def ev_k(oi, si, bank):
                                t0, n = SL[si]
                                copy_op(kf[:, t0:t0 + n], bank.ap[:, 0:n], [bank.key], [("kf", si)])
                            gemm_B(wt, wkey, 8, 256, (0,), xn_rhs, ev_k)
                            wv = wt[:].rearrange("p (k n) -> p k n", k=8)
                            for tt in range(9):
                                si = tt // 4
                                bank = pools["G"].next()
                                for kc in range(8):
                                    S.op("pe", lambda e, o=bank.ap[:, 0:128], w=xn[:, kc, tt * 128:(tt + 1) * 128], r=wv[:, kc, 128:256], kc=kc:
                                         e.matmul(o, w, r, start=(kc == 0), stop=(kc == 7)),
                                         reads=[wkey, ("xn", kc, si)], writes=[bank.key], inc=(kc == 7))
                                if tt == 8 or (tt == 7 and h == 1):
                                    vs = vst_ring.next()
                                    sem = "st_v%d" % vs.key[1]
                                    copy_op(vs.ap[:], bank.ap[:, 0:128], [bank.key], [vs.key], eng="act")
                                    copy_op(vtok[:, 1 + tt, :], vs.ap[:], [vs.key], [("v", 1 + tt)], eng="dve")
                                    if tt == 7:
                                        S.dma("sp", o_vp[l], vs.ap[:], sem, reads=[vs.key], is_out=True)
                                    else:
                                        for j in range(2):
                                            S.dma("sp", o_vs[l, 2 * h + j, 64:128, :], vs.ap[64 * j:64 * j + 64, :], sem,
                                                  reads=[vs.key], is_out=True)
                                else:
                                    copy_op(vtok[:, 1 + tt, :], bank.ap[:, 0:128], [bank.key], [("v", 1 + tt)])
                                if tt == 7 and h == 0:
                                    copy_op(vcar[l][:], vtok[:, 1 + tt, :], [("v", 1 + tt)], [("vcar", l)])

                    _stage(4)
                    for si, (t0, n) in enumerate(SL):
                        hn = [(c, uq[:, c, t0:t0 + n], [("uq", c, si)], gcol(l, 32)) for c in (0, 1, 2, 3)]
                        hn.append((-1, kf[:, t0:t0 + n], [("kf", si)], gcol(l, 33)))
                        for b0 in range(0, len(hn), 3):
                            grp = hn[b0:b0 + 3]
                            rss = rms_rstd_batch([([(ap, keys)], bd64[:], "bd64", n) for (_, ap, keys, _) in grp])
                            pump(2)
                            for (c, ap, keys, g_ap), rs in zip(grp, rss):
                                norm_apply(ap, ap, g_ap, rs, n, keys, keys)
                        copy_op(kh[:, 128 + t0:128 + t0 + n], kf[:, t0:t0 + n], [("kf", si)], [("kh", si)], eng="dve")
                        if si == 1 and h == 0:
                            copy_op(kcar[l][:], kf[:, 896:1024], [("kf", 1)], [("kcar", l)], eng="dve")
                        if si == 1 and h == 1:
                            S.dma("sp", o_kp[l], kf[:, 896:1024], "st_k1", reads=[("kf", 1)], is_out=True)
                        if si == 2:
                            for j in range(2):
                                S.dma("sp", o_ks[l, 2 * h + j, :, 64:128], kf[:, 1024 + 64 * j:1088 + 64 * j], "st_k2",
                                      reads=[("kf", 2)], is_out=True)

                    assert len(pending) <= 12
                    for si, (t0, n) in enumerate(SL):
                        hn = [(c, uq[:, c, t0:t0 + n], [("uq", c, si)], gcol(l, 34)) for c in (10, 11)]
                        rss = rms_rstd_batch([([(ap, keys)], bd64[:], "bd64", n) for (_, ap, keys, _) in hn])
                        pump(1)
                        for (c, ap, keys, g_ap), rs in zip(hn, rss):
                            norm_apply(ap, ap, g_ap, rs, n, keys, keys)

                    _stage(5)
                    pe_prev = None
                    for phase in (0, 1):
                        if phase == 1:
                            pump(len(pending))
                        for si, (t0, n) in enumerate(SL):
                            if phase == 1:
                                segs = [(t0, n, None)] if si < 2 else [(t0, 64, 0), (t0 + 64, 64, 1)]
                                for (s0, sn, j) in segs:
                                    pe = pe_ring.next()
                                    if j is not None:
                                        c0 = 16 * l
                                        src = cv[:, c0:c0 + 16].rearrange("p (c s r) -> p c s r", c=2, s=4)[:, :, 2 * h + j, :]
                                        copy_op(pe.ap[:, :, 0:2], src, ["cv"], [pe.key], eng="dve")
                                    elif si == 0 and h == 0:
                                        S.op("dve", lambda e, o=pe.ap[:, :, 0:2]: e.memset(o, 0.0), writes=[pe.key])
                                    elif si == 0:
                                        copy_op(pe.ap[:, :, 0:2], pcar[l][:], [("pcar", l)], [pe.key], eng="dve")
                                    else:
                                        copy_op(pe.ap[:, :, 0:2], pe_prev.ap[:, :, 512:514], [pe_prev.key], [pe.key], eng="dve")
                                    ckeys = [("uq", c, si) for c in (6, 7, 8, 9)]
                                    S.op("dve", lambda e, o=pe.ap[:, :, 2:2 + sn], a=uq[:, 6:8, s0:s0 + sn], b=uq[:, 8:10, s0:s0 + sn]:
                                         e.tensor_tensor(out=o, in0=a, in1=b, op=ALU.mult), reads=ckeys, writes=[pe.key])
                                    want_out = (j is not None) or (si == 1 and h == 1)
                                    if want_out:
                                        ptl = ptl_ring.next()
                                        e0 = s0 + sn - 2
                                        S.op("dve", lambda e, o=ptl.ap[:], a=uq[:, 6:8, e0:e0 + 2], b=uq[:, 8:10, e0:e0 + 2]:
                                             e.tensor_tensor(out=o, in0=a, in1=b, op=ALU.mult), reads=ckeys, writes=[ptl.key])
                                        sem = "st_ptl%d" % ptl.key[1]
                                        if j is not None:
                                            S.dma("sp", o_cs[l, :, :, 2 * h + j, :], ptl.ap[:], sem, reads=[ptl.key], is_out=True)
                                        else:
                                            S.dma("sp", o_cp[l].rearrange("p (c r) -> p c r", c=2), ptl.ap[:], sem, reads=[ptl.key], is_out=True)
                                    if si == 1 and h == 0:
                                        copy_op(pcar[l][:], pe.ap[:, :, 512:514], [pe.key], [("pcar", l)], eng="dve")
                                    off = 0 if j is None else 64 * j
                                    for c in range(2):
                                        bank = pools["S"].next()
                                        for r in range(3):
                                            S.op("pe", lambda e, o=bank.ap[:, 0:sn], w=cdg[par][:, (2 * r + c) * 128:(2 * r + c + 1) * 128],
                                                 x=pe.ap[:, c, r:r + sn], r=r:
                                                 e.matmul(o, w, x, start=(r == 0), stop=(r == 2)),
                                                 reads=[pe.key, ckey], writes=[bank.key], inc=(r == 2))
                                        S.op("dve", lambda e, o=cy_f[:, c, off:off + sn], a=bank.ap[:, 0:sn], b=uq[:, 4 + c, s0:s0 + sn]:
                                             e.tensor_tensor(out=o, in0=a, in1=b, op=ALU.mult),
                                             reads=[bank.key, ("uq", 4 + c, si)], writes=["cy_f"])
                                    pe_prev = pe

                            if phase == 0:
                                nq = n // 64
                                chunk_state = {}

                                def swa_blocks(qi):
                                    qc = t0 + 64 * qi
                                    blocks = []
                                    if si < 2:
                                        m = qc // 64
                                        mg = 16 * h + m
                                        lo = max(0, mg - 2) - 16 * h
                                        nl = lo
                                        while nl <= m:
                                            if nl % 2 == 0 and nl + 1 <= m:
                                                nk, pb = 128, 0
                                            else:
                                                nk, pb = 64, 64 * (nl % 2)
                                            col = 128 + 64 * nl
                                            kkey = ("kh", -1) if nl < 0 else ("kh", (64 * nl) // 512)
                                            tile = (nl + 2) // 2
                                            blocks.append((lambda g, col=col, nk=nk: kh[64 * g:64 * g + 64, col:col + nk], [kkey],
                                                           lambda g, tile=tile, pb=pb, nk=nk: vtok[pb:pb + nk, tile, 64 * g:64 * g + 64],
                                                           [("v", tile)], pb, nk))
                                            nl += nk // 64
                                    else:
                                        j = qi
                                        blocks.append((lambda g, j=j: kcTs[par][64 * g:64 * g + 64, j, :], [ckey],
                                                       lambda g, j=j: vcs[par][:, j, 64 * g:64 * g + 64], [ckey], 0, 128))
                                        col = 128 + 1024 + 64 * j
                                        pb = 64 * j
                                        blocks.append((lambda g, col=col: kh[64 * g:64 * g + 64, col:col + 64], [("kh", 2)],
                                                       lambda g, pb=pb: vtok[pb:pb + 64, 9, 64 * g:64 * g + 64], [("v", 9)], pb, 64))
                                    return blocks

                                def swa_A(qi):
                                    qc = t0 + 64 * qi
                                    blocks = swa_blocks(qi)
                                    qkeys = [("uq", c, si) for c in range(4)]
                                    pTs = []
                                    for g in range(2):
                                        bank = pools["S"].next()
                                        pT = pT_ring.next()
                                        pTs.append(pT)
                                        for bi, (kfn, kkeys, vfn, vkeys, pb, nk) in enumerate(blocks):
                                            S.op("pe", lambda e, o=bank.ap[pb:pb + nk, bi * 256:(bi + 1) * 256], w=kfn(g),
                                                 r=uq[64 * g:64 * g + 64, 0:4, qc:qc + 64], g=g, pb=pb:
                                                 e.matmul(o, w, r, start=True, stop=True, tile_position=(64 * g, pb)),
                                                 reads=kkeys + qkeys, writes=[bank.key], inc=True)
                                        for bi, (kfn, kkeys, vfn, vkeys, pb, nk) in enumerate(blocks):
                                            S.op("act", lambda e, o=pT.ap[pb:pb + nk, bi, :], a=bank.ap[pb:pb + nk, bi * 256:(bi + 1) * 256]:
                                                 e.activation(out=o, in_=a, func=AF.Exp, scale=0.125),
                                                 reads=[bank.key], writes=[pT.key])
                                    chunk_state[qi] = (blocks, pTs)

                                def swa_B(qi):
                                    blocks, pTs = chunk_state.pop(qi)
                                    bo = pools["O"].next()
                                    nb = len(blocks)
                                    for g in range(2):
                                        for part in range(2):
                                            for bi, (kfn, kkeys, vfn, vkeys, pb, nk) in enumerate(blocks):
                                                lhs = vfn(g) if part == 0 else ones1[pb:pb + nk, 0:64]
                                                S.op("pe", lambda e, o=bo.ap[64 * g:64 * g + 64, part * 256:(part + 1) * 256], w=lhs,
                                                     r=pTs[g].ap[pb:pb + nk, bi, :], bi=bi, pb=pb, g=g:
                                                     e.matmul(o, w, r, start=(bi == 0), stop=(bi == nb - 1), tile_position=(pb, 64 * g)),
                                                     reads=(vkeys if part == 0 else ["ones1"]) + [pTs[g].key], writes=[bo.key],
                                                     inc=(bi == nb - 1))
                                    den = den_ring.next()
                                    rcp = rcp_ring.next()
                                    S.op("dve", lambda e, o=den.ap[:, 0:256].rearrange("p (a q) -> p a q", a=4),
                                         i=bo.ap[:, 256:512].rearrange("p (a q) -> p a q", a=4),
                                         b=esink[:, 4 * l:4 * l + 4].unsqueeze(2).to_broadcast([128, 4, 64]):
                                         e.tensor_tensor(out=o, in0=i, in1=b, op=ALU.add),
                                         reads=[bo.key, "esink"], writes=[den.key])
                                    S.op("act", lambda e, o=rcp.ap[:, 0:256], i=den.ap[:, 0:256]: e.activation(out=o, in_=i, func=AF.Ln),
                                         reads=[den.key], writes=[rcp.key])
                                    S.op("act", lambda e, o=rcp.ap[:, 0:256], i=rcp.ap[:, 0:256]: e.activation(out=o, in_=i, func=AF.Exp, scale=-1.0),
                                         reads=[rcp.key], writes=[rcp.key])
                                    S.op("dve", lambda e, o=a_f[:, :, 64 * qi:64 * qi + 64],
                                         i=bo.ap[:, 0:256].rearrange("p (a q) -> p a q", a=4),
                                         r=rcp.ap[:, 0:256].rearrange("p (a q) -> p a q", a=4):
                                         e.tensor_tensor(out=o, in0=i, in1=r, op=ALU.mult),
                                         reads=[bo.key, rcp.key], writes=["a_f"])

                                ustate = {}

                                def pair_A(u):
                                    pi, g = divmod(u, 2)
                                    m = 8 * si + 2 * pi
                                    mg = 16 * h + m
                                    qc = 64 * m
                                    tiles = []
                                    if mg >= 2:
                                        nl = m - 2
                                        tiles.append((128 + 64 * nl, ("kh", -1) if nl < 0 else ("kh", (64 * nl) // 512), m // 2, "a"))
                                    tiles.append((128 + 64 * m, ("kh", si), m // 2 + 1, "b"))
                                    qkeys = [("uq", c, si) for c in range(4)]
                                    res = []
                                    for (col, kkey, vt, kind) in tiles:
                                        bank = pools["S"].next()
                                        pT = pT_ring.next()
                                        S.op("pe", lambda e, o=bank.ap[:, 0:512], w=kh[64 * g:64 * g + 64, col:col + 128],
                                             r=uq[64 * g:64 * g + 64, 0:4, qc:qc + 128], g=g:
                                             e.matmul(o, w, r, start=True, stop=True, tile_position=(64 * g, 0)),
                                             reads=[kkey] + qkeys, writes=[bank.key], inc=True)
                                        pv = pT.ap[:].rearrange("p b q -> p (b q)")
                                        S.op("act", lambda e, o=pv, a=bank.ap[:, 0:512]: e.activation(out=o, in_=a, func=AF.Exp, scale=0.125),
                                             reads=[bank.key], writes=[pT.key])
                                        p4 = pv.rearrange("p (a t) -> p a t", a=4)
                                        z = p4[0:64, :, 64:128] if kind == "a" else p4[64:128, :, 0:64]
                                        S.op("dve", lambda e, o=z: e.memset(o, 0.0), reads=[pT.key], writes=[pT.key])
                                        res.append((pv, pT.key, vt))
                                    ustate[u] = res

                                def pair_B(u):
                                    pi, g = divmod(u, 2)
                                    res = ustate.pop(u)
                                    if g == 0:
                                        ustate[("bo", pi)] = (pools["O4"].next(), pools["O4"].next())
                                    bo_pv, bo_dn = ustate[("bo", pi)]
                                    nt = len(res)
                                    for part, bo in ((0, bo_pv), (1, bo_dn)):
                                        for ti, (pv, pkey, vt) in enumerate(res):
                                            lhs = vtok[:, vt, 64 * g:64 * g + 64] if part == 0 else ones1[:, 0:64]
                                            S.op("pe", lambda e, o=bo.ap[64 * g:64 * g + 64, 0:512], w=lhs, r=pv, ti=ti, g=g:
                                                 e.matmul(o, w, r, start=(ti == 0), stop=(ti == nt - 1), tile_position=(0, 64 * g)),
                                                 reads=([("v", vt)] if part == 0 else ["ones1"]) + [pkey], writes=[bo.key],
                                                 inc=(ti == nt - 1))
                                    if g == 1:
                                        del ustate[("bo", pi)]
                                        den = den_ring.next()
                                        rcp = rcp_ring.next()
                                        S.op("dve", lambda e, o=den.ap[:].rearrange("p (a t) -> p a t", a=4),
                                             i=bo_dn.ap[:, 0:512].rearrange("p (a t) -> p a t", a=4),
                                             b=esink[:, 4 * l:4 * l + 4].unsqueeze(2).to_broadcast([128, 4, 128]):
                                             e.tensor_tensor(out=o, in0=i, in1=b, op=ALU.add),
                                             reads=[bo_dn.key, "esink"], writes=[den.key])
                                        S.op("act", lambda e, o=rcp.ap[:], i=den.ap[:]: e.activation(out=o, in_=i, func=AF.Ln),
                                             reads=[den.key], writes=[rcp.key])
                                        S.op("act", lambda e, o=rcp.ap[:]: e.activation(out=o, in_=o, func=AF.Exp, scale=-1.0),
                                             reads=[rcp.key], writes=[rcp.key])
                                        S.op("dve", lambda e, o=a_f[:, :, 128 * pi:128 * pi + 128],
                                             i=bo_pv.ap[:, 0:512].rearrange("p (a t) -> p a t", a=4),
                                             r=rcp.ap[:].rearrange("p (a t) -> p a t", a=4):
                                             e.tensor_tensor(out=o, in0=i, in1=r, op=ALU.mult),
                                             reads=[bo_pv.key, rcp.key], writes=["a_f"])

                                if si < 2:
                                    nu = 8
                                    for u in range(nu + 1):
                                        if u < nu:
                                            pair_A(u)
                                            pump(1)
                                        if u >= 1:
                                            pair_B(u - 1)
                                            pump(1)
                                else:
                                    for qi in range(nq + 1):
                                        if qi < nq:
                                            swa_A(qi)
                                            pump(1)
                                        if qi >= 1:
                                            swa_B(qi - 1)
                                            pump(1)

                                pump(len(pending))
                                rs = rms_rstd([(a_f[:, c, 0:n], ["a_f"]) for c in range(4)], ones512[:], "ones512", n)
                                for c in range(4):
                                    norm_apply(xn[:, c, t0:t0 + n], a_f[:, c, 0:n], gcol(l, 24 + c), rs, n, ["a_f"], [("xn", c, si)])
                            if phase == 1:
                                _stage(7)
                                if si < 2:
                                    units = [(t0 + 256 * sub, 256, 256 * sub, None) for sub in range(2)]
                                else:
                                    units = [(t0 + 64 * j, 64, 64 * j, j) for j in range(2)]
                                mem_items = [(u, c) for u in units for c in range(2)]
                                mstate = {}

                                def mem_A(idx):
                                    (u0, un, uoff, j), c = mem_items[idx]
                                    pTs = []
                                    for hj in range(2):
                                        bank = pools["S"].next()
                                        pT = pT_ring.next()
                                        pTs.append(pT)
                                        for kt in range(2):
                                            if j is None:
                                                lhs, lkeys = mkh[l][64 * hj:64 * hj + 64, c, kt * 128:(kt + 1) * 128], [("mkh", l)]
                                            else:
                                                lhs, lkeys = mkcTs[par][64 * hj:64 * hj + 64, j, c, kt * 128:(kt + 1) * 128], [ckey]
                                            S.op("pe", lambda e, o=bank.ap[:, kt * 256:kt * 256 + un], w=lhs,
                                                 r=uq[64 * hj:64 * hj + 64, 10 + c, u0:u0 + un], hj=hj:
                                                 e.matmul(o, w, r, start=True, stop=True, tile_position=(64 * hj, 0)),
                                                 reads=lkeys + [("uq", 10 + c, si)], writes=[bank.key], inc=True)
                                        S.op("act", lambda e, o=pT.ap[:, :, 0:un], a=bank.ap[:].rearrange("p (k q) -> p k q", k=2)[:, :, 0:un]:
                                             e.activation(out=o, in_=a, func=AF.Exp, scale=0.125),
                                             reads=[bank.key], writes=[pT.key])
                                    mstate[idx] = pTs

                                def mem_B(idx):
                                    (u0, un, uoff, j), c = mem_items[idx]
                                    pTs = mstate.pop(idx)
                                    bo = pools["O"].next()
                                    for hj in range(2):
                                        hcol = (2 * c + hj) * 64
                                        for part in range(2):
                                            for kt in range(2):
                                                if part == 1:
                                                    lhs, lkeys = ones1[:, 0:64], ["ones1"]
                                                elif j is None:
                                                    lhs, lkeys = mvb[l][:, kt, hcol:hcol + 64], [("mvb", l)]
                                                else:
                                                    lhs, lkeys = mvcs[par][:, j, kt, hcol:hcol + 64], [ckey]
                                                S.op("pe", lambda e, o=bo.ap[64 * hj:64 * hj + 64, part * 256:part * 256 + un], w=lhs,
                                                     r=pTs[hj].ap[:, kt, 0:un], kt=kt, hj=hj:
                                                     e.matmul(o, w, r, start=(kt == 0), stop=(kt == 1), tile_position=(0, 64 * hj)),
                                                     reads=lkeys + [pTs[hj].key], writes=[bo.key], inc=(kt == 1))
                                    rcp = rcp_ring.next()
                                    S.op("act", lambda e, o=rcp.ap[:, 0:un], i=bo.ap[:, 256:256 + un]: e.activation(out=o, in_=i, func=AF.Ln),
                                         reads=[bo.key], writes=[rcp.key])
                                    S.op("act", lambda e, o=rcp.ap[:, 0:un], i=rcp.ap[:, 0:un]: e.activation(out=o, in_=i, func=AF.Exp, scale=-1.0),
                                         reads=[rcp.key], writes=[rcp.key])
                                    S.op("dve", lambda e, o=mo_f[:, c, uoff:uoff + un], i=bo.ap[:, 0:un], r=rcp.ap[:, 0:un]:
                                         e.tensor_tensor(out=o, in0=i, in1=r, op=ALU.mult),
                                         reads=[bo.key, rcp.key], writes=["mo_f"])

                                for idx in range(len(mem_items) + 1):
                                    if idx < len(mem_items):
                                        mem_A(idx)
                                    if idx >= 1:
                                        mem_B(idx - 1)

                                _stage(8)
                                rss = rms_rstd_batch([
                                    ([(cy_f[:, c, 0:n], ["cy_f"]) for c in range(2)], ones256[:], "ones256", n),
                                    ([(mo_f[:, c, 0:n], ["mo_f"]) for c in range(2)], ones256[:], "ones256", n)])
                                for c in range(2):
                                    norm_apply(xn[:, 4 + c, t0:t0 + n], cy_f[:, c, 0:n], gcol(l, 28 + c), rss[0], n, ["cy_f"], [("xn", 4 + c, si)])
                                for c in range(2):
                                    norm_apply(xn[:, 6 + c, t0:t0 + n], mo_f[:, c, 0:n], gcol(l, 30 + c), rss[1], n, ["mo_f"], [("xn", 6 + c, si)])

                    _stage(9)
                    def ev_res(oc):
                        def f(oi, si, bank, oc=oc):
                            t0, n = SL[si]
                            c = oc(oi)
                            S.op("dve", lambda e, o=xT[:, c, t0:t0 + n], b=bank.ap[:, 0:n]:
                                 e.tensor_tensor(out=o, in0=b, in1=o, op=ALU.add),
                                 reads=[bank.key, ("x", c, si)], writes=[("x", c, si)])
                        return f
                    for t in range(4):
                        wt, wkey = wtile()
                        gemm_B(wt, wkey, 8, 256, (0, 1), xn_rhs, ev_res(lambda oi, t=t: 2 * t + oi))

                    _stage(10)
                    for si, (t0, n) in enumerate(SL):
                        rs = rms_rstd([(xT[:, c, t0:t0 + n], [("x", c, si)]) for c in range(8)], ones1024[:], "ones1024", n)
                        for c in range(8):
                            norm_apply(xn[:, c, t0:t0 + n], xT[:, c, t0:t0 + n], gcol(l, 8 + c), rs, n,
                                       [("x", c, si)], [("xn", c, si)])
                    for hf in range(2):
                        for jj in range(11):
                            wt, wkey = wtile()
                            wv = wt[:].rearrange("p (k n) -> p k n", k=8)
                            for si, (t0, n) in enumerate(SL):
                                bg = pools["G"].next()
                                bu = pools["G"].next()
                                for (bank, co) in ((bg, 0), (bu, 128)):
                                    for kc in range(8):
                                        S.op("pe", lambda e, o=bank.ap[:, 0:n], w=wv[:, kc, co:co + 128], r=xn[:, kc, t0:t0 + n], kc=kc:
                                             e.matmul(o, w, r, start=(kc == 0), stop=(kc == 7)),
                                             reads=[wkey, ("xn", kc, si)], writes=[bank.key], inc=(kc == 7))
                                sg = sg_ring.next()
                                S.op("act", lambda e, o=sg.ap[:, 0:n], a=bg.ap[:, 0:n]: e.activation(out=o, in_=a, func=AF.Silu),
                                     reads=[bg.key], writes=[sg.key])
                                S.op("dve", lambda e, o=uq[:, jj, t0:t0 + n], a=bu.ap[:, 0:n], b=sg.ap[:, 0:n]:
                                     e.tensor_tensor(out=o, in0=a, in1=b, op=ALU.mult),
                                     reads=[bu.key, sg.key], writes=[("uq", jj, si)])
                        for c in range(8):
                            wt, wkey = wtile()

                            def act_rhs(kc, si):
                                t0, n = SL[si]
                                return uq[:, kc, t0:t0 + n], [("uq", kc, si)]
                            gemm_B(wt, wkey, 11, 128, (0,), act_rhs, ev_res(lambda oi, c=c: c))

                for c in range(8):
                    S.dma("sp", yT[h, c * 128:(c + 1) * 128, :], xT[:, c, :], "st_y",
                          reads=[("x", c, si) for si in range(3)], is_out=True)

        try:
            _stage(1)
            main_body()
        except _Stop:
            pass
        S.finish()
        block = es.enter_context(nc.Block())

        @block.tensor
        def _(e):
            S.replay("pe", e)

        @block.scalar
        def _(e):
            S.replay("act", e)

        @block.vector
        def _(e):
            S.replay("dve", e)

        @block.gpsimd
        def _(e):
            S.replay("pool", e)

        @block.sync
        def _(e):
            S.replay("sp", e)
    return nc


def _tile_cols(wblk):
    K = wblk.shape[0] // 128
    return np.ascontiguousarray(wblk.reshape(K, 128, wblk.shape[1]).transpose(1, 0, 2)).reshape(128, -1)


def _prep_shared(inp):
    L = DEPTH
    w_in, w_mem_kv, w_out, w_gu, w_down = (np.asarray(inp[k], np.float32) for k in
                                           ("w_in", "w_mem_kv", "w_out", "w_gate_up", "w_down"))
    WA = np.empty((L, 35, 128, 2048), np.float32)
    WD = np.empty((L, 2, 8, 128, 1408), np.float32)
    qperm = np.concatenate([np.concatenate([np.arange(64) + 64 * c, np.arange(64) + 64 * (4 + c)]) for c in range(4)])
    rperm = np.concatenate([qperm, np.arange(512, 1024)])
    for l in range(L):
        wq = w_in[l][:, qperm]
        WA[l, 0] = _tile_cols(wq[:, 0:256])
        WA[l, 1] = _tile_cols(wq[:, 256:512])
        for t, c0 in zip(range(2, 7), (512, 768, 1024, 1280, 1536)):
            WA[l, t] = _tile_cols(w_in[l][:, c0:c0 + 256])
        WA[l, 7] = _tile_cols(w_mem_kv[l][:, 0:256])
        WA[l, 8] = _tile_cols(w_mem_kv[l][:, 256:512])
        wo = w_out[l][rperm, :]
        for t in range(4):
            WA[l, 9 + t] = _tile_cols(wo[:, 256 * t:256 * (t + 1)])
        for j in range(22):
            blk = np.concatenate([w_gu[l][:, 128 * j:128 * (j + 1)], w_gu[l][:, D_FF + 128 * j:D_FF + 128 * (j + 1)]], axis=1)
            WA[l, 13 + j] = _tile_cols(blk)
        for hf in range(2):
            rows = w_down[l][1408 * hf:1408 * (hf + 1)]
            for c in range(8):
                WD[l, hf, c] = _tile_cols(rows[:, 128 * c:128 * (c + 1)])
    gt = np.zeros((128, L * GL), np.float32)

    def cols(v):
        return np.asarray(v, np.float32).reshape(8, 128).T
    for l in range(L):
        b = l * GL
        gt[:, b + 0:b + 8] = cols(inp["attn_norm_g"][l])
        gt[:, b + 8:b + 16] = cols(inp["ffn_norm_g"][l])
        gt[:, b + 16:b + 24] = cols(inp["mem_norm_g"][l])
        gt[:, b + 24:b + 32] = cols(np.asarray(inp["out_norm_g"][l])[rperm])
        gt[:, b + 32] = np.tile(np.asarray(inp["q_norm_g"][l]), 2)
        gt[:, b + 33] = np.tile(np.asarray(inp["k_norm_g"][l]), 2)
        gt[:, b + 34] = np.tile(np.asarray(inp["mq_norm_g"][l]), 2)
        gt[:, b + 35] = np.tile(np.asarray(inp["mk_norm_g"][l]), 2)
        cw = np.asarray(inp["conv_w"][l], np.float32)
        for r in range(3):
            for c in range(2):
                gt[:, b + 36 + 2 * r + c] = cw[r, 128 * c:128 * (c + 1)]
        sk = np.asarray(inp["sinks"][l], np.float32)
        for g in range(2):
            for a in range(4):
                gt[64 * g:64 * (g + 1), b + 42 + a] = sk[4 * g + a]
    cd = np.zeros((L, 128, 6, 128), np.float32)
    ar = np.arange(128)
    for l in range(L):
        cw = np.asarray(inp["conv_w"][l], np.float32)
        for r in range(3):
            for c in range(2):
                cd[l, ar, 2 * r + c, ar] = cw[r, 128 * c:128 * (c + 1)]
    return {"WA": WA, "WD": WD, "gtab": gt, "cdiag": cd.reshape(L, 128, 768)}


def _prep_core(inp, core):
    L = DEPTH
    xp = np.asarray(inp["x_prompt"][core], np.float32)
    xs = np.asarray(inp["x_sample"][4 * core:4 * core + 4], np.float32)
    xT = np.empty((2, D, TH), np.float32)
    for h in range(2):
        xT[h, :, 0:1024] = xp[1024 * h:1024 * (h + 1)].T
        for j in range(2):
            xT[h, :, 1024 + 64 * j:1088 + 64 * j] = xs[2 * h + j].T
    sl = slice(4 * core, 4 * core + 4)
    ck = np.asarray(inp["cache_win_k"][:, sl], np.float32).reshape(L, 4, 128, 128)
    cvv = np.asarray(inp["cache_win_v"][:, sl], np.float32).reshape(L, 4, 128, 128)
    cc = np.asarray(inp["cache_conv"][:, sl], np.float32)
    cv = np.ascontiguousarray(cc.reshape(L, 4, 2, 2, 128).transpose(4, 0, 3, 1, 2)).reshape(128, L * 16)
    cmk = np.asarray(inp["cache_mem_k"][:, sl], np.float32).reshape(L, 4, 256, 256)
    cmv = np.asarray(inp["cache_mem_v"][:, sl], np.float32).reshape(L, 4, 256, 256)
    return {
        "xT": xT,
        "memT": np.ascontiguousarray(np.asarray(inp["mem_prompt"][core], np.float32).T),
        "kcT": np.ascontiguousarray(ck.transpose(0, 1, 3, 2)),
        "vc": np.ascontiguousarray(cvv),
        "cv": cv,
        "mkcT": np.ascontiguousarray(cmk.transpose(0, 1, 3, 2)),
        "mvc": np.ascontiguousarray(cmv),
    }


_NC_CACHE = {}


def kernel(**inputs):
    L = DEPTH
    n_layers = int(inputs.pop("_n_layers", DEPTH))
    if n_layers not in _NC_CACHE:
        _NC_CACHE[n_layers] = build_program(n_layers)
    nc = _NC_CACHE[n_layers]
    shared = _prep_shared(inputs)
    in_maps = []
    for core in range(NCORES):
        m = dict(shared)
        m.update(_prep_core(inputs, core))
        in_maps.append(m)
    res = run_bass_kernel_spmd(nc, in_maps, core_ids=list(range(NCORES)))
    R = res.results
    yp = np.empty((8, 2048, D), np.float32)
    ys = np.empty((32, 64, D), np.float32)
    wk_p = np.empty((L, 8, 128, 2, 64), np.float32)
    wv_p = np.empty((L, 8, 128, 2, 64), np.float32)
    cv_p = np.empty((L, 8, 2, 256), np.float32)
    mk_p = np.empty((L, 8, 256, 4, 64), np.float32)
    mv_p = np.empty((L, 8, 256, 4, 64), np.float32)
    wk_s = np.empty((L, 32, 128, 2, 64), np.float32)
    wv_s = np.empty((L, 32, 128, 2, 64), np.float32)
    cv_s = np.empty((L, 32, 2, 256), np.float32)
    for core in range(NCORES):
        r = R[core]
        yT = np.asarray(r["yT"])
        for h in range(2):
            yp[core, 1024 * h:1024 * (h + 1)] = yT[h][:, 0:1024].T
            for j in range(2):
                ys[4 * core + 2 * h + j] = yT[h][:, 1024 + 64 * j:1088 + 64 * j].T
        wk_p[:, core] = np.asarray(r["o_kp"]).transpose(0, 2, 1).reshape(L, 128, 2, 64)
        wv_p[:, core] = np.asarray(r["o_vp"]).reshape(L, 128, 2, 64)
        cv_p[:, core] = np.asarray(r["o_cp"]).reshape(L, 128, 2, 2).transpose(0, 3, 2, 1).reshape(L, 2, 256)
        mk_p[:, core] = np.asarray(r["o_mk"]).reshape(L, 256, 256).transpose(0, 2, 1).reshape(L, 256, 4, 64)
        mv_p[:, core] = np.asarray(r["o_mv"]).reshape(L, 256, 4, 64)
        wk_s[:, 4 * core:4 * core + 4] = np.asarray(r["o_ks"]).transpose(0, 1, 3, 2).reshape(L, 4, 128, 2, 64)
        wv_s[:, 4 * core:4 * core + 4] = np.asarray(r["o_vs"]).reshape(L, 4, 128, 2, 64)
        cv_s[:, 4 * core:4 * core + 4] = np.asarray(r["o_cs"]).transpose(0, 3, 4, 2, 1).reshape(L, 4, 2, 256)
    return (yp, ys, wk_p, wv_p, cv_p, mk_p, mv_p, wk_s, wv_s, cv_s)
```

```python
import numpy as np
from contextlib import ExitStack
import concourse.bass as bass
import concourse.mybir as mybir
from concourse.bass_utils import run_bass_kernel_spmd

F32 = mybir.dt.float32
BF16 = mybir.dt.bfloat16
AF = mybir.ActivationFunctionType
ALU = mybir.AluOpType

DEPTH = 4
D = 1024
NCORES = 8
TH = 1152
SL = [(0, 512), (512, 512), (1024, 128)]
EPS = 1e-6
GL = 46
NSLOT = 3
PF = 2
D_FF = 2816
STOP = 0
POOL_STRICT = True
POOLS_SHARED = True


class _Stop(Exception):
    pass


def _stage(k):
    if STOP == k:
        raise _Stop()


class Ev:
    __slots__ = ("sem", "val", "eng")

    def __init__(self, sem, val, eng):
        self.sem, self.val, self.eng = sem, val, eng


class Buf:
    def __init__(self, ap, key):
        self.ap, self.key = ap, key


class Ring:
    def __init__(self, bufs):
        self.bufs, self.i = bufs, 0

    def next(self):
        b = self.bufs[self.i % len(self.bufs)]
        self.i += 1
        return b


class Sched:
    CENG = ("pe", "act", "dve", "pool")

    def __init__(self, nc, es):
        self.nc, self.es = nc, es
        self.prog = {e: [] for e in ("pe", "act", "dve", "pool", "sp")}
        self.csem = {e: es.enter_context(nc.semaphore("c_" + e)) for e in self.CENG}
        self.cnt = {e: 0 for e in self.CENG}
        self.waited = {e: {} for e in self.prog}
        self.lastw = {}
        self.readers = {}
        self.dcnt = {}
        self.dsems = {}
        self.out_sems = set()

    def dsem(self, name):
        if name not in self.dsems:
            self.dsems[name] = self.es.enter_context(self.nc.semaphore(name))
            self.dcnt[name] = 0
        return name

    def _need(self, eng, ev, need, raw):
        if ev is None:
            return
        if ev.eng == eng:
            if eng == "pe" or eng == "sp":
                return
            if not raw and (eng != "pool" or not POOL_STRICT):
                return
        if need.get(ev.sem, 0) < ev.val:
            need[ev.sem] = ev.val

    def _deps(self, eng, reads, writes):
        need = {}
        for k in reads:
            self._need(eng, self.lastw.get(k), need, True)
        for k in writes:
            self._need(eng, self.lastw.get(k), need, False)
            for ev in self.readers.get(k, ()):
                self._need(eng, ev, need, False)
        waits = []
        for sid, val in need.items():
            if self.waited[eng].get(sid, 0) < val:
                self.waited[eng][sid] = val
                waits.append((sid, val))
        return waits

    def _commit(self, ev, reads, writes):
        for k in writes:
            self.lastw[k] = ev
            self.readers[k] = []
        for k in reads:
            self.readers.setdefault(k, []).append(ev)

    def op(self, eng, fn, reads=(), writes=(), inc=True):
        waits = self._deps(eng, reads, writes)
        if inc:
            self.cnt[eng] += 1
            ev = Ev("c_" + eng, self.cnt[eng], eng)
        else:
            ev = Ev("c_" + eng, self.cnt[eng] + 1, eng)
        self._commit(ev, reads, writes)
        self.prog[eng].append((waits, fn, ("c_" + eng, 1) if inc else None))

    def dma(self, q, out, in_, sem, reads=(), writes=(), is_out=False):
        self.dsem(sem)
        waits = self._deps(q, reads, writes)
        waits = [w for w in waits if w[0] != sem]
        self.dcnt[sem] += 16
        ev = Ev(sem, self.dcnt[sem], "dma")
        self._commit(ev, reads, writes)
        self.prog[q].append((waits, lambda e, o=out, i=in_: e.dma_start(out=o, in_=i), (sem, 16)))
        if is_out:
            self.out_sems.add(sem)

    def finish(self):
        waits = [(s, self.dcnt[s]) for s in sorted(self.dcnt) if self.dcnt[s] > 0]
        self.prog["sp"].append((waits, None, None))

    def semh(self, sid):
        return self.csem[sid[2:]] if sid.startswith("c_") else self.dsems[sid]

    def replay(self, eng, e):
        for waits, fn, inc in self.prog[eng]:
            for sid, val in waits:
                e.wait_ge(self.semh(sid), val)
            if fn is None:
                continue
            ins = fn(e)
            if inc is not None:
                ins.then_inc(self.semh(inc[0]), inc[1])


def build_program(n_layers=DEPTH):
    nc = bass.Bass("TRN2", target_bir_lowering=False)
    L = DEPTH

    def din(name, shape):
        return nc.dram_tensor(name, shape, F32, kind="ExternalInput").ap()

    def dout(name, shape):
        return nc.dram_tensor(name, shape, F32, kind="ExternalOutput").ap()

    xT_in = din("xT", [2, D, TH])
    memT_in = din("memT", [D, 256])
    kcT_in = din("kcT", [L, 4, 128, 128])
    vc_in = din("vc", [L, 4, 128, 128])
    cv_in = din("cv", [128, L * 16])
    mkcT_in = din("mkcT", [L, 4, 256, 256])
    mvc_in = din("mvc", [L, 4, 256, 256])
    WA = din("WA", [L, 35, 128, 2048])
    WD = din("WD", [L, 2, 8, 128, 1408])
    gtab_in = din("gtab", [128, L * GL])
    cdiag_in = din("cdiag", [L, 128, 768])

    yT = dout("yT", [2, D, TH])
    o_kp = dout("o_kp", [L, 128, 128])
    o_vp = dout("o_vp", [L, 128, 128])
    o_cp = dout("o_cp", [L, 128, 4])
    o_mk = dout("o_mk", [L, 2, 128, 256])
    o_mv = dout("o_mv", [L, 256, 256])
    o_ks = dout("o_ks", [L, 4, 128, 128])
    o_vs = dout("o_vs", [L, 4, 128, 128])
    o_cs = dout("o_cs", [L, 128, 2, 4, 2])

    with ExitStack() as es:
        S = Sched(nc, es)

        def sb(name, shape, dt):
            return es.enter_context(nc.sbuf_tensor(name, shape, dt))

        xT = sb("xT_sb", [128, 8, TH], F32)
        xn = sb("xn", [128, 8, TH], BF16)
        uq = sb("uq", [128, 12, TH], BF16)
        kf = sb("kf", [128, TH], F32)
        kh = sb("kh", [128, 128 + TH], BF16)
        vtok = sb("vtok", [128, 10, 128], BF16)
        wsl = [sb("wsl%d" % i, [128, 2048], BF16) for i in range(NSLOT)]
        gtab = sb("gtab_sb", [128, L * GL], F32)
        esink = sb("esink", [128, L * 4], F32)
        ones1024 = sb("ones1024", [128, 128], BF16)
        ones512 = sb("ones512", [128, 128], BF16)
        ones256 = sb("ones256", [128, 128], BF16)
        bd64 = sb("bd64", [128, 128], BF16)
        ones1 = sb("ones1", [128, 64], BF16)
        sq_ring = Ring([Buf(sb("sqb%d" % i, [128, 512], BF16), ("sqb", i)) for i in range(8)])
        ln_ring = Ring([Buf(sb("lnb%d" % i, [128, 512], F32), ("lnb", i)) for i in range(5)])
        a_f = sb("a_f", [128, 4, 512], F32)
        cy_f = sb("cy_f", [128, 2, 512], F32)
        mo_f = sb("mo_f", [128, 2, 512], F32)
        pe_ring = Ring([Buf(sb("pex%d" % i, [128, 2, 514], BF16), ("pex", i)) for i in range(2)])
        ptl_ring = Ring([Buf(sb("ptl%d" % i, [128, 2, 2], F32), ("ptl", i)) for i in range(2)])
        cdg = [sb("cdg%d" % i, [128, 768], BF16) for i in range(2)]
        pT_ring = Ring([Buf(sb("pT%d" % i, [128, 2, 256], BF16), ("pT", i)) for i in range(6)])
        den_ring = Ring([Buf(sb("den%d" % i, [128, 512], F32), ("den", i)) for i in range(2)])
        rcp_ring = Ring([Buf(sb("rcp%d" % i, [128, 512], F32), ("rcp", i)) for i in range(2)])
        sg_ring = Ring([Buf(sb("sg%d" % i, [128, 512], F32), ("sg", i)) for i in range(3)])
        vst_ring = Ring([Buf(sb("vst%d" % i, [128, 128], F32), ("vst", i)) for i in range(2)])
        kcTs = [sb("kcTs%d" % i, [128, 2, 128], BF16) for i in range(2)]
        vcs = [sb("vcs%d" % i, [128, 2, 128], BF16) for i in range(2)]
        mkcTs = [sb("mkcTs%d" % i, [128, 2, 2, 256], BF16) for i in range(2)]
        mvcs = [sb("mvcs%d" % i, [128, 2, 2, 256], BF16) for i in range(2)]
        cv = sb("cv_sb", [128, L * 16], F32)
        memT = sb("memT_sb", [128, 8, 256], F32)
        mrstd = sb("mrstd", [128, 256], F32)
        memn = sb("memn", [128, 8, 256], BF16)
        mk_f = sb("mk_f", [128, 2, 256], F32)
        mv_f = sb("mv_f", [128, 2, 256], F32)
        mkh = [sb("mkh%d" % l, [128, 2, 256], BF16) for l in range(L)]
        mvb = [sb("mvb%d" % l, [128, 2, 256], BF16) for l in range(L)]
        kcar = [sb("kcar%d" % l, [128, 128], BF16) for l in range(L)]
        vcar = [sb("vcar%d" % l, [128, 128], BF16) for l in range(L)]
        pcar = [sb("pcar%d" % l, [128, 2, 2], BF16) for l in range(L)]

        banks = [Buf(es.enter_context(nc.psum_tensor("psb%d" % i, [128, 512], F32)), ("ps", i)) for i in range(8)]
        if POOLS_SHARED:
            pools = {"G": Ring(banks[0:4]), "S": Ring(banks[0:4]), "O": Ring(banks[4:6]), "O4": Ring(banks[4:8]), "N": Ring(banks[6:8])}
        else:
            pools = {"G": Ring(banks[0:4]), "S": Ring(banks[4:6]), "O": Ring(banks[6:7]), "N": Ring(banks[7:8])}

        def gcol(l, j, n=1):
            return gtab[:, l * GL + j: l * GL + j + n]

        evac_ctr = [0]

        def copy_op(out, in_, reads, writes, eng=None):
            if eng is None:
                eng = ("act", "dve")[evac_ctr[0] % 2]
                evac_ctr[0] += 1
            if eng == "act":
                S.op("act", lambda e, o=out, i=in_: e.activation(out=o, in_=i, func=AF.Copy), reads, writes)
            else:
                S.op("dve", lambda e, o=out, i=in_: e.tensor_copy(out=o, in_=i), reads, writes)

        S.op("pool", lambda e: e.memset(ones1024[:], 1.0 / 1024), writes=["ones1024"])
        S.op("pool", lambda e: e.memset(ones512[:], 1.0 / 512), writes=["ones512"])
        S.op("pool", lambda e: e.memset(ones256[:], 1.0 / 256), writes=["ones256"])
        S.op("pool", lambda e: e.memset(ones1[:], 1.0), writes=["ones1"])
        S.op("pool", lambda e: e.memset(bd64[:], 0.0), writes=["bd64"])
        S.op("pool", lambda e: e.memset(bd64[0:64, 0:64], 1.0 / 64), writes=["bd64"])
        S.op("pool", lambda e: e.memset(bd64[64:128, 64:128], 1.0 / 64), writes=["bd64"])
        S.dma("sp", gtab[:], gtab_in, "ld_init", writes=["gtab"])
        S.dma("sp", cv[:], cv_in, "ld_init", writes=["cv"])
        S.dma("sp", memT[:], memT_in.rearrange("(c p) t -> p c t", p=128), "ld_init", writes=["memT"])
        for k in ("gtab", "cv", "memT"):
            S.lastw[k] = Ev("ld_init", S.dcnt["ld_init"], "dma")
        for l in range(L):
            for s in range(4):
                S.dma("sp", o_ks[l, s, :, 0:64], kcT_in[l, s, :, 64:128], "st_d2d", is_out=True)
                S.dma("sp", o_vs[l, s, 0:64, :], vc_in[l, s, 64:128, :], "st_d2d", is_out=True)
        for l in range(L):
            S.op("act", lambda e, l=l: e.activation(out=esink[:, 4 * l:4 * l + 4], in_=gcol(l, 42, 4), func=AF.Exp),
                 reads=["gtab"], writes=["esink"])

        wseq = []
        for h in range(2):
            for l in range(n_layers):
                if h == 0:
                    wseq += [WA[l, 7], WA[l, 8]]
                wseq += [WA[l, t] for t in (0, 1, 2, 6, 3, 4, 5, 9, 10, 11, 12)]
                for hf in range(2):
                    wseq += [WA[l, 13 + 11 * hf + jj] for jj in range(11)]
                    wseq += [WD[l, hf, c] for c in range(8)]
        wstate = {"next_load": 0, "next_use": 0}

        def wtile():
            i = wstate["next_use"]
            wstate["next_use"] += 1
            while wstate["next_load"] <= min(i + PF, len(wseq) - 1):
                j = wstate["next_load"]
                src = wseq[j]
                E = src.shape[-1]
                S.dma("pool", wsl[j % NSLOT][:, 0:E], src, "ld_w%d" % (j % NSLOT), writes=[("w", j % NSLOT)])
                wstate["next_load"] += 1
            return wsl[i % NSLOT], ("w", i % NSLOT)

        def rms_rstd_batch(chains):
            assert len(chains) <= 3 and sum(len(c[0]) for c in chains) <= 8
            sqs = []
            for (srcs, ones_ap, ones_key, n) in chains:
                row = []
                for (ap, keys) in srcs:
                    sq = sq_ring.next()
                    S.op("act", lambda e, o=sq.ap[:, 0:n], a=ap: e.activation(out=o, in_=a, func=AF.Square),
                         reads=keys, writes=[sq.key])
                    row.append(sq)
                sqs.append(row)
            bks = []
            for (srcs, ones_ap, ones_key, n), row in zip(chains, sqs):
                bank = pools["N"].next()
                last = len(row) - 1
                for i, sq in enumerate(row):
                    S.op("pe", lambda e, o=bank.ap[:, 0:n], w=ones_ap, r=sq.ap[:, 0:n], i=i, last=last:
                         e.matmul(o, w, r, start=(i == 0), stop=(i == last)),
                         reads=[sq.key, ones_key], writes=[bank.key], inc=True)
                ln = ln_ring.next()
                S.op("act", lambda e, o=ln.ap[:, 0:n], a=bank.ap[:, 0:n]: e.activation(out=o, in_=a, func=AF.Ln, bias=EPS, scale=1.0),
                     reads=[bank.key], writes=[ln.key])
                bks.append(ln)
            out = []
            for (srcs, ones_ap, ones_key, n), ln in zip(chains, bks):
                S.op("act", lambda e, o=ln.ap[:, 0:n], a=ln.ap[:, 0:n]: e.activation(out=o, in_=a, func=AF.Exp, scale=-0.5),
                     reads=[ln.key], writes=[ln.key])
                out.append(ln)
            return out

        def rms_rstd(srcs, ones_ap, ones_key, n):
            return rms_rstd_batch([(srcs, ones_ap, ones_key, n)])[0]

        def norm_apply(out, in_, g_ap, rs, n, reads, writes):
            S.op("dve", lambda e, o=out, i=in_, g=g_ap, r=rs.ap[:, 0:n]:
                 e.scalar_tensor_tensor(out=o, in0=i, scalar=g, in1=r, op0=ALU.mult, op1=ALU.mult),
                 reads=list(reads) + [rs.key, "gtab"], writes=writes)

        def gemm_B(wt, wkey, KC, NCOL, ocs, rhs_fn, evac_fn):
            wv = wt[:, 0:KC * NCOL].rearrange("p (k n) -> p k n", k=KC)
            for oi in ocs:
                for si, (t0, n) in enumerate(SL):
                    bank = pools["G"].next()
                    for kc in range(KC):
                        rap, rkeys = rhs_fn(kc, si)
                        S.op("pe", lambda e, o=bank.ap[:, 0:n], w=wv[:, kc, oi * 128:(oi + 1) * 128], r=rap, kc=kc:
                             e.matmul(o, w, r, start=(kc == 0), stop=(kc == KC - 1)),
                             reads=[wkey] + rkeys, writes=[bank.key], inc=(kc == KC - 1))
                    evac_fn(oi, si, bank)

        pending = []

        def gemm_groups(get_wt, KC, NCOL, ocs, rhs_fn, evac_fn):
            st = {}

            def grp(oi, si):
                if "wt" not in st:
                    st["wt"], st["wkey"] = get_wt()
                wt, wkey = st["wt"], st["wkey"]
                wv = wt[:, 0:KC * NCOL].rearrange("p (k n) -> p k n", k=KC)
                t0, n = SL[si]
                bank = pools["G"].next()
                for kc in range(KC):
                    rap, rkeys = rhs_fn(kc, si)
                    S.op("pe", lambda e, o=bank.ap[:, 0:n], w=wv[:, kc, oi * 128:(oi + 1) * 128], r=rap, kc=kc:
                         e.matmul(o, w, r, start=(kc == 0), stop=(kc == KC - 1)),
                         reads=[wkey] + rkeys, writes=[bank.key], inc=(kc == KC - 1))
                evac_fn(oi, si, bank)
            return [(lambda oi=oi, si=si: grp(oi, si)) for oi in ocs for si in range(len(SL))]

        def pump(k=1):
            for _ in range(k):
                if pending:
                    pending.pop(0)()

        def xn_rhs(kc, si):
            t0, n = SL[si]
            return xn[:, kc, t0:t0 + n], [("xn", kc, si)]

        rs_m = rms_rstd([(memT[:, c, :], ["memT"]) for c in range(8)], ones1024[:], "ones1024", 256)
        copy_op(mrstd[:], rs_m.ap[:, 0:256], [rs_m.key], ["mrstd"], eng="dve")

        def main_body():
            for h in range(2):
                for c in range(8):
                    S.dma("sp", xT[:, c, :], xT_in[h, c * 128:(c + 1) * 128, :], "ld_x%d" % c,
                          writes=[("x", c, si) for si in range(3)])
                for l in range(n_layers):
                    par = (h * L + l) % 2
                    ckey = ("cache", par)
                    S.dma("pool", cdg[par][:], cdiag_in[l], "ld_c%d" % par, writes=[ckey])
                    for j in range(2):
                        s = 2 * h + j
                        S.dma("pool", kcTs[par][:, j, :], kcT_in[l, s], "ld_c%d" % par, writes=[ckey])
                        S.dma("pool", vcs[par][:, j, :], vc_in[l, s], "ld_c%d" % par, writes=[ckey])
                        S.dma("pool", mkcTs[par][:, j, :, :], mkcT_in[l, s].rearrange("(c p) k -> p c k", p=128),
                              "ld_c%d" % par, writes=[ckey])
                        S.dma("pool", mvcs[par][:, j, :, :], mvc_in[l, s].rearrange("(t p) f -> p t f", p=128),
                              "ld_c%d" % par, writes=[ckey])

                    for si, (t0, n) in enumerate(SL):
                        rs = rms_rstd([(xT[:, c, t0:t0 + n], [("x", c, si)]) for c in range(8)], ones1024[:], "ones1024", n)
                        for c in range(8):
                            norm_apply(xn[:, c, t0:t0 + n], xT[:, c, t0:t0 + n], gcol(l, c), rs, n,
                                       [("x", c, si)], [("xn", c, si)])

                    _stage(2)
                    if h == 0:
                        for c in range(8):
                            S.op("dve", lambda e, c=c, l=l: e.scalar_tensor_tensor(
                                out=memn[:, c, :], in0=memT[:, c, :], scalar=gcol(l, 16 + c), in1=mrstd[:],
                                op0=ALU.mult, op1=ALU.mult), reads=["memT", "mrstd", "gtab"], writes=[("memn", c)])
                        _stage(21)
                        wt, wkey = wtile()
                        wv = wt[:].rearrange("p (k n) -> p k n", k=8)
                        for c in range(2):
                            bank = pools["G"].next()
                            for kc in range(8):
                                S.op("pe", lambda e, o=bank.ap[:, 0:256], w=wv[:, kc, c * 128:(c + 1) * 128], r=memn[:, kc, :], kc=kc:
                                     e.matmul(o, w, r, start=(kc == 0), stop=(kc == 7)),
                                     reads=[wkey, ("memn", kc)], writes=[bank.key], inc=(kc == 7))
                            copy_op(mk_f[:, c, :], bank.ap[:, 0:256], [bank.key], [("mk_f", c)])
                            rs = rms_rstd([(mk_f[:, c, :], [("mk_f", c)])], bd64[:], "bd64", 256)
                            norm_apply(mk_f[:, c, :], mk_f[:, c, :], gcol(l, 35), rs, 256, [("mk_f", c)], [("mk_f", c)])
                            copy_op(mkh[l][:, c, :], mk_f[:, c, :], [("mk_f", c)], [("mkh", l)])
                            S.dma("sp", o_mk[l, c], mk_f[:, c, :], "st_mk%d" % c, reads=[("mk_f", c)], is_out=True)
                        _stage(22)
                        wt, wkey = wtile()
                        wv = wt[:].rearrange("p (k n) -> p k n", k=8)
                        for kt in range(2):
                            bank = pools["G"].next()
                            for kc in range(8):
                                S.op("pe", lambda e, o=bank.ap[:, 0:256], w=memn[:, kc, kt * 128:(kt + 1) * 128], r=wv[:, kc, :], kc=kc:
                                     e.matmul(o, w, r, start=(kc == 0), stop=(kc == 7)),
                                     reads=[wkey, ("memn", kc)], writes=[bank.key], inc=(kc == 7))
                            copy_op(mv_f[:, kt, :], bank.ap[:, 0:256], [bank.key], [("mv_f", kt)], eng="act")
                            copy_op(mvb[l][:, kt, :], mv_f[:, kt, :], [("mv_f", kt)], [("mvb", l)], eng="dve")
                            S.dma("sp", o_mv[l, kt * 128:(kt + 1) * 128, :], mv_f[:, kt, :], "st_mv%d" % kt, reads=[("mv_f", kt)], is_out=True)

                    _stage(3)
                    if h == 1:
                        copy_op(kh[:, 0:128], kcar[l][:], [("kcar", l)], [("kh", -1)])
                        copy_op(vtok[:, 0, :], vcar[l][:], [("vcar", l)], [("v", 0)])
                    umap = {0: (0, 1), 1: (2, 3), 3: (4, 5), 4: (6, 7), 5: (8, 9), 6: (10, 11)}
                    for t in (0, 1, 2, 6, 3, 4, 5):
                        if t != 2:
                            chunks = umap[t]

                            def ev_u(oi, si, bank, chunks=chunks, eng=(None if t < 2 else "dve")):
                                t0, n = SL[si]
                                copy_op(uq[:, chunks[oi], t0:t0 + n], bank.ap[:, 0:n], [bank.key], [("uq", chunks[oi], si)], eng=eng)
                            grps = gemm_groups(wtile, 8, 256, (0, 1), xn_rhs, ev_u)
                            if t < 2:
                                for gfn in grps:
                                    gfn()
                            else:
                                pending.extend(grps)
                        else:
                            wt, wkey = wtile()
                            def ev_k(oi, si, bank):
                                t0, n = SL[si]
                                copy_op(kf[:, t0:t0 + n], bank.ap[:, 0:n], [bank.key], [("kf", si)])
                            gemm_B(wt, wkey, 8, 256, (0,), xn_rhs, ev_k)
                            wv = wt[:].rearrange("p (k n) -> p k n", k=8)
                            for tt in range(9):
                                si = tt // 4
                                bank = pools["G"].next()
                                for kc in range(8):
                                    S.op("pe", lambda e, o=bank.ap[:, 0:128], w=xn[:, kc, tt * 128:(tt + 1) * 128], r=wv[:, kc, 128:256], kc=kc:
                                         e.matmul(o, w, r, start=(kc == 0), stop=(kc == 7)),
                                         reads=[wkey, ("xn", kc, si)], writes=[bank.key], inc=(kc == 7))
                                if tt == 8 or (tt == 7 and h == 1):
                                    vs = vst_ring.next()
                                    sem = "st_v%d" % vs.key[1]
                                    copy_op(vs.ap[:], bank.ap[:, 0:128], [bank.key], [vs.key], eng="act")
                                    copy_op(vtok[:, 1 + tt, :], vs.ap[:], [vs.key], [("v", 1 + tt)], eng="dve")
                                    if tt == 7:
                                        S.dma("sp", o_vp[l], vs.ap[:], sem, reads=[vs.key], is_out=True)
                                    else:
                                        for j in range(2):
                                            S.dma("sp", o_vs[l, 2 * h + j, 64:128, :], vs.ap[64 * j:64 * j + 64, :], sem,
                                                  reads=[vs.key], is_out=True)
                                else:
                                    copy_op(vtok[:, 1 + tt, :], bank.ap[:, 0:128], [bank.key], [("v", 1 + tt)])
                                if tt == 7 and h == 0:
                                    copy_op(vcar[l][:], vtok[:, 1 + tt, :], [("v", 1 + tt)], [("vcar", l)])

                    _stage(4)
                    for si, (t0, n) in enumerate(SL):
                        hn = [(c, uq[:, c, t0:t0 + n], [("uq", c, si)], gcol(l, 32)) for c in (0, 1, 2, 3)]
                        hn.append((-1, kf[:, t0:t0 + n], [("kf", si)], gcol(l, 33)))
                        for b0 in range(0, len(hn), 3):
                            grp = hn[b0:b0 + 3]
                            rss = rms_rstd_batch([([(ap, keys)], bd64[:], "bd64", n) for (_, ap, keys, _) in grp])
                            pump(2)
                            for (c, ap, keys, g_ap), rs in zip(grp, rss):
                                norm_apply(ap, ap, g_ap, rs, n, keys, keys)
                        copy_op(kh[:, 128 + t0:128 + t0 + n], kf[:, t0:t0 + n], [("kf", si)], [("kh", si)], eng="dve")
                        if si == 1 and h == 0:
                            copy_op(kcar[l][:], kf[:, 896:1024], [("kf", 1)], [("kcar", l)], eng="dve")
                        if si == 1 and h == 1:
                            S.dma("sp", o_kp[l], kf[:, 896:1024], "st_k1", reads=[("kf", 1)], is_out=True)
                        if si == 2:
                            for j in range(2):
                                S.dma("sp", o_ks[l, 2 * h + j, :, 64:128], kf[:, 1024 + 64 * j:1088 + 64 * j], "st_k2",
                                      reads=[("kf", 2)], is_out=True)

                    assert len(pending) <= 12
                    for si, (t0, n) in enumerate(SL):
                        hn = [(c, uq[:, c, t0:t0 + n], [("uq", c, si)], gcol(l, 34)) for c in (10, 11)]
                        rss = rms_rstd_batch([([(ap, keys)], bd64[:], "bd64", n) for (_, ap, keys, _) in hn])
                        pump(1)
                        for (c, ap, keys, g_ap), rs in zip(hn, rss):
                            norm_apply(ap, ap, g_ap, rs, n, keys, keys)

                    _stage(5)
                    pe_prev = None
                    for phase in (0, 1):
                        if phase == 1:
                            pump(len(pending))
                        for si, (t0, n) in enumerate(SL):
                            if phase == 1:
                                segs = [(t0, n, None)] if si < 2 else [(t0, 64, 0), (t0 + 64, 64, 1)]
                                for (s0, sn, j) in segs:
                                    pe = pe_ring.next()
                                    if j is not None:
                                        c0 = 16 * l
                                        src = cv[:, c0:c0 + 16].rearrange("p (c s r) -> p c s r", c=2, s=4)[:, :, 2 * h + j, :]
                                        copy_op(pe.ap[:, :, 0:2], src, ["cv"], [pe.key], eng="dve")
                                    elif si == 0 and h == 0:
                                        S.op("dve", lambda e, o=pe.ap[:, :, 0:2]: e.memset(o, 0.0), writes=[pe.key])
                                    elif si == 0:
                                        copy_op(pe.ap[:, :, 0:2], pcar[l][:], [("pcar", l)], [pe.key], eng="dve")
                                    else:
                                        copy_op(pe.ap[:, :, 0:2], pe_prev.ap[:, :, 512:514], [pe_prev.key], [pe.key], eng="dve")
                                    ckeys = [("uq", c, si) for c in (6, 7, 8, 9)]
                                    S.op("dve", lambda e, o=pe.ap[:, :, 2:2 + sn], a=uq[:, 6:8, s0:s0 + sn], b=uq[:, 8:10, s0:s0 + sn]:
                                         e.tensor_tensor(out=o, in0=a, in1=b, op=ALU.mult), reads=ckeys, writes=[pe.key])
                                    want_out = (j is not None) or (si == 1 and h == 1)
                                    if want_out:
                                        ptl = ptl_ring.next()
                                        e0 = s0 + sn - 2
                                        S.op("dve", lambda e, o=ptl.ap[:], a=uq[:, 6:8, e0:e0 + 2], b=uq[:, 8:10, e0:e0 + 2]:
                                             e.tensor_tensor(out=o, in0=a, in1=b, op=ALU.mult), reads=ckeys, writes=[ptl.key])
                                        sem = "st_ptl%d" % ptl.key[1]
                                        if j is not None:
                                            S.dma("sp", o_cs[l, :, :, 2 * h + j, :], ptl.ap[:], sem, reads=[ptl.key], is_out=True)
                                        else:
                                            S.dma("sp", o_cp[l].rearrange("p (c r) -> p c r", c=2), ptl.ap[:], sem, reads=[ptl.key], is_out=True)
                                    if si == 1 and h == 0:
                                        copy_op(pcar[l][:], pe.ap[:, :, 512:514], [pe.key], [("pcar", l)], eng="dve")
                                    off = 0 if j is None else 64 * j
                                    for c in range(2):
                                        bank = pools["S"].next()
                                        for r in range(3):
                                            S.op("pe", lambda e, o=bank.ap[:, 0:sn], w=cdg[par][:, (2 * r + c) * 128:(2 * r + c + 1) * 128],
                                                 x=pe.ap[:, c, r:r + sn], r=r:
                                                 e.matmul(o, w, x, start=(r == 0), stop=(r == 2)),
                                                 reads=[pe.key, ckey], writes=[bank.key], inc=(r == 2))
                                        S.op("dve", lambda e, o=cy_f[:, c, off:off + sn], a=bank.ap[:, 0:sn], b=uq[:, 4 + c, s0:s0 + sn]:
                                             e.tensor_tensor(out=o, in0=a, in1=b, op=ALU.mult),
                                             reads=[bank.key, ("uq", 4 + c, si)], writes=["cy_f"])
                                    pe_prev = pe

                            if phase == 0:
                                nq = n // 64
                                chunk_state = {}

                                def swa_blocks(qi):
                                    qc = t0 + 64 * qi
                                    blocks = []
                                    if si < 2:
                                        m = qc // 64
                                        mg = 16 * h + m
                                        lo = max(0, mg - 2) - 16 * h
                                        nl = lo
                                        while nl <= m:
                                            if nl % 2 == 0 and nl + 1 <= m:
                                                nk, pb = 128, 0
                                            else:
                                                nk, pb = 64, 64 * (nl % 2)
                                            col = 128 + 64 * nl
                                            kkey = ("kh", -1) if nl < 0 else ("kh", (64 * nl) // 512)
                                            tile = (nl + 2) // 2
                                            blocks.append((lambda g, col=col, nk=nk: kh[64 * g:64 * g + 64, col:col + nk], [kkey],
                                                           lambda g, tile=tile, pb=pb, nk=nk: vtok[pb:pb + nk, tile, 64 * g:64 * g + 64],
                                                           [("v", tile)], pb, nk))
                                            nl += nk // 64
                                    else:
                                        j = qi
                                        blocks.append((lambda g, j=j: kcTs[par][64 * g:64 * g + 64, j, :], [ckey],
                                                       lambda g, j=j: vcs[par][:, j, 64 * g:64 * g + 64], [ckey], 0, 128))
                                        col = 128 + 1024 + 64 * j
                                        pb = 64 * j
                                        blocks.append((lambda g, col=col: kh[64 * g:64 * g + 64, col:col + 64], [("kh", 2)],
                                                       lambda g, pb=pb: vtok[pb:pb + 64, 9, 64 * g:64 * g + 64], [("v", 9)], pb, 64))
                                    return blocks

                                def swa_A(qi):
                                    qc = t0 + 64 * qi
                                    blocks = swa_blocks(qi)
                                    qkeys = [("uq", c, si) for c in range(4)]
                                    pTs = []
                                    for g in range(2):
                                        bank = pools["S"].next()
                                        pT = pT_ring.next()
                                        pTs.append(pT)
                                        for bi, (kfn, kkeys, vfn, vkeys, pb, nk) in enumerate(blocks):
                                            S.op("pe", lambda e, o=bank.ap[pb:pb + nk, bi * 256:(bi + 1) * 256], w=kfn(g),
                                                 r=uq[64 * g:64 * g + 64, 0:4, qc:qc + 64], g=g, pb=pb:
                                                 e.matmul(o, w, r, start=True, stop=True, tile_position=(64 * g, pb)),
                                                 reads=kkeys + qkeys, writes=[bank.key], inc=True)
                                        for bi, (kfn, kkeys, vfn, vkeys, pb, nk) in enumerate(blocks):
                                            S.op("act", lambda e, o=pT.ap[pb:pb + nk, bi, :], a=bank.ap[pb:pb + nk, bi * 256:(bi + 1) * 256]:
                                                 e.activation(out=o, in_=a, func=AF.Exp, scale=0.125),
                                                 reads=[bank.key], writes=[pT.key])
                                    chunk_state[qi] = (blocks, pTs)

                                def swa_B(qi):
                                    blocks, pTs = chunk_state.pop(qi)
                                    bo = pools["O"].next()
                                    nb = len(blocks)
                                    for g in range(2):
                                        for part in range(2):
                                            for bi, (kfn, kkeys, vfn, vkeys, pb, nk) in enumerate(blocks):
                                                lhs = vfn(g) if part == 0 else ones1[pb:pb + nk, 0:64]
                                                S.op("pe", lambda e, o=bo.ap[64 * g:64 * g + 64, part * 256:(part + 1) * 256], w=lhs,
                                                     r=pTs[g].ap[pb:pb + nk, bi, :], bi=bi, pb=pb, g=g:
                                                     e.matmul(o, w, r, start=(bi == 0), stop=(bi == nb - 1), tile_position=(pb, 64 * g)),
                                                     reads=(vkeys if part == 0 else ["ones1"]) + [pTs[g].key], writes=[bo.key],
                                                     inc=(bi == nb - 1))
                                    den = den_ring.next()
                                    rcp = rcp_ring.next()
                                    S.op("dve", lambda e, o=den.ap[:, 0:256].rearrange("p (a q) -> p a q", a=4),
                                         i=bo.ap[:, 256:512].rearrange("p (a q) -> p a q", a=4),
                                         b=esink[:, 4 * l:4 * l + 4].unsqueeze(2).to_broadcast([128, 4, 64]):
                                         e.tensor_tensor(out=o, in0=i, in1=b, op=ALU.add),
                                         reads=[bo.key, "esink"], writes=[den.key])
                                    S.op("act", lambda e, o=rcp.ap[:, 0:256], i=den.ap[:, 0:256]: e.activation(out=o, in_=i, func=AF.Ln),
                                         reads=[den.key], writes=[rcp.key])
                                    S.op("act", lambda e, o=rcp.ap[:, 0:256], i=rcp.ap[:, 0:256]: e.activation(out=o, in_=i, func=AF.Exp, scale=-1.0),
                                         reads=[rcp.key], writes=[rcp.key])
                                    S.op("dve", lambda e, o=a_f[:, :, 64 * qi:64 * qi + 64],
                                         i=bo.ap[:, 0:256].rearrange("p (a q) -> p a q", a=4),
                                         r=rcp.ap[:, 0:256].rearrange("p (a q) -> p a q", a=4):
                                         e.tensor_tensor(out=o, in0=i, in1=r, op=ALU.mult),
                                         reads=[bo.key, rcp.key], writes=["a_f"])

                                ustate = {}

                                def pair_A(u):
                                    pi, g = divmod(u, 2)
                                    m = 8 * si + 2 * pi
                                    mg = 16 * h + m
                                    qc = 64 * m
                                    tiles = []
                                    if mg >= 2:
                                        nl = m - 2
                                        tiles.append((128 + 64 * nl, ("kh", -1) if nl < 0 else ("kh", (64 * nl) // 512), m // 2, "a"))
                                    tiles.append((128 + 64 * m, ("kh", si), m // 2 + 1, "b"))
                                    qkeys = [("uq", c, si) for c in range(4)]
                                    res = []
                                    for (col, kkey, vt, kind) in tiles:
                                        bank = pools["S"].next()
                                        pT = pT_ring.next()
                                        S.op("pe", lambda e, o=bank.ap[:, 0:512], w=kh[64 * g:64 * g + 64, col:col + 128],
                                             r=uq[64 * g:64 * g + 64, 0:4, qc:qc + 128], g=g:
                                             e.matmul(o, w, r, start=True, stop=True, tile_position=(64 * g, 0)),
                                             reads=[kkey] + qkeys, writes=[bank.key], inc=True)
                                        pv = pT.ap[:].rearrange("p b q -> p (b q)")
                                        S.op("act", lambda e, o=pv, a=bank.ap[:, 0:512]: e.activation(out=o, in_=a, func=AF.Exp, scale=0.125),
                                             reads=[bank.key], writes=[pT.key])
                                        p4 = pv.rearrange("p (a t) -> p a t", a=4)
                                        z = p4[0:64, :, 64:128] if kind == "a" else p4[64:128, :, 0:64]
                                        S.op("dve", lambda e, o=z: e.memset(o, 0.0), reads=[pT.key], writes=[pT.key])
                                        res.append((pv, pT.key, vt))
                                    ustate[u] = res

                                def pair_B(u):
                                    pi, g = divmod(u, 2)
                                    res = ustate.pop(u)
                                    if g == 0:
                                        ustate[("bo", pi)] = (pools["O4"].next(), pools["O4"].next())
                                    bo_pv, bo_dn = ustate[("bo", pi)]
                                    nt = len(res)
                                    for part, bo in ((0, bo_pv), (1, bo_dn)):
                                        for ti, (pv, pkey, vt) in enumerate(res):
                                            lhs = vtok[:, vt, 64 * g:64 * g + 64] if part == 0 else ones1[:, 0:64]
                                            S.op("pe", lambda e, o=bo.ap[64 * g:64 * g + 64, 0:512], w=lhs, r=pv, ti=ti, g=g:
                                                 e.matmul(o, w, r, start=(ti == 0), stop=(ti == nt - 1), tile_position=(0, 64 * g)),
                                                 reads=([("v", vt)] if part == 0 else ["ones1"]) + [pkey], writes=[bo.key],
                                                 inc=(ti == nt - 1))
                                    if g == 1:
                                        del ustate[("bo", pi)]
                                        den = den_ring.next()
                                        rcp = rcp_ring.next()
                                        S.op("dve", lambda e, o=den.ap[:].rearrange("p (a t) -> p a t", a=4),
                                             i=bo_dn.ap[:, 0:512].rearrange("p (a t) -> p a t", a=4),
                                             b=esink[:, 4 * l:4 * l + 4].unsqueeze(2).to_broadcast([128, 4, 128]):
                                             e.tensor_tensor(out=o, in0=i, in1=b, op=ALU.add),
                                             reads=[bo_dn.key, "esink"], writes=[den.key])
                                        S.op("act", lambda e, o=rcp.ap[:], i=den.ap[:]: e.activation(out=o, in_=i, func=AF.Ln),
                                             reads=[den.key], writes=[rcp.key])
                                        S.op("act", lambda e, o=rcp.ap[:]: e.activation(out=o, in_=o, func=AF.Exp, scale=-1.0),
                                             reads=[rcp.key], writes=[rcp.key])
                                        S.op("dve", lambda e, o=a_f[:, :, 128 * pi:128 * pi + 128],
                                             i=bo_pv.ap[:, 0:512].rearrange("p (a t) -> p a t", a=4),
                                             r=rcp.ap[:].rearrange("p (a t) -> p a t", a=4):
                                             e.tensor_tensor(out=o, in0=i, in1=r, op=ALU.mult),
                                             reads=[bo_pv.key, rcp.key], writes=["a_f"])

                                if si < 2:
                                    nu = 8
                                    for u in range(nu + 1):
                                        if u < nu:
                                            pair_A(u)
                                            pump(1)
                                        if u >= 1:
                                            pair_B(u - 1)
                                            pump(1)
                                else:
                                    for qi in range(nq + 1):
                                        if qi < nq:
                                            swa_A(qi)
                                            pump(1)
                                        if qi >= 1:
                                            swa_B(qi - 1)
                                            pump(1)

                                pump(len(pending))
                                rs = rms_rstd([(a_f[:, c, 0:n], ["a_f"]) for c in range(4)], ones512[:], "ones512", n)
                                for c in range(4):
                                    norm_apply(xn[:, c, t0:t0 + n], a_f[:, c, 0:n], gcol(l, 24 + c), rs, n, ["a_f"], [("xn", c, si)])
                            if phase == 1:
                                _stage(7)
                                if si < 2:
                                    units = [(t0 + 256 * sub, 256, 256 * sub, None) for sub in range(2)]
                                else:
                                    units = [(t0 + 64 * j, 64, 64 * j, j) for j in range(2)]
                                mem_items = [(u, c) for u in units for c in range(2)]
                                mstate = {}

                                def mem_A(idx):
                                    (u0, un, uoff, j), c = mem_items[idx]
                                    pTs = []
                                    for hj in range(2):
                                        bank = pools["S"].next()
                                        pT = pT_ring.next()
                                        pTs.append(pT)
                                        for kt in range(2):
                                            if j is None:
                                                lhs, lkeys = mkh[l][64 * hj:64 * hj + 64, c, kt * 128:(kt + 1) * 128], [("mkh", l)]
                                            else:
                                                lhs, lkeys = mkcTs[par][64 * hj:64 * hj + 64, j, c, kt * 128:(kt + 1) * 128], [ckey]
                                            S.op("pe", lambda e, o=bank.ap[:, kt * 256:kt * 256 + un], w=lhs,
                                                 r=uq[64 * hj:64 * hj + 64, 10 + c, u0:u0 + un], hj=hj:
                                                 e.matmul(o, w, r, start=True, stop=True, tile_position=(64 * hj, 0)),
                                                 reads=lkeys + [("uq", 10 + c, si)], writes=[bank.key], inc=True)
                                        S.op("act", lambda e, o=pT.ap[:, :, 0:un], a=bank.ap[:].rearrange("p (k q) -> p k q", k=2)[:, :, 0:un]:
                                             e.activation(out=o, in_=a, func=AF.Exp, scale=0.125),
                                             reads=[bank.key], writes=[pT.key])
                                    mstate[idx] = pTs

                                def mem_B(idx):
                                    (u0, un, uoff, j), c = mem_items[idx]
                                    pTs = mstate.pop(idx)
                                    bo = pools["O"].next()
                                    for hj in range(2):
                                        hcol = (2 * c + hj) * 64
                                        for part in range(2):
                                            for kt in range(2):
                                                if part == 1:
                                                    lhs, lkeys = ones1[:, 0:64], ["ones1"]
                                                elif j is None:
                                                    lhs, lkeys = mvb[l][:, kt, hcol:hcol + 64], [("mvb", l)]
                                                else:
                                                    lhs, lkeys = mvcs[par][:, j, kt, hcol:hcol + 64], [ckey]
                                                S.op("pe", lambda e, o=bo.ap[64 * hj:64 * hj + 64, part * 256:part * 256 + un], w=lhs,
                                                     r=pTs[hj].ap[:, kt, 0:un], kt=kt, hj=hj:
                                                     e.matmul(o, w, r, start=(kt == 0), stop=(kt == 1), tile_position=(0, 64 * hj)),
                                                     reads=lkeys + [pTs[hj].key], writes=[bo.key], inc=(kt == 1))
                                    rcp = rcp_ring.next()
                                    S.op("act", lambda e, o=rcp.ap[:, 0:un], i=bo.ap[:, 256:256 + un]: e.activation(out=o, in_=i, func=AF.Ln),
                                         reads=[bo.key], writes=[rcp.key])
                                    S.op("act", lambda e, o=rcp.ap[:, 0:un], i=rcp.ap[:, 0:un]: e.activation(out=o, in_=i, func=AF.Exp, scale=-1.0),
                                         reads=[rcp.key], writes=[rcp.key])
                                    S.op("dve", lambda e, o=mo_f[:, c, uoff:uoff + un], i=bo.ap[:, 0:un], r=rcp.ap[:, 0:un]:
                                         e.tensor_tensor(out=o, in0=i, in1=r, op=ALU.mult),
                                         reads=[bo.key, rcp.key], writes=["mo_f"])

                                for idx in range(len(mem_items) + 1):
                                    if idx < len(mem_items):
                                        mem_A(idx)
                                    if idx >= 1:
                                        mem_B(idx - 1)

                                _stage(8)
                                rss = rms_rstd_batch([
                                    ([(cy_f[:, c, 0:n], ["cy_f"]) for c in range(2)], ones256[:], "ones256", n),
                                    ([(mo_f[:, c, 0:n], ["mo_f"]) for c in range(2)], ones256[:], "ones256", n)])
                                for c in range(2):
                                    norm_apply(xn[:, 4 + c, t0:t0 + n], cy_f[:, c, 0:n], gcol(l, 28 + c), rss[0], n, ["cy_f"], [("xn", 4 + c, si)])
                                for c in range(2):
                                    norm_apply(xn[:, 6 + c, t0:t0 + n], mo_f[:, c, 0:n], gcol(l, 30 + c), rss[1], n, ["mo_f"], [("xn", 6 + c, si)])

                    _stage(9)
                    def ev_res(oc):
                        def f(oi, si, bank, oc=oc):
                            t0, n = SL[si]
                            c = oc(oi)
                            S.op("dve", lambda e, o=xT[:, c, t0:t0 + n], b=bank.ap[:, 0:n]:
                                 e.tensor_tensor(out=o, in0=b, in1=o, op=ALU.add),
                                 reads=[bank.key, ("x", c, si)], writes=[("x", c, si)])
                        return f
                    for t in range(4):
                        wt, wkey = wtile()
                        gemm_B(wt, wkey, 8, 256, (0, 1), xn_rhs, ev_res(lambda oi, t=t: 2 * t + oi))

                    _stage(10)
                    for si, (t0, n) in enumerate(SL):
                        rs = rms_rstd([(xT[:, c, t0:t0 + n], [("x", c, si)]) for c in range(8)], ones1024[:], "ones1024", n)
                        for c in range(8):
                            norm_apply(xn[:, c, t0:t0 + n], xT[:, c, t0:t0 + n], gcol(l, 8 + c), rs, n,
                                       [("x", c, si)], [("xn", c, si)])
                    for hf in range(2):
                        for jj in range(11):
                            wt, wkey = wtile()
                            wv = wt[:].rearrange("p (k n) -> p k n", k=8)
                            for si, (t0, n) in enumerate(SL):
                                bg = pools["G"].next()
                                bu = pools["G"].next()
                                for (bank, co) in ((bg, 0), (bu, 128)):
                                    for kc in range(8):
                                        S.op("pe", lambda e, o=bank.ap[:, 0:n], w=wv[:, kc, co:co + 128], r=xn[:, kc, t0:t0 + n], kc=kc:
                                             e.matmul(o, w, r, start=(kc == 0), stop=(kc == 7)),
                                             reads=[wkey, ("xn", kc, si)], writes=[bank.key], inc=(kc == 7))
                                sg = sg_ring.next()
                                S.op("act", lambda e, o=sg.ap[:, 0:n], a=bg.ap[:, 0:n]: e.activation(out=o, in_=a, func=AF.Silu),
                                     reads=[bg.key], writes=[sg.key])
                                S.op("dve", lambda e, o=uq[:, jj, t0:t0 + n], a=bu.ap[:, 0:n], b=sg.ap[:, 0:n]:
                                     e.tensor_tensor(out=o, in0=a, in1=b, op=ALU.mult),
                                     reads=[bu.key, sg.key], writes=[("uq", jj, si)])
                        for c in range(8):
                            wt, wkey = wtile()

                            def act_rhs(kc, si):
                                t0, n = SL[si]
                                return uq[:, kc, t0:t0 + n], [("uq", kc, si)]
                            gemm_B(wt, wkey, 11, 128, (0,), act_rhs, ev_res(lambda oi, c=c: c))

                for c in range(8):
                    S.dma("sp", yT[h, c * 128:(c + 1) * 128, :], xT[:, c, :], "st_y%d" % c,
                          reads=[("x", c, si) for si in range(3)], is_out=True)

        try:
            _stage(1)
            main_body()
        except _Stop:
            pass
        S.finish()
        block = es.enter_context(nc.Block())

        @block.tensor
        def _(e):
            S.replay("pe", e)

        @block.scalar
        def _(e):
            S.replay("act", e)

        @block.vector
        def _(e):
            S.replay("dve", e)

        @block.gpsimd
        def _(e):
            S.replay("pool", e)

        @block.sync
        def _(e):
            S.replay("sp", e)
    return nc


def _tile_cols(wblk):
    K = wblk.shape[0] // 128
    return np.ascontiguousarray(wblk.reshape(K, 128, wblk.shape[1]).transpose(1, 0, 2)).reshape(128, -1)


def _prep_shared(inp):
    L = DEPTH
    w_in, w_mem_kv, w_out, w_gu, w_down = (np.asarray(inp[k], np.float32) for k in
                                           ("w_in", "w_mem_kv", "w_out", "w_gate_up", "w_down"))
    WA = np.empty((L, 35, 128, 2048), np.float32)
    WD = np.empty((L, 2, 8, 128, 1408), np.float32)
    qperm = np.concatenate([np.concatenate([np.arange(64) + 64 * c, np.arange(64) + 64 * (4 + c)]) for c in range(4)])
    rperm = np.concatenate([qperm, np.arange(512, 1024)])
    for l in range(L):
        wq = w_in[l][:, qperm]
        WA[l, 0] = _tile_cols(wq[:, 0:256])
        WA[l, 1] = _tile_cols(wq[:, 256:512])
        for t, c0 in zip(range(2, 7), (512, 768, 1024, 1280, 1536)):
            WA[l, t] = _tile_cols(w_in[l][:, c0:c0 + 256])
        WA[l, 7] = _tile_cols(w_mem_kv[l][:, 0:256])
        WA[l, 8] = _tile_cols(w_mem_kv[l][:, 256:512])
        wo = w_out[l][rperm, :]
        for t in range(4):
            WA[l, 9 + t] = _tile_cols(wo[:, 256 * t:256 * (t + 1)])
        for j in range(22):
            blk = np.concatenate([w_gu[l][:, 128 * j:128 * (j + 1)], w_gu[l][:, D_FF + 128 * j:D_FF + 128 * (j + 1)]], axis=1)
            WA[l, 13 + j] = _tile_cols(blk)
        for hf in range(2):
            rows = w_down[l][1408 * hf:1408 * (hf + 1)]
            for c in range(8):
                WD[l, hf, c] = _tile_cols(rows[:, 128 * c:128 * (c + 1)])
    gt = np.zeros((128, L * GL), np.float32)

    def cols(v):
        return np.asarray(v, np.float32).reshape(8, 128).T
    for l in range(L):
        b = l * GL
        gt[:, b + 0:b + 8] = cols(inp["attn_norm_g"][l])
        gt[:, b + 8:b + 16] = cols(inp["ffn_norm_g"][l])
        gt[:, b + 16:b + 24] = cols(inp["mem_norm_g"][l])
        gt[:, b + 24:b + 32] = cols(np.asarray(inp["out_norm_g"][l])[rperm])
        gt[:, b + 32] = np.tile(np.asarray(inp["q_norm_g"][l]), 2)
        gt[:, b + 33] = np.tile(np.asarray(inp["k_norm_g"][l]), 2)
        gt[:, b + 34] = np.tile(np.asarray(inp["mq_norm_g"][l]), 2)
        gt[:, b + 35] = np.tile(np.asarray(inp["mk_norm_g"][l]), 2)
        cw = np.asarray(inp["conv_w"][l], np.float32)
        for r in range(3):
            for c in range(2):
                gt[:, b + 36 + 2 * r + c] = cw[r, 128 * c:128 * (c + 1)]
        sk = np.asarray(inp["sinks"][l], np.float32)
        for g in range(2):
            for a in range(4):
                gt[64 * g:64 * (g + 1), b + 42 + a] = sk[4 * g + a]
    cd = np.zeros((L, 128, 6, 128), np.float32)
    ar = np.arange(128)
    for l in range(L):
        cw = np.asarray(inp["conv_w"][l], np.float32)
        for r in range(3):
            for c in range(2):
                cd[l, ar, 2 * r + c, ar] = cw[r, 128 * c:128 * (c + 1)]
    return {"WA": WA, "WD": WD, "gtab": gt, "cdiag": cd.reshape(L, 128, 768)}


def _prep_core(inp, core):
    L = DEPTH
    xp = np.asarray(inp["x_prompt"][core], np.float32)
    xs = np.asarray(inp["x_sample"][4 * core:4 * core + 4], np.float32)
    xT = np.empty((2, D, TH), np.float32)
    for h in range(2):
        xT[h, :, 0:1024] = xp[1024 * h:1024 * (h + 1)].T
        for j in range(2):
            xT[h, :, 1024 + 64 * j:1088 + 64 * j] = xs[2 * h + j].T
    sl = slice(4 * core, 4 * core + 4)
    ck = np.asarray(inp["cache_win_k"][:, sl], np.float32).reshape(L, 4, 128, 128)
    cvv = np.asarray(inp["cache_win_v"][:, sl], np.float32).reshape(L, 4, 128, 128)
    cc = np.asarray(inp["cache_conv"][:, sl], np.float32)
    cv = np.ascontiguousarray(cc.reshape(L, 4, 2, 2, 128).transpose(4, 0, 3, 1, 2)).reshape(128, L * 16)
    cmk = np.asarray(inp["cache_mem_k"][:, sl], np.float32).reshape(L, 4, 256, 256)
    cmv = np.asarray(inp["cache_mem_v"][:, sl], np.float32).reshape(L, 4, 256, 256)
    return {
        "xT": xT,
        "memT": np.ascontiguousarray(np.asarray(inp["mem_prompt"][core], np.float32).T),
        "kcT": np.ascontiguousarray(ck.transpose(0, 1, 3, 2)),
        "vc": np.ascontiguousarray(cvv),
        "cv": cv,
        "mkcT": np.ascontiguousarray(cmk.transpose(0, 1, 3, 2)),
        "mvc": np.ascontiguousarray(cmv),
    }


_NC_CACHE = {}


def kernel(**inputs):
    L = DEPTH
    n_layers = int(inputs.pop("_n_layers", DEPTH))
    if n_layers not in _NC_CACHE:
        _NC_CACHE[n_layers] = build_program(n_layers)
    nc = _NC_CACHE[n_layers]
    shared = _prep_shared(inputs)
    in_maps = []
    for core in range(NCORES):
        m = dict(shared)
        m.update(_prep_core(inputs, core))
        in_maps.append(m)
    res = run_bass_kernel_spmd(nc, in_maps, core_ids=list(range(NCORES)))
    R = res.results
    yp = np.empty((8, 2048, D), np.float32)
    ys = np.empty((32, 64, D), np.float32)
    wk_p = np.empty((L, 8, 128, 2, 64), np.float32)
    wv_p = np.empty((L, 8, 128, 2, 64), np.float32)
    cv_p = np.empty((L, 8, 2, 256), np.float32)
    mk_p = np.empty((L, 8, 256, 4, 64), np.float32)
    mv_p = np.empty((L, 8, 256, 4, 64), np.float32)
    wk_s = np.empty((L, 32, 128, 2, 64), np.float32)
    wv_s = np.empty((L, 32, 128, 2, 64), np.float32)
    cv_s = np.empty((L, 32, 2, 256), np.float32)
    for core in range(NCORES):
        r = R[core]
        yT = np.asarray(r["yT"])
        for h in range(2):
            yp[core, 1024 * h:1024 * (h + 1)] = yT[h][:, 0:1024].T
            for j in range(2):
                ys[4 * core + 2 * h + j] = yT[h][:, 1024 + 64 * j:1088 + 64 * j].T
        wk_p[:, core] = np.asarray(r["o_kp"]).transpose(0, 2, 1).reshape(L, 128, 2, 64)
        wv_p[:, core] = np.asarray(r["o_vp"]).reshape(L, 128, 2, 64)
        cv_p[:, core] = np.asarray(r["o_cp"]).reshape(L, 128, 2, 2).transpose(0, 3, 2, 1).reshape(L, 2, 256)
        mk_p[:, core] = np.asarray(r["o_mk"]).reshape(L, 256, 256).transpose(0, 2, 1).reshape(L, 256, 4, 64)
        mv_p[:, core] = np.asarray(r["o_mv"]).reshape(L, 256, 4, 64)
        wk_s[:, 4 * core:4 * core + 4] = np.asarray(r["o_ks"]).transpose(0, 1, 3, 2).reshape(L, 4, 128, 2, 64)
        wv_s[:, 4 * core:4 * core + 4] = np.asarray(r["o_vs"]).reshape(L, 4, 128, 2, 64)
        cv_s[:, 4 * core:4 * core + 4] = np.asarray(r["o_cs"]).transpose(0, 3, 4, 2, 1).reshape(L, 4, 2, 256)
    return (yp, ys, wk_p, wv_p, cv_p, mk_p, mv_p, wk_s, wv_s, cv_s)
```
